# Optimizing a Trainium2 kernel written in Bass

```python
import math
import jax, jax.numpy as jnp
from jax import lax
import numpy as np

D_MODEL = 2048
BATCH = 4
SEQ = 2048
DEPTH = 4
DEC_BATCH = 128
DEC_SEQ = 4
PAST_LEN = 16384
PAGE_SIZE = 128

N_MIXERS = 4
EXPAND = 2
D_INNER = EXPAND * D_MODEL
NORM_EPS = 1e-6
CHUNK = 64
GLA_CHUNK = 16

RWKV_HEAD = 64
RWKV_HEADS = D_INNER // RWKV_HEAD
RWKV_LORA_W = 96
RWKV_LORA_A = 96
RWKV_IN = 4 * D_INNER + RWKV_LORA_W + RWKV_LORA_A
RWKV_GN_EPS = 64e-5

RET_HEADS = 8
RET_DK = D_MODEL // RET_HEADS
RET_DV = D_INNER // RET_HEADS
RET_IN = 2 * D_MODEL + 2 * D_INNER
ROPE_BASE = 10000.0

SSD_HEADDIM = 64
SSD_HEADS = D_INNER // SSD_HEADDIM
SSD_STATE = 128
SSD_GROUPS = 8
SSD_CONV = 4
SSD_CONV_DIM = D_INNER + 2 * SSD_GROUPS * SSD_STATE
SSD_IN = D_INNER + SSD_CONV_DIM + SSD_HEADS
SSD_NORM_EPS = 1e-5

GLA_HEADS = 4
GLA_KEY = D_MODEL // 2
GLA_DK = GLA_KEY // GLA_HEADS
GLA_DV = D_INNER // GLA_HEADS
GLA_LORA = 16
GLA_LOGIT_NORM = 16.0
GLA_IN = 2 * GLA_KEY + 2 * D_INNER + GLA_LORA

kernel_name = 'hybrid_rwkv7_retnet_mamba2_gla_step'


def rmsnorm(x, g):
    x32 = x.astype(jnp.float32)
    y = x32 * lax.rsqrt(jnp.mean(x32 * x32, axis=-1, keepdims=True) + NORM_EPS)
    return (y * g.astype(jnp.float32)).astype(x.dtype)


def head_rms(y, eps):
    y32 = y.astype(jnp.float32)
    return y32 * lax.rsqrt(jnp.mean(y32 * y32, axis=-1, keepdims=True) + eps)


def chunk_len(T, c):
    return c if T % c == 0 else math.gcd(T, c)


def scalar_decay_chunked(q, k, v, log_a, S0, chunk):
    Bsz, T, G, K = q.shape
    R, V = v.shape[3], v.shape[4]
    c = chunk_len(T, chunk)
    n = T // c
    dt = v.dtype
    qc = q.reshape(Bsz, n, c, G, K)
    kc = k.reshape(Bsz, n, c, G, K)
    vc = v.reshape(Bsz, n, c, G, R, V)
    L = jnp.cumsum(log_a.astype(jnp.float32).reshape(Bsz, n, c, G, R), axis=2)
    causal = jnp.tril(jnp.ones((c, c), dtype=bool))[None, None, :, :, None, None]
    seg = L[:, :, :, None] - L[:, :, None, :]
    decay = jnp.exp(jnp.where(causal, seg, -jnp.inf)).astype(dt)
    scores = jnp.einsum('bntgk,bnsgk->bntsg', qc, kc)
    y_intra = jnp.einsum('bntsg,bntsgr,bnsgrv->bntgrv', scores, decay, vc)

    def step(S, inp):
        q_n, k_n, v_n, L_n = inp
        y_n = jnp.einsum('btgk,btgr,bgrkv->btgrv', q_n, jnp.exp(L_n).astype(dt), S)
        w_end = jnp.exp(L_n[:, -1:] - L_n).astype(dt)
        S = (jnp.exp(L_n[:, -1]).astype(dt)[..., None, None] * S
             + jnp.einsum('bsgk,bsgr,bsgrv->bgrkv', k_n, w_end, v_n))
        return S, y_n

    xs = (jnp.moveaxis(qc, 1, 0), jnp.moveaxis(kc, 1, 0), jnp.moveaxis(vc, 1, 0), jnp.moveaxis(L, 1, 0))
    S_T, y_inter = lax.scan(step, S0, xs)
    y = y_intra + jnp.moveaxis(y_inter, 0, 1)
    return y.reshape(Bsz, T, G, R, V), S_T


def vector_decay_chunked(q, k, v, log_a, S0, chunk):
    Bsz, T, H, K = q.shape
    V = v.shape[-1]
    c = chunk_len(T, chunk)
    n = T // c
    dt = v.dtype
    L = jnp.cumsum(log_a.astype(jnp.float32).reshape(Bsz, n, c, H, K), axis=2)
    qf = q.astype(jnp.float32).reshape(Bsz, n, c, H, K)
    kf = k.astype(jnp.float32).reshape(Bsz, n, c, H, K)
    vc = v.reshape(Bsz, n, c, H, V)
    qe = qf * jnp.exp(L)
    ke = kf * jnp.exp(-L)
    causal = jnp.tril(jnp.ones((c, c), dtype=bool))
    A = jnp.where(causal, jnp.einsum('bnthk,bnshk->bnhts', qe, ke), 0.0).astype(dt)
    y_intra = jnp.einsum('bnhts,bnshv->bnthv', A, vc)
    kend = (kf * jnp.exp(L[:, :, -1:] - L)).astype(dt)
    gend = jnp.exp(L[:, :, -1]).astype(dt)

    def step(S, inp):
        qe_n, kend_n, g_n, v_n = inp
        y_n = jnp.einsum('bthk,bhkv->bthv', qe_n, S)
        S = g_n[..., None] * S + jnp.einsum('bshk,bshv->bhkv', kend_n, v_n)
        return S, y_n

    xs = (jnp.moveaxis(qe.astype(dt), 1, 0), jnp.moveaxis(kend, 1, 0), jnp.moveaxis(gend, 1, 0), jnp.moveaxis(vc, 1, 0))
    S_T, y_inter = lax.scan(step, S0, xs)
    y = y_intra + jnp.moveaxis(y_inter, 0, 1)
    return y.reshape(Bsz, T, H, V), S_T


def rwkv7_scan(r, w, k, v, kk, a, S0):
    def step(S, inp):
        r_t, w_t, k_t, v_t, kk_t, a_t = inp
        sa = jnp.einsum('bhvk,bhk->bhv', S, -kk_t)
        S = (S * w_t[:, :, None, :] + sa[..., None] * (kk_t * a_t)[:, :, None, :]
             + v_t[..., None] * k_t[:, :, None, :])
        y = jnp.einsum('bhvk,bhk->bhv', S, r_t)
        return S, y

    xs = (jnp.moveaxis(r, 1, 0), jnp.moveaxis(w, 1, 0), jnp.moveaxis(k, 1, 0),
          jnp.moveaxis(v, 1, 0), jnp.moveaxis(kk, 1, 0), jnp.moveaxis(a, 1, 0))
    S_T, y = lax.scan(step, S0, xs)
    return jnp.moveaxis(y, 0, 1), S_T


def rotary(x, pos):
    half = x.shape[-1] // 2
    inv = 1.0 / (ROPE_BASE ** jnp.linspace(0.0, 1.0, half, dtype=jnp.float32))
    ang = pos.astype(jnp.float32)[:, None] * inv[None, :]
    cos = jnp.cos(ang)[None, :, None, :]
    sin = jnp.sin(ang)[None, :, None, :]
    x32 = x.astype(jnp.float32)
    x1, x2 = x32[..., :half], x32[..., half:]
    return jnp.concatenate([x1 * cos - x2 * sin, x1 * sin + x2 * cos], axis=-1).astype(x.dtype)


def rwkv7_mixer(xn, shift, S0, w_in, mu, w0, w_up, a0, a_up, k_k, k_a, r_k, ln_w, ln_b, w_out):
    Bsz, T, _ = xn.shape
    dt = xn.dtype
    p_all = jnp.concatenate([shift[:, None, :], xn], axis=1) @ w_in
    p, p_prev = p_all[:, 1:], p_all[:, :-1]
    p = p + mu * (p_prev - p)
    r, k, v, g, wd, ad = jnp.split(p, [D_INNER, 2 * D_INNER, 3 * D_INNER, 4 * D_INNER,
                                       4 * D_INNER + RWKV_LORA_W], axis=-1)
    w_raw = (w0 + jnp.tanh(wd) @ w_up).astype(jnp.float32)
    decay = jnp.exp(-jnp.exp(-jax.nn.softplus(-w_raw) - 0.5)).astype(dt)
    a = jax.nn.sigmoid(a0 + ad @ a_up)
    heads = lambda t: t.reshape(Bsz, T, RWKV_HEADS, RWKV_HEAD)
    kk32 = heads(k * k_k).astype(jnp.float32)
    kk = (kk32 / jnp.maximum(jnp.sqrt(jnp.sum(kk32 * kk32, axis=-1, keepdims=True)), 1e-12)).astype(dt)
    k = k * (1.0 + (a - 1.0) * k_a)
    r, k, v, a, decay = heads(r), heads(k), heads(v), heads(a), heads(decay)
    y, S = rwkv7_scan(r, decay, k, v, kk, a, S0)
    y32 = y.astype(jnp.float32)
    mean = jnp.mean(y32, axis=-1, keepdims=True)
    var = jnp.mean(jnp.square(y32 - mean), axis=-1, keepdims=True)
    y = ((y32 - mean) * lax.rsqrt(var + RWKV_GN_EPS)).reshape(Bsz, T, D_INNER) * ln_w + ln_b
    bonus = (jnp.sum(r * k * r_k, axis=-1, keepdims=True) * v).reshape(Bsz, T, D_INNER)
    out = ((y.astype(dt) + bonus) * jax.nn.silu(g)) @ w_out
    return out, xn[:, -1], S


def retention_mixer(xn, pos, S0, w_in, w_out):
    Bsz, T, _ = xn.shape
    q, k, v, g = jnp.split(xn @ w_in, [D_MODEL, 2 * D_MODEL, 2 * D_MODEL + D_INNER], axis=-1)
    q = rotary(q.reshape(Bsz, T, RET_HEADS, RET_DK), pos)
    k = rotary(k.reshape(Bsz, T, RET_HEADS, RET_DK), pos) * (RET_DK ** -0.5)
    v = v.reshape(Bsz, T, RET_HEADS, 1, RET_DV)
    log_gamma = jnp.log1p(-(2.0 ** (-5.0 - jnp.arange(RET_HEADS, dtype=jnp.float32))))
    log_a = jnp.broadcast_to(log_gamma[None, None, :, None], (Bsz, T, RET_HEADS, 1))
    y, S = scalar_decay_chunked(q, k, v, log_a, S0[:, :, None], CHUNK)
    y = head_rms(y.reshape(Bsz, T, RET_HEADS, RET_DV), NORM_EPS).reshape(Bsz, T, D_INNER).astype(xn.dtype)
    out = (y * jax.nn.silu(g)) @ w_out
    return out, S[:, :, 0]


def ssd_mixer(xn, conv_state, S0, w_in, conv_w, conv_b, dt_bias, A_log, D_skip, norm_w, w_out):
    Bsz, T, _ = xn.shape
    dtp = xn.dtype
    G, R = SSD_GROUPS, SSD_HEADS // SSD_GROUPS
    z, xBC, dt_raw = jnp.split(xn @ w_in, [D_INNER, D_INNER + SSD_CONV_DIM], axis=-1)
    xpad = jnp.concatenate([conv_state, xBC], axis=1)
    conv = lax.conv_general_dilated(xpad, conv_w[:, None, :], window_strides=(1,), padding='VALID',
                                    dimension_numbers=('NWC', 'WIO', 'NWC'),
                                    feature_group_count=SSD_CONV_DIM) + conv_b
    xBC_c = jax.nn.silu(conv)
    xs, Bm, Cm = jnp.split(xBC_c, [D_INNER, D_INNER + SSD_GROUPS * SSD_STATE], axis=-1)
    dt = jax.nn.softplus((dt_raw + dt_bias).astype(jnp.float32)).reshape(Bsz, T, G, R)
    A = -jnp.exp(A_log.astype(jnp.float32)).reshape(G, R)
    x_h = xs.reshape(Bsz, T, G, R, SSD_HEADDIM)
    v = x_h * dt[..., None].astype(dtp)
    y, S = scalar_decay_chunked(Cm.reshape(Bsz, T, G, SSD_STATE), Bm.reshape(Bsz, T, G, SSD_STATE), v,
                                dt * A, S0.reshape(Bsz, G, R, SSD_STATE, SSD_HEADDIM), CHUNK)
    y = y + x_h * D_skip.reshape(G, R, 1)
    y = y.reshape(Bsz, T, D_INNER) * jax.nn.silu(z)
    y = head_rms(y.reshape(Bsz, T, G, D_INNER // G), SSD_NORM_EPS).reshape(Bsz, T, D_INNER)
    out = (y * norm_w.astype(jnp.float32)).astype(dtp) @ w_out
    return out, xpad[:, -(SSD_CONV - 1):], S.reshape(Bsz, SSD_HEADS, SSD_STATE, SSD_HEADDIM)


def gla_mixer(xn, S0, w_in, a_up, a_bias, norm_w, w_out):
    Bsz, T, _ = xn.shape
    q, k, v, g, ad = jnp.split(xn @ w_in, [GLA_KEY, 2 * GLA_KEY, 2 * GLA_KEY + D_INNER,
                                           2 * GLA_KEY + 2 * D_INNER], axis=-1)
    log_a = jax.nn.log_sigmoid((ad @ a_up + a_bias).astype(jnp.float32)) / GLA_LOGIT_NORM
    q = q.reshape(Bsz, T, GLA_HEADS, GLA_DK) * (GLA_DK ** -0.5)
    k = k.reshape(Bsz, T, GLA_HEADS, GLA_DK)
    v = v.reshape(Bsz, T, GLA_HEADS, GLA_DV)
    y, S = vector_decay_chunked(q, k, v, log_a.reshape(Bsz, T, GLA_HEADS, GLA_DK), S0, GLA_CHUNK)
    y = (head_rms(y, NORM_EPS) * norm_w.astype(jnp.float32)).reshape(Bsz, T, D_INNER).astype(xn.dtype)
    out = (y * jax.nn.silu(g)) @ w_out
    return out, S


def trunk(x, pos, states, norm_gains, final_norm, rwkv_p, ret_p, ssd_p, gla_p):
    shift, wkv, ret_s, conv_s, ssd_s, gla_s = states
    for i in range(DEPTH):
        xn = rmsnorm(x, norm_gains[i])
        m = i % N_MIXERS
        if m == 0:
            out, shift, wkv = rwkv7_mixer(xn, shift, wkv, *rwkv_p)
        elif m == 1:
            out, ret_s = retention_mixer(xn, pos, ret_s, *ret_p)
        elif m == 2:
            out, conv_s, ssd_s = ssd_mixer(xn, conv_s, ssd_s, *ssd_p)
        else:
            out, gla_s = gla_mixer(xn, gla_s, *gla_p)
        x = x + out
    return rmsnorm(x, final_norm), (shift, wkv, ret_s, conv_s, ssd_s, gla_s)


def setup_inputs(seed: int = 0) -> dict:
    key = jax.random.key(seed)
    ks = iter(jax.random.split(key, 48))
    nrm = lambda shape, scale: jax.random.normal(next(ks), shape, jnp.float32) * scale
    unif = lambda shape, lo, hi: jax.random.uniform(next(ks), shape, jnp.float32, lo, hi)
    E = D_INNER
    dt0 = jnp.exp(unif((SSD_HEADS,), math.log(1e-3), math.log(1e-1)))
    inp = {}
    inp['x_prompt'] = nrm((BATCH, SEQ, D_MODEL), 1.0)
    inp['x_sample'] = nrm((DEC_BATCH, DEC_SEQ, D_MODEL), 1.0)
    inp['state_rwkv_shift'] = nrm((DEC_BATCH, D_MODEL), 1.0)
    inp['state_rwkv_wkv'] = nrm((DEC_BATCH, RWKV_HEADS, RWKV_HEAD, RWKV_HEAD), 0.3)
    inp['state_ret'] = nrm((DEC_BATCH, RET_HEADS, RET_DK, RET_DV), 0.3)
    inp['state_ssd_conv'] = nrm((DEC_BATCH, SSD_CONV - 1, SSD_CONV_DIM), 1.0)
    inp['state_ssd'] = nrm((DEC_BATCH, SSD_HEADS, SSD_STATE, SSD_HEADDIM), 0.3)
    inp['state_gla'] = nrm((DEC_BATCH, GLA_HEADS, GLA_DK, GLA_DV), 0.3)
    inp['norm_gains'] = 1.0 + nrm((DEPTH, D_MODEL), 0.02)
    inp['final_norm'] = 1.0 + nrm((D_MODEL,), 0.02)
    inp['rwkv_w_in'] = nrm((D_MODEL, RWKV_IN), D_MODEL ** -0.5)
    inp['rwkv_mu'] = unif((RWKV_IN,), 0.0, 1.0)
    inp['rwkv_w0'] = unif((E,), -6.0, 1.0)
    inp['rwkv_w_up'] = nrm((RWKV_LORA_W, E), 0.1 * RWKV_LORA_W ** -0.5)
    inp['rwkv_a0'] = nrm((E,), 0.1)
    inp['rwkv_a_up'] = nrm((RWKV_LORA_A, E), 0.1 * RWKV_LORA_A ** -0.5)
    inp['rwkv_k_k'] = 0.85 + nrm((E,), 0.02)
    inp['rwkv_k_a'] = 1.0 + nrm((E,), 0.02)
    inp['rwkv_r_k'] = nrm((RWKV_HEADS, RWKV_HEAD), 0.1)
    inp['rwkv_ln_w'] = 1.0 + nrm((E,), 0.02)
    inp['rwkv_ln_b'] = nrm((E,), 0.02)
    inp['rwkv_w_out'] = nrm((E, D_MODEL), E ** -0.5)
    inp['ret_w_in'] = nrm((D_MODEL, RET_IN), D_MODEL ** -0.5)
    inp['ret_w_out'] = nrm((E, D_MODEL), E ** -0.5)
    inp['ssd_w_in'] = nrm((D_MODEL, SSD_IN), D_MODEL ** -0.5)
    inp['ssd_conv_w'] = nrm((SSD_CONV, SSD_CONV_DIM), SSD_CONV ** -0.5)
    inp['ssd_conv_b'] = nrm((SSD_CONV_DIM,), 0.02)
    inp['ssd_dt_bias'] = dt0 + jnp.log(-jnp.expm1(-dt0))
    inp['ssd_A_log'] = jnp.log(unif((SSD_HEADS,), 1.0, 16.0))
    inp['ssd_D'] = 1.0 + nrm((SSD_HEADS,), 0.02)
    inp['ssd_norm_w'] = 1.0 + nrm((E,), 0.02)
    inp['ssd_w_out'] = nrm((E, D_MODEL), E ** -0.5)
    inp['gla_w_in'] = nrm((D_MODEL, GLA_IN), D_MODEL ** -0.5)
    inp['gla_a_up'] = nrm((GLA_LORA, GLA_KEY), GLA_LORA ** -0.5)
    inp['gla_a_bias'] = nrm((GLA_KEY,), 0.1)
    inp['gla_norm_w'] = 1.0 + nrm((GLA_DV,), 0.02)
    inp['gla_w_out'] = nrm((E, D_MODEL), E ** -0.5)
    return inp


def reference(x_prompt, x_sample, state_rwkv_shift, state_rwkv_wkv, state_ret, state_ssd_conv, state_ssd,
              state_gla, norm_gains, final_norm, rwkv_w_in, rwkv_mu, rwkv_w0, rwkv_w_up, rwkv_a0, rwkv_a_up,
              rwkv_k_k, rwkv_k_a, rwkv_r_k, rwkv_ln_w, rwkv_ln_b, rwkv_w_out, ret_w_in, ret_w_out, ssd_w_in,
              ssd_conv_w, ssd_conv_b, ssd_dt_bias, ssd_A_log, ssd_D, ssd_norm_w, ssd_w_out, gla_w_in, gla_a_up,
              gla_a_bias, gla_norm_w, gla_w_out):
    rwkv_p = (rwkv_w_in, rwkv_mu, rwkv_w0, rwkv_w_up, rwkv_a0, rwkv_a_up, rwkv_k_k, rwkv_k_a, rwkv_r_k,
              rwkv_ln_w, rwkv_ln_b, rwkv_w_out)
    ret_p = (ret_w_in, ret_w_out)
    ssd_p = (ssd_w_in, ssd_conv_w, ssd_conv_b, ssd_dt_bias, ssd_A_log, ssd_D, ssd_norm_w, ssd_w_out)
    gla_p = (gla_w_in, gla_a_up, gla_a_bias, gla_norm_w, gla_w_out)

    Bp, Tp = x_prompt.shape[0], x_prompt.shape[1]
    dt = x_prompt.dtype
    init_prompt = (jnp.zeros((Bp, D_MODEL), dt),
                   jnp.zeros((Bp, RWKV_HEADS, RWKV_HEAD, RWKV_HEAD), dt),
                   jnp.zeros((Bp, RET_HEADS, RET_DK, RET_DV), dt),
                   jnp.zeros((Bp, SSD_CONV - 1, SSD_CONV_DIM), dt),
                   jnp.zeros((Bp, SSD_HEADS, SSD_STATE, SSD_HEADDIM), dt),
                   jnp.zeros((Bp, GLA_HEADS, GLA_DK, GLA_DV), dt))
    y_prompt, (p_shift, p_wkv, p_ret, p_conv, p_ssd, p_gla) = trunk(
        x_prompt, jnp.arange(Tp), init_prompt, norm_gains, final_norm, rwkv_p, ret_p, ssd_p, gla_p)

    init_sample = (state_rwkv_shift, state_rwkv_wkv, state_ret, state_ssd_conv, state_ssd, state_gla)
    y_sample, (s_shift, s_wkv, s_ret, s_conv, s_ssd, s_gla) = trunk(
        x_sample, PAST_LEN + jnp.arange(x_sample.shape[1]), init_sample, norm_gains, final_norm,
        rwkv_p, ret_p, ssd_p, gla_p)

    return (y_prompt, y_sample, p_shift, p_wkv, p_ret, p_conv, p_ssd, p_gla,
            s_shift, s_wkv, s_ret, s_conv, s_ssd, s_gla)
```

```python
import math, contextlib
import numpy as np
import concourse.bass as bass
import concourse.mybir as mybir
from concourse.bass_utils import run_bass_kernel_spmd

F32 = mybir.dt.float32; BF16 = mybir.dt.bfloat16
AF = mybir.ActivationFunctionType; ALU = mybir.AluOpType; AX = mybir.AxisListType

D = 2048; E = 4096
EPOCH = 3000


class KB:
    def __init__(self, nc, es, n_dma_sems=6):
        self.nc, self.es = nc, es
        self.eng = {'pe': nc.tensor, 'dve': nc.vector, 'act': nc.scalar, 'pool': nc.gpsimd, 'sp': nc.sync}
        self.sem = {}; self.cnt = {}; self.nsem = 0; self.allsems = []
        for e in self.eng: self._new_sem(e)
        self.waited = {}
        self.last_w = {}; self.readers = {}
        self.dsem = {}; self.n_dma_sems = n_dma_sems
        self.ninstr = 0
        self.reg = {}
        self.rr = {}

    def sb(self, name, shape, dt=F32, gran=None, es=None):
        t = (es or self.es).enter_context(self.nc.sbuf_tensor(name, list(shape), dt))
        self.reg[name] = ('sb', gran); return t
    def ps(self, name, shape, dt=F32, es=None):
        t = (es or self.es).enter_context(self.nc.psum_tensor(name, list(shape), dt))
        self.reg[name] = ('ps', None); return t
    def dram(self, name, shape, dt=F32, kind="Internal", gran=None):
        t = self.nc.dram_tensor(name, list(shape), dt, kind=kind).ap()
        self.reg[name] = ('dram', gran); return t

    def tokens(self, ap):
        name = ap.name
        kind, gran = self.reg.get(name, ('x', None))
        if gran is None: return [name]
        dims = ap.ap; off = ap.offset
        if kind == 'dram':
            lo = off; hi = off + sum((c - 1) * abs(s) for s, c in dims)
        else:
            pstep = dims[0][0]
            lo = off % pstep if pstep else off
            hi = lo + sum((c - 1) * abs(s) for s, c in dims[1:])
        return [(name, b) for b in range(lo // gran, hi // gran + 1)]

    def _new_sem(self, e):
        s = self.es.enter_context(self.nc.semaphore(f"s_{e}_{self.nsem}")); self.nsem += 1
        self.sem[e] = s; self.cnt[e] = 0; self.allsems.append(s)
    def _wait(self, e, deps):
        need = {}
        for ev in deps:
            if ev is None: continue
            s, v = ev
            if need.get(s.name, (None, 0))[1] < v: need[s.name] = (s, v)
        for name, (s, v) in need.items():
            if self.waited.get((e, name), 0) < v:
                self.eng[e].wait_ge(s, v); self.waited[(e, name)] = v; self.ninstr += 1
    def _deps(self, rt, wt):
        deps = []
        for t in rt:
            if t in self.last_w: deps.append(self.last_w[t])
        for t in wt:
            if t in self.last_w: deps.append(self.last_w[t])
            deps += self.readers.get(t, [])
        return deps
    def _record(self, ev, rt, wt):
        for t in rt: self.readers.setdefault(t, []).append(ev)
        for t in wt: self.last_w[t] = ev; self.readers[t] = []
    def _toks(self, aps):
        r = []
        for a in aps:
            if a is None or isinstance(a, (int, float)): continue
            r += self.tokens(a)
        return r
    def op(self, e, fn, ins, outs):
        rt = self._toks(ins); wt = self._toks(outs)
        self._wait(e, self._deps(rt, wt))
        ins_ = fn(self.eng[e]); self.ninstr += 1
        if self.cnt[e] >= EPOCH: self._new_sem(e)
        self.cnt[e] += 1
        ins_.then_inc(self.sem[e], 1)
        ev = (self.sem[e], self.cnt[e])
        self._record(ev, rt, wt)
        return ev
    def dma(self, q, out, in_, **kw):
        pool = self.dsem.setdefault(q, {'sems': [], 'vals': [], 'i': 0})
        if len(pool['sems']) < self.n_dma_sems:
            s = self.es.enter_context(self.nc.semaphore(f"d_{q}_{len(pool['sems'])}"))
            pool['sems'].append(s); pool['vals'].append(0)
        i = pool['i'] % self.n_dma_sems; pool['i'] += 1
        s = pool['sems'][i]
        rt = self._toks([in_]); wt = self._toks([out])
        deps = self._deps(rt, wt)
        if pool['vals'][i] > 0: deps.append((s, pool['vals'][i]))
        self._wait(q, deps)
        ins = self.eng[q].dma_start(out=out, in_=in_, **kw); self.ninstr += 1
        pool['vals'][i] += 16
        ins.then_inc(s, 16)
        ev = (s, pool['vals'][i])
        self._record(ev, rt, wt)
        return ev
    def all_events(self):
        evs = []
        for e in self.eng:
            if self.cnt[e] > 0: evs.append((self.sem[e], self.cnt[e]))
        for q, pool in self.dsem.items():
            for s, v in zip(pool['sems'], pool['vals']):
                if v > 0: evs.append((s, v))
        return evs
    def barrier(self, engines=('pe', 'dve', 'act', 'pool', 'sp')):
        evs = self.all_events()
        for e in engines: self._wait(e, evs)
        self.last_w = {}; self.readers = {}

    def tt(self, e, out, a, b, op):
        return self.op(e, lambda en: en.tensor_tensor(out=out, in0=a, in1=b, op=op), [a, b], [out])
    def ts(self, e, out, a, s1, op0, s2=None, op1=None, accum=None):
        kw = {}
        if op1 is not None: kw['op1'] = op1
        if accum is not None: kw['accum_out'] = accum
        return self.op(e, lambda en: en.tensor_scalar(out=out, in0=a, scalar1=s1, scalar2=s2, op0=op0, **kw),
                       [a, s1, s2], [out, accum])
    def stt(self, e, out, a, s, b, op0, op1):
        return self.op(e, lambda en: en.scalar_tensor_tensor(out=out, in0=a, scalar=s, in1=b, op0=op0, op1=op1),
                       [a, s, b], [out])
    def act(self, out, a, func, bias=None, scale=None, accum=None):
        kw = {}
        if bias is not None: kw['bias'] = bias
        if scale is not None: kw['scale'] = scale
        if accum is not None: kw['accum_out'] = accum
        return self.op('act', lambda en: en.activation(out=out, in_=a, func=func, **kw), [a, bias, scale], [out, accum])
    def copy(self, e, out, a):
        if e == 'act':
            return self.op(e, lambda en: en.copy(out=out, in_=a), [a], [out])
        return self.op(e, lambda en: en.tensor_copy(out=out, in_=a), [a], [out])
    def memset(self, e, out, v):
        return self.op(e, lambda en: en.memset(out, v), [], [out])
    def recip(self, out, a):
        return self.op('dve', lambda en: en.reciprocal(out=out, in_=a), [a], [out])
    def mm(self, out, pairs, start=True, stop=True):
        ins = []
        for l, r in pairs: ins += [l, r]
        n = len(pairs)
        def f(en):
            for i, (l, r) in enumerate(pairs):
                x = en.matmul(out, lhsT=l, rhs=r, start=(start and i == 0), stop=(stop and i == n - 1))
            return x
        return self.op('pe', f, ins + ([] if start else [out]), [out])
    def tr(self, out, a, ident):
        return self.op('pe', lambda en: en.transpose(out=out, in_=a, identity=ident), [a, ident], [out])
    def pool_rr(self, key, items):
        i = self.rr.get(key, 0); self.rr[key] = i + 1
        return items[i % len(items)]


def group_tables(C, nseq, slen, pos0s, ntiles):
    t = np.arange(C); seq = t // slen; pos = t % slen
    same = (seq[:, None] == seq[None, :])
    mT_incl = (same & (t[:, None] <= t[None, :])).astype(np.float32)
    mT_strict = (same & (t[:, None] < t[None, :])).astype(np.float32)
    tb = {}
    def pad(a, rows=128):
        o = np.zeros((rows,) + a.shape[1:], np.float32); o[:a.shape[0]] = a; return o
    tb['mTi'] = pad(mT_incl); tb['mTs'] = pad(mT_strict)
    tb['mi'] = pad(mT_incl.T.copy()); tb['ms'] = pad(mT_strict.T.copy())
    tb['ones'] = pad(same.astype(np.float32))
    sm = np.zeros((C, max(nseq, 1)), np.float32); sm[t, seq] = 1.0
    tb['seqmask'] = pad(sm)
    Sh = np.zeros((128, 3, C), np.float32); Sc = np.zeros((128, 3, C), np.float32)
    for d in (1, 2, 3):
        for tt_ in range(C):
            if pos[tt_] - d >= 0: Sh[tt_ - d, d - 1, tt_] = 1.0
            elif slen == 4: Sc[3 * seq[tt_] + (3 + pos[tt_] - d), d - 1, tt_] = 1.0
            else: Sc[3 + pos[tt_] - d, d - 1, tt_] = 1.0
    tb['convSh'] = Sh; tb['convSc'] = Sc
    selB = np.zeros((128, max(nseq, 1), 128), np.float32)
    selB[t, seq, :] = 1.0
    tb['selB'] = selB
    rwSh = np.zeros((128, C), np.float32); rwE0 = np.zeros((128, C), np.float32)
    for tt_ in range(C):
        if pos[tt_] >= 1: rwSh[tt_ - 1, tt_] = 1.0
        elif slen == 4: rwSh[C + seq[tt_], tt_] = 1.0
        else: rwE0[0, tt_] = 1.0
    tb['rwSh'] = rwSh; tb['rwE0'] = rwE0
    tb['mask2'] = np.concatenate([tb['mTs'][:, None, :], tb['mTi'][:, None, :]], axis=1).copy()
    H = 8
    lg = np.log1p(-(2.0 ** (-5.0 - np.arange(H, dtype=np.float32)))).astype(np.float32)
    diff = (t[None, :] - t[:, None]).astype(np.float32)
    DT = np.exp(diff[:, None, :] * lg[None, :, None]) * mT_incl[:, None, :]
    tb['retDT'] = pad(DT.astype(np.float32))
    gq = np.exp((pos[None, None, :] + 1.0) * lg[None, :, None])
    tb['retgq'] = np.broadcast_to(gq, (128, H, C)).astype(np.float32).copy()
    gk = np.exp((slen - 1.0 - pos[:, None]) * lg[None, :])
    tb['retgk'] = pad(gk.astype(np.float32))
    tb['retgC'] = np.exp(slen * lg).astype(np.float64)
    half = 128
    inv = (1.0 / (np.float32(10000.0) ** np.linspace(0.0, 1.0, half, dtype=np.float32))).astype(np.float32)
    cs = np.zeros((ntiles, 128, half), np.float32); sn = np.zeros((ntiles, 128, half), np.float32)
    for ti in range(ntiles):
        p = (pos0s[ti] + pos).astype(np.float32)
        ang = (p[:, None] * inv[None, :]).astype(np.float32)
        cs[ti, :C] = np.cos(ang); sn[ti, :C] = np.sin(ang)
    tb['cos'] = cs; tb['sin'] = sn
    return tb


class Prog:
    def __init__(self, TP, NS, layers=(0, 1, 2, 3), final=True, dbg=99):
        self.dbg = dbg
        self.TP, self.NS, self.layers, self.final = TP, NS, tuple(layers), final
        self.NTP = TP // 128
        self.CS = NS * 4
        self.consts = {}
        self.gp = group_tables(128, 1, 128, [128 * i for i in range(self.NTP)], self.NTP)
        self.gs = group_tables(self.CS, NS, 4, [16384], 1)

    def build(self):
        nc = bass.Bass("TRN2", target_bir_lowering=False)
        self.nc = nc
        es = contextlib.ExitStack()
        with es:
            kb = KB(nc, es); self.kb = kb
            self.declare_io()
            self.setup_common()
            src_p, src_s = self.io['x_prompt'], self.io['x_sample']
            for li in self.layers:
                last = (li == self.layers[-1])
                with contextlib.ExitStack() as les:
                    self.les = les
                    m = li % 4
                    fn = [self.layer_rwkv, self.layer_ret, self.layer_ssd, self.layer_gla][m]
                    fn(li, src_p, src_s, last and self.final)
                    kb.barrier()
                src_p, src_s = self.io['y_prompt'], self.io['y_sample']
            kb.barrier(engines=('sp',))
        return nc

    def declare_io(self):
        kb, nc = self.kb, self.nc
        TP, NS, CS = self.TP, self.NS, self.CS
        io = {}
        pref = {0: ('rwkv_', 'state_rwkv', 'p_shift', 'p_wkv', 's_shift', 's_wkv'), 1: ('ret_', 'state_ret', 'p_ret', 's_ret'),
                2: ('ssd_', 'state_ssd', 'p_conv', 'p_ssd', 's_conv', 's_ssd'), 3: ('gla_', 'state_gla', 'p_gla', 's_gla')}
        allp = sum(pref.values(), ())
        def need(name):
            if not name.startswith(allp): return True
            return any(name.startswith(pref[l % 4]) for l in self.layers)
        self.need = need
        def inp(name, shape, gran=None):
            if need(name): io[name] = kb.dram(name, shape, F32, kind="ExternalInput", gran=gran)
        def outp(name, shape, gran=None):
            if need(name): io[name] = kb.dram(name, shape, F32, kind="ExternalOutput", gran=gran)
        inp('x_prompt', [TP, D], gran=128 * D); inp('x_sample', [CS, D])
        inp('state_rwkv_shift', [NS, D]); inp('state_rwkv_wkv', [NS, 64, 64, 64])
        inp('state_ret', [NS, 8, 256, 512]); inp('state_ssd_conv', [NS, 3, 6144])
        inp('state_ssd', [NS, 64, 128, 64]); inp('state_gla', [NS, 4, 256, 1024])
        inp('norm_gains', [4, D]); inp('final_norm', [D])
        inp('rwkv_w_in', [D, 16576]); inp('rwkv_mu', [16576]); inp('rwkv_w0', [E]); inp('rwkv_w_up', [96, E])
        inp('rwkv_a0', [E]); inp('rwkv_a_up', [96, E]); inp('rwkv_k_k', [E]); inp('rwkv_k_a', [E])
        inp('rwkv_r_k', [64, 64]); inp('rwkv_ln_w', [E]); inp('rwkv_ln_b', [E]); inp('rwkv_w_out', [E, D])
        inp('ret_w_in', [D, 12288]); inp('ret_w_out', [E, D])
        inp('ssd_w_in', [D, 10304]); inp('ssd_conv_w', [4, 6144]); inp('ssd_conv_b', [6144]); inp('ssd_dt_bias', [64])
        inp('ssd_A_log', [64]); inp('ssd_D', [64]); inp('ssd_norm_w', [E]); inp('ssd_w_out', [E, D])
        inp('gla_w_in', [D, 10256]); inp('gla_a_up', [16, 1024]); inp('gla_a_bias', [1024]); inp('gla_norm_w', [1024])
        inp('gla_w_out', [E, D])
        outp('y_prompt', [TP, D], gran=128 * D); outp('y_sample', [CS, D])
        outp('p_shift', [1, D]); outp('p_wkv', [64, 64, 64]); outp('p_ret', [8, 256, 512]); outp('p_conv', [3, 6144])
        outp('p_ssd', [64, 128, 64]); outp('p_gla', [4, 256, 1024])
        outp('s_shift', [NS, D]); outp('s_wkv', [NS, 64, 64, 64]); outp('s_ret', [NS, 8, 256, 512])
        outp('s_conv', [NS, 3, 6144]); outp('s_ssd', [NS, 64, 128, 64]); outp('s_gla', [NS, 4, 256, 1024])
        for gname, g in (('p', self.gp), ('s', self.gs)):
            for k, v in g.items():
                if isinstance(v, np.ndarray) and v.dtype == np.float32:
                    nm = f"c_{gname}_{k}"
                    self.consts[nm] = v
                    io[nm] = kb.dram(nm, list(v.shape), F32, kind="ExternalInput")
        idn = np.eye(128, dtype=np.float32)
        self.consts['c_ident'] = idn
        io['c_ident'] = kb.dram('c_ident', [128, 128], F32, kind="ExternalInput")
        self.io = io

    def setup_common(self):
        kb = self.kb; io = self.io
        self.identf = kb.sb('identf', [128, 128], F32)
        self.identb = kb.sb('identb', [128, 128], BF16)
        kb.dma('sp', self.identf[:], io['c_ident'][:, :])
        kb.copy('dve', self.identb[:], self.identf[:])
        self.onesf = kb.sb('onesf', [128, 128], F32)
        kb.memset('dve', self.onesf[:], 1.0)
        self.psb = [kb.ps(f'psb{i}', [128, 512], F32) for i in range(8)]
        self.wbufs = [kb.sb(f'wbuf{i}', [128, 16, 512], BF16) for i in range(2)]
        self.wq = []
        self.wissued = 0; self.wused = 0
        self.G = {}
        for gname, g, C in (('p', self.gp, 128), ('s', self.gs, self.CS)):
            G = {'C': C, 'name': gname}
            for k in ('mTi', 'mTs', 'ones', 'seqmask', 'retgk'):
                v = g[k]
                t = kb.sb(f'g{gname}_{k}', list(v.shape), F32)
                kb.dma('sp', t[:], io[f'c_{gname}_{k}'][:, :])
                G[k] = t
            G['retgC'] = [float(x) for x in g['retgC']]
            self.G[gname] = G
        self.mlow = {}
        for gname in ('p', 's'):
            t = kb.sb(f'mlow_{gname}', [128, self.G[gname]['C']], F32)
            kb.dma('sp', t[:], io[f'c_{gname}_ms'][:, :]); self.mlow[gname] = t
        self.G['p']['nseq'] = 1; self.G['s']['nseq'] = self.NS
        self.G['p']['slen'] = 128; self.G['s']['slen'] = 4
        self.xt = kb.sb('xt', [128, D], F32, gran=512)
        self.xnb = kb.sb('xnb', [128, D], BF16)
        self.xnT = kb.sb('xnT', [128, 16, 128], BF16)
        self.sqj = kb.sb('sqj', [128, D], BF16)
        self.stat = [kb.sb(f'stat{i}', [128, 8], F32) for i in range(4)]
        self.gain = kb.sb('gain', [128, D], F32)
        self.ygT = kb.sb('ygT', [128, 32, 128], BF16)

    def psum(self):
        return self.kb.pool_rr('psum', self.psb[0:6])
    def psum_long(self):
        return self.kb.pool_rr('psuml', self.psb[6:8])

    def wplan(self, blocks):
        self.wq = list(blocks); self.wissued = 0; self.wused = 0
    def _wissue(self):
        kb = self.kb
        ap, n = self.wq[self.wissued]
        buf = self.wbufs[self.wissued % 2]
        kb.dma('pool', buf[:, :, 0:n], ap.rearrange("(k p) c -> p k c", p=128))
        self.wissued += 1
    def wnext(self):
        while self.wissued < len(self.wq) and self.wissued < self.wused + 2:
            self._wissue()
        buf = self.wbufs[self.wused % 2]; self.wused += 1
        return buf

    def load_norm(self, src_ap, C, row0=0, xn32=None):
        kb = self.kb
        sl = slice(row0, row0 + C)
        kb.dma('sp', self.xt[sl, :], src_ap)
        st = kb.pool_rr('stat', self.stat)
        kb.act(self.sqj[sl, :], self.xt[sl, :], AF.Square, accum=st[sl, 0:1])
        kb.ts('dve', st[sl, 1:2], st[sl, 0:1], 1.0 / D, ALU.mult, 1e-6, ALU.add)
        kb.act(st[sl, 2:3], st[sl, 1:2], AF.Sqrt)
        kb.recip(st[sl, 3:4], st[sl, 2:3])
        kb.stt('dve', self.xnb[sl, :], self.xt[sl, :], st[sl, 3:4], self.gain[sl, :], ALU.mult, ALU.mult)
        if xn32 is not None:
            kb.stt('dve', xn32[sl, :], self.xt[sl, :], st[sl, 3:4], self.gain[sl, :], ALU.mult, ALU.mult)

    def make_xnT(self, R):
        kb = self.kb
        for half in range(2):
            ps = self.psum()
            pv = ps[:].bitcast(BF16)
            for j in range(8):
                k = half * 8 + j
                kb.tr(pv[:, j * 128:j * 128 + R], self.xnb[0:R, k * 128:(k + 1) * 128], self.identb[0:R, 0:R])
            src = pv.rearrange("p (j t) -> p j t", j=8)[:, :, 0:R]
            kb.copy('dve' if half == 0 else 'act', self.xnT[:, half * 8:(half + 1) * 8, 0:R], src)

    def in_proj(self, R, ncols, P, evac=None):
        kb = self.kb
        nblk = (ncols + 511) // 512
        for b in range(nblk):
            n = min(512, ncols - b * 512)
            w = self.wnext()
            ps = self.psum()
            kb.mm(ps[0:R, 0:n], [(self.xnT[:, k, 0:R], w[:, k, 0:n]) for k in range(16)])
            if evac is not None:
                evac(b, n, ps)
            else:
                kb.copy('act' if b % 2 == 0 else 'dve', P[0:R, b * 512:b * 512 + n], ps[0:R, 0:n])

    def transpose_yg(self, yg, R):
        kb = self.kb
        for q in range(4):
            ps = self.psum(); pv = ps[:].bitcast(BF16)
            for j in range(8):
                k = q * 8 + j
                kb.tr(pv[:, j * 128:j * 128 + R], yg[0:R, k * 128:(k + 1) * 128], self.identb[0:R, 0:R])
            src = pv.rearrange("p (j t) -> p j t", j=8)[:, :, 0:R]
            kb.copy('dve' if q % 2 == 0 else 'act', self.ygT[:, q * 8:(q + 1) * 8, 0:R], src)

    def out_proj(self, R, dst_ap, final):
        kb = self.kb
        pss = [self.psum() for _ in range(4)]
        for cb in range(4):
            for half in range(2):
                w = self.wnext()
                kb.mm(pss[cb][0:R, :], [(self.ygT[:, half * 16 + k, 0:R], w[:, k, :]) for k in range(16)],
                      start=(half == 0), stop=(half == 1))
            kb.tt('dve', self.xt[0:R, cb * 512:(cb + 1) * 512], pss[cb][0:R, :], self.xt[0:R, cb * 512:(cb + 1) * 512], ALU.add)
        if final:
            st = kb.pool_rr('stat', self.stat)
            kb.act(self.sqj[0:R, :], self.xt[0:R, :], AF.Square, accum=st[0:R, 0:1])
            kb.ts('dve', st[0:R, 1:2], st[0:R, 0:1], 1.0 / D, ALU.mult, 1e-6, ALU.add)
            kb.act(st[0:R, 2:3], st[0:R, 1:2], AF.Sqrt)
            kb.recip(st[0:R, 3:4], st[0:R, 2:3])
            kb.stt('dve', self.xt[0:R, :], self.xt[0:R, :], st[0:R, 3:4], self.fgain[0:R, :], ALU.mult, ALU.mult)
        kb.dma('sp', dst_ap, self.xt[0:R, :])

    def wblocks(self, w_in, ncols, w_out):
        bl = []
        for b in range((ncols + 511) // 512):
            n = min(512, ncols - b * 512)
            bl.append((w_in[:, b * 512:b * 512 + n], n))
        for cb in range(4):
            for half in range(2):
                bl.append((w_out[half * 2048:(half + 1) * 2048, cb * 512:(cb + 1) * 512], 512))
        return bl

    def tiles(self, src_p, src_s):
        io = self.io
        for ti in range(self.NTP):
            yield ('p', ti, src_p[ti * 128:(ti + 1) * 128, :], io['y_prompt'][ti * 128:(ti + 1) * 128, :], 128)
        yield ('s', 0, src_s[:, :], io['y_sample'][:, :], self.CS)

    def load_gain(self, li):
        self.kb.dma('sp', self.gain[:], self.io['norm_gains'][li].partition_broadcast(128))

    def layer_ret(self, li, src_p, src_s, final):
        kb, io, les = self.kb, self.io, self.les
        NS = self.NS
        self.load_gain(li)
        P = kb.sb('retP', [128, 4096], F32, gran=512, es=les)
        cosT = kb.sb('ret_cos', [128, 128], F32, es=les); sinT = kb.sb('ret_sin', [128, 128], F32, es=les)
        cosk = kb.sb('ret_cosk', [128, 128], F32, es=les); sink = kb.sb('ret_sink', [128, 128], F32, es=les)
        t1 = kb.sb('ret_t1', [128, 8, 128], F32, es=les); t2 = kb.sb('ret_t2', [128, 8, 128], F32, es=les)
        qr = kb.sb('ret_qr', [128, 8, 256], BF16, es=les); kr = kb.sb('ret_kr', [128, 8, 256], BF16, es=les)
        kd = kb.sb('ret_kd', [128, 8, 256], BF16, es=les)
        vb = kb.sb('ret_vb', [128, E], BF16, gran=512, es=les)
        sg = kb.sb('ret_sg', [128, E], BF16, gran=512, es=les)
        qT = kb.sb('ret_qT', [128, 8, 2, 128], BF16, es=les); qdT = kb.sb('ret_qdT', [128, 8, 2, 128], BF16, es=les)
        kT = kb.sb('ret_kT', [128, 8, 2, 128], BF16, es=les)
        AT = [kb.sb(f'ret_AT{i}', [128, 128], BF16, es=les) for i in range(2)]
        ybuf = P
        yg = vb
        ss = kb.sb('ret_ss', [128, 8], F32, es=les); rs = kb.sb('ret_rs', [128, 8], F32, es=les)
        H = kb.sb('ret_H', [128, 8, 2, 512], F32, gran=1024, es=les)
        Hb = kb.sb('ret_Hb', [128, 8, 2, 512], BF16, gran=1024, es=les)
        Hs = [H[:, i] for i in range(2)]
        Hsb = [Hb[:, i] for i in range(2)]
        qdm = [kb.sb(f'ret_qdm{i}', [128, 2, 128], BF16, es=les) for i in range(2)]
        kdm = [kb.sb(f'ret_kdm{i}', [128, 256], BF16, es=les) for i in range(2)]
        if final:
            self.fgain = kb.sb('fgain', [128, D], F32, es=les)
            kb.dma('sp', self.fgain[:], io['final_norm'].partition_broadcast(128))
        DTt = {}; gqt = {}
        for gname in ('p', 's'):
            DTt[gname] = kb.sb(f'ret_DT{gname}', [128, 8, self.G[gname]['C']], F32, es=les)
            kb.dma('sp', DTt[gname][:], io[f'c_{gname}_retDT'][:, :, :])
            gqt[gname] = kb.sb(f'ret_gq{gname}', [128, 8, self.G[gname]['C']], F32, es=les)
            kb.dma('sp', gqt[gname][:], io[f'c_{gname}_retgq'][:, :, :])
        ntl = self.NTP + 1
        self.wplan(self.wblocks(io['ret_w_in'], 12288, io['ret_w_out']) * ntl)
        for (gname, ti, src, dst, C) in self.tiles(src_p, src_s):
            G = self.G[gname]; nseq = G['nseq']
            first = (gname == 'p' and ti == 0)
            self.load_norm(src, C)
            self.make_xnT(C)
            def evac(bk, n, ps, C=C):
                if bk < 8: kb.copy('act' if bk % 2 else 'dve', P[0:C, bk * 512:(bk + 1) * 512], ps[0:C, :])
                elif bk < 16: kb.copy('act' if bk % 2 else 'dve', vb[0:C, (bk - 8) * 512:(bk - 7) * 512], ps[0:C, :])
                else: kb.act(sg[0:C, (bk - 16) * 512:(bk - 15) * 512], ps[0:C, :], AF.Silu)
            self.in_proj(C, 12288, P, evac=evac)
            if self.dbg <= 1:
                kb.dma('sp', dst, self.xt[0:C, :]); continue
            kb.dma('sp', cosT[:], io[f'c_{gname}_cos'][ti]); kb.dma('sp', sinT[:], io[f'c_{gname}_sin'][ti])
            kb.op('act', lambda en: en.mul(out=cosk[:], in_=cosT[:], mul=1.0 / 16), [cosT[:]], [cosk[:]])
            kb.op('act', lambda en: en.mul(out=sink[:], in_=sinT[:], mul=1.0 / 16), [sinT[:]], [sink[:]])
            for (base, outb, cs_, sn_) in ((0, qr, cosT, sinT), (2048, kr, cosk, sink)):
                x3 = P[0:C, base:base + 2048].rearrange("p (h d) -> p h d", h=8)
                x1 = x3[:, :, 0:128]; x2 = x3[:, :, 128:256]
                cb_ = cs_[0:C, :].unsqueeze(1).to_broadcast([C, 8, 128]); sb_ = sn_[0:C, :].unsqueeze(1).to_broadcast([C, 8, 128])
                kb.tt('dve', t1[0:C], x1, cb_, ALU.mult)
                kb.tt('dve', t2[0:C], x2, sb_, ALU.mult)
                kb.tt('dve', outb[0:C, :, 0:128], t1[0:C], t2[0:C], ALU.subtract)
                kb.tt('dve', t1[0:C], x1, sb_, ALU.mult)
                kb.tt('dve', t2[0:C], x2, cb_, ALU.mult)
                kb.tt('dve', outb[0:C, :, 128:256], t1[0:C], t2[0:C], ALU.add)
            kb.tt('dve', kd[0:C], kr[0:C], G['retgk'][0:C, :].unsqueeze(2).to_broadcast([C, 8, 256]), ALU.mult)
            for (srcb, dsts) in ((qr, (qT, qdT)), (kr, (kT,))):
                for half in range(2):
                    ps = self.psum(); pv = ps[:].bitcast(BF16)
                    for j in range(8):
                        h = half * 4 + j // 2; kc = j % 2
                        kb.tr(pv[:, j * 128:j * 128 + C], srcb[0:C, h, kc * 128:(kc + 1) * 128], self.identb[0:C, 0:C])
                    srcv = pv.rearrange("p (h k t) -> p h k t", h=4, k=2)[:, :, :, 0:C]
                    kb.copy('act', dsts[0][:, half * 4:(half + 1) * 4, :, 0:C], srcv)
                    if len(dsts) > 1:
                        gv = gqt[gname][:, half * 4:(half + 1) * 4, :].unsqueeze(2).to_broadcast([128, 4, 2, C])
                        kb.tt('dve', dsts[1][:, half * 4:(half + 1) * 4, :, 0:C], srcv, gv, ALU.mult)
            if self.dbg <= 2:
                kb.dma('sp', dst, self.xt[0:C, :]); continue
            for h in range(8):
                ps = self.psum()
                kb.mm(ps[0:C, 0:C], [(kT[:, h, kc, 0:C], qT[:, h, kc, 0:C]) for kc in range(2)])
                at = kb.pool_rr('retAT', AT)
                kb.tt('dve', at[0:C, 0:C], ps[0:C, 0:C], DTt[gname][0:C, h, :], ALU.mult)
                yps = self.psum_long()
                vsl = vb[0:C, h * 512:(h + 1) * 512]
                if gname == 'p':
                    if first:
                        kb.mm(yps[0:C, :], [(at[0:C, 0:C], vsl)])
                    else:
                        kb.mm(yps[0:C, :], [(at[0:C, 0:C], vsl)] + [(qdT[:, h, kc, 0:C], Hb[:, h, kc, :]) for kc in range(2)])
                    for kc in range(2):
                        hps = self.psum()
                        kb.mm(hps[:, :], [(kd[0:C, h, kc * 128:(kc + 1) * 128], vsl)])
                        if first:
                            kb.copy('act', H[:, h, kc, :], hps[:, :])
                        else:
                            kb.stt('dve', H[:, h, kc, :], H[:, h, kc, :], G['retgC'][h], hps[:, :], ALU.mult, ALU.add)
                        kb.copy('act', Hb[:, h, kc, :], H[:, h, kc, :])
                else:
                    kb.mm(yps[0:C, :], [(at[0:C, 0:C], vsl)], stop=False)
                    for i in range(nseq):
                        hs = kb.pool_rr('retHs', Hs); hsb = kb.pool_rr('retHsb', Hsb)
                        qm = kb.pool_rr('retqdm', qdm); km = kb.pool_rr('retkdm', kdm)
                        kb.dma('sp', hs[:], io['state_ret'][i, h].rearrange("(kc p) v -> p kc v", p=128))
                        kb.copy('act', hsb[:], hs[:])
                        kb.memset('dve', qm[:, :, 0:C], 0.0)
                        kb.copy('dve', qm[:, :, 4 * i:4 * i + 4], qdT[:, h, :, 4 * i:4 * i + 4])
                        kb.mm(yps[0:C, :], [(qm[:, kc, 0:C], hsb[:, kc, :]) for kc in range(2)], start=False, stop=(i == nseq - 1))
                        kb.ts('dve', km[0:C, :], kd[0:C, h, :], G['seqmask'][0:C, i:i + 1], ALU.mult)
                        for kc in range(2):
                            hps = self.psum()
                            kb.mm(hps[:, :], [(km[0:C, kc * 128:(kc + 1) * 128], vsl)])
                            kb.stt('dve', hs[:, kc, :], hs[:, kc, :], G['retgC'][h], hps[:, :], ALU.mult, ALU.add)
                        kb.dma('sp', io['s_ret'][i, h].rearrange("(kc p) v -> p kc v", p=128), hs[:])
                kb.copy('act', ybuf[0:C, h * 512:(h + 1) * 512], yps[0:C, :])
                kb.act(self.sqj[0:C, 0:512], yps[0:C, :], AF.Square, accum=ss[0:C, h:h + 1])
            if self.dbg <= 3:
                kb.dma('sp', dst, self.xt[0:C, :]); continue
            kb.ts('dve', rs[0:C, :], ss[0:C, :], 1.0 / 512, ALU.mult, 1e-6, ALU.add)
            kb.act(rs[0:C, :], rs[0:C, :], AF.Sqrt)
            kb.recip(rs[0:C, :], rs[0:C, :])
            y3 = ybuf[0:C, :].rearrange("p (h v) -> p h v", h=8)
            kb.tt('dve', y3, y3, rs[0:C, :].unsqueeze(2).to_broadcast([C, 8, 512]), ALU.mult)
            kb.tt('dve', yg[0:C, :], ybuf[0:C, :], sg[0:C, :], ALU.mult)
            self.transpose_yg(yg, C)
            self.out_proj(C, dst, final)
            if gname == 'p' and ti == self.NTP - 1:
                kb.dma('sp', io['p_ret'].rearrange("h (kc p) v -> p h kc v", p=128), H[:])


    def layer_rwkv(self, li, src_p, src_s, final):
        kb, io, les = self.kb, self.io, self.les
        NS = self.NS
        self.load_gain(li)
        sb = lambda n, shp, dt=F32, gran=None: kb.sb(n, shp, dt, gran=gran, es=les)
        Rb = sb('rw_R', [128, E], F32, 512); Kb = sb('rw_K', [128, E], F32, 512)
        vb = sb('rw_vb', [128, E], BF16, 512); sg = sb('rw_sg', [128, E], BF16, 512); yg = sg
        wa = sb('rw_wa', [128, 192], F32); twT = sb('rw_twT', [96, 128], F32); adT = sb('rw_adT', [96, 128], F32)
        pblk = [sb(f'rw_pblk{i}', [128, 512]) for i in range(2)]
        mub = [sb(f'rw_mu{i}', [128, 512]) for i in range(1)]
        cbl = [sb(f'rw_cb{i}', [1, 512]) for i in range(2)]
        tbl = sb('rw_tbl', [128, 7, 512]); lora = sb('rw_lora', [96, 2, 512])
        lw = sb('rw_lw', [128, 512]); Lc = sb('rw_Lc', [128, 512]); Ll = sb('rw_Ll', [128, 512])
        e1 = sb('rw_e1', [128, 512]); kk = sb('rw_kk', [128, 512]); av = sb('rw_a', [128, 512]); kp = sb('rw_kp', [128, 512])
        t1 = sb('rw_t1', [128, 512]); t2 = sb('rw_t2', [128, 512]); ltmp = t1
        st8 = sb('rw_st8', [128, 8]); st8b = sb('rw_st8b', [128, 8]); bsum = sb('rw_bsum', [128, 8])
        Atb = sb('rw_At', [128, 512], BF16); Btb = sb('rw_Bt', [128, 512], BF16); Ktb = sb('rw_Kt', [128, 512], BF16)
        Rtb = sb('rw_Rt', [128, 512], BF16); Bend = sb('rw_Bend', [128, 512], BF16); Kend = sb('rw_Kend', [128, 512], BF16)
        BtT = sb('rw_BtT', [128, 4, 128], BF16); KtT = sb('rw_KtT', [128, 4, 128], BF16); ARt = sb('rw_ARt', [128, 4, 2, 128], BF16)
        NP = [sb(f'rw_NP{i}', [128, 2, 128], BF16) for i in range(1)]; MQ = [sb(f'rw_MQ{i}', [128, 2, 128], BF16) for i in range(1)]
        Xn = [sb(f'rw_X{i}', [128, 128], BF16) for i in range(2)]; XT = [sb(f'rw_XT{i}', [128, 128], BF16) for i in range(2)]
        U = Lc; Ub = sb('rw_Ub', [128, 512], BF16); ARz = sb('rw_ARz', [128, 2, 4, 2, 128], BF16)
        yb = e1
        wlcol = sb('rw_wl', [128, 4, max(NS, 1)])
        H2 = sb('rw_H2', [128, 32, 64], F32, 64); Hb2 = sb('rw_Hb2', [128, 32, 64], BF16, 64)
        HsAll = H2[:, 0:max(NS, 1), :]; HsbAll = Hb2[:, 0:max(NS, 1), :]
        amAll = sb('rw_amAll', [128, max(NS, 1), 2, 64], BF16)
        Sin = [sb(f'rw_Sin{i}', [64, 128]) for i in range(1)]; Sout = [sb(f'rw_Sout{i}', [64, 128]) for i in range(1)]
        bkm = [sb(f'rw_bkm{i}', [128, 2, 128], BF16) for i in range(1)]
        xn32 = self.ygT[:].rearrange("p a b -> p (a b)").bitcast(F32)
        Sh = {}; E0 = {}; M2 = {}
        for gname in ('p', 's'):
            Cg = self.G[gname]['C']
            Sh[gname] = sb(f'rw_Sh{gname}', [128, Cg]); kb.dma('sp', Sh[gname][:], io[f'c_{gname}_rwSh'][:, :])
            E0[gname] = sb(f'rw_E0{gname}', [128, Cg]); kb.dma('sp', E0[gname][:], io[f'c_{gname}_rwE0'][:, :])
            M2[gname] = sb(f'rw_M2{gname}', [128, 2, Cg]); kb.dma('sp', M2[gname][:], io[f'c_{gname}_mask2'][:, :, :])
        carry = kb.dram('rw_carry', [2, 16576], F32)
        kb.reg['rw_carry'] = ('dram', 16576)
        if final:
            self.fgain = sb('fgain', [128, D]); kb.dma('sp', self.fgain[:], io['final_norm'].partition_broadcast(128))
        ntl = self.NTP + 1
        self.wplan(self.wblocks(io['rwkv_w_in'], 16576, io['rwkv_w_out']) * ntl)
        tnames = ['rwkv_w0', 'rwkv_a0', 'rwkv_k_k', 'rwkv_k_a', 'rwkv_ln_w', 'rwkv_ln_b']
        rkflat = io['rwkv_r_k'].rearrange("h k -> (h k)")
        for (gname, ti, src, dst, C) in self.tiles(src_p, src_s):
            G = self.G[gname]; nseq = G['nseq']
            first = (gname == 'p' and ti == 0)
            nlev = 7 if gname == 'p' else 2
            self.load_norm(src, C, xn32=xn32)
            R = C
            if gname == 'p':
                if ti == self.NTP - 1: kb.dma('sp', io['p_shift'][:, :], xn32[C - 1:C, :])
            else:
                for i in range(nseq): kb.dma('sp', io['s_shift'][i:i + 1, :], xn32[4 * i + 3:4 * i + 4, :])
                R = C + nseq
                kb.dma('sp', self.xt[C:R, :], io['state_rwkv_shift'][:, :])
                kb.copy('act', self.xnb[C:R, :], self.xt[C:R, :])
            self.make_xnT(R)
            def evac(bk, n, ps, C=C, R=R, gname=gname, ti=ti, first=first):
                pb_ = kb.pool_rr('rwpblk', pblk); mu_ = kb.pool_rr('rwmu', mub)
                kb.copy('act', pb_[0:R, 0:n], ps[0:R, 0:n])
                kb.dma('sp', mu_[:, 0:n], io['rwkv_mu'][bk * 512:bk * 512 + n].partition_broadcast(128))
                prs = [(Sh[gname][0:R, :], pb_[0:R, 0:n])]
                if gname == 'p':
                    kb.dma('sp', carry[(ti + 1) % 2, bk * 512:bk * 512 + n], pb_[C - 1:C, 0:n])
                    if not first:
                        cb_ = kb.pool_rr('rwcb', cbl)
                        kb.dma('sp', cb_[0:1, 0:n], carry[ti % 2, bk * 512:bk * 512 + n])
                        prs.append((E0[gname][0:1, :], cb_[0:1, 0:n]))
                ps2 = self.psum()
                kb.mm(ps2[0:C, 0:n], prs)
                kb.tt('dve', ltmp[0:C, 0:n], ps2[0:C, 0:n], pb_[0:C, 0:n], ALU.subtract)
                kb.tt('dve', ltmp[0:C, 0:n], ltmp[0:C, 0:n], mu_[0:C, 0:n], ALU.mult)
                if bk < 8: kb.tt('dve', Rb[0:C, bk * 512:(bk + 1) * 512], pb_[0:C, :], ltmp[0:C, :], ALU.add)
                elif bk < 16: kb.tt('dve', Kb[0:C, (bk - 8) * 512:(bk - 7) * 512], pb_[0:C, :], ltmp[0:C, :], ALU.add)
                elif bk < 24: kb.tt('dve', vb[0:C, (bk - 16) * 512:(bk - 15) * 512], pb_[0:C, :], ltmp[0:C, :], ALU.add)
                elif bk < 32:
                    kb.tt('dve', ltmp[0:C, :], pb_[0:C, :], ltmp[0:C, :], ALU.add)
                    kb.act(sg[0:C, (bk - 24) * 512:(bk - 23) * 512], ltmp[0:C, :], AF.Silu)
                else: kb.tt('dve', wa[0:C, :], pb_[0:C, 0:192], ltmp[0:C, 0:192], ALU.add)
            self.in_proj(R, 16576, None, evac=evac)
            kb.act(wa[0:C, 0:96], wa[0:C, 0:96], AF.Tanh)
            for (o, dstT) in ((0, twT), (96, adT)):
                ps = self.psum()
                kb.tr(ps[0:96, 0:C], wa[0:C, o:o + 96], self.identf[0:C, 0:C])
                kb.copy('dve', dstT[:, 0:C], ps[0:96, 0:C])
            for hb in range(8):
                cs = slice(hb * 512, (hb + 1) * 512)
                for j, nm in enumerate(tnames):
                    kb.dma('sp', tbl[:, j, :], io[nm][cs].partition_broadcast(128))
                kb.dma('sp', tbl[:, 6, :], rkflat[cs].partition_broadcast(128))
                kb.dma('sp', lora[:, 0, :], io['rwkv_w_up'][:, cs]); kb.dma('sp', lora[:, 1, :], io['rwkv_a_up'][:, cs])
                h3 = lambda ap: ap.rearrange("p (h d) -> p h d", h=8)
                ps = self.psum()
                kb.mm(ps[0:C, :], [(twT[:, 0:C], lora[:, 0, :])])
                kb.tt('dve', lw[0:C, :], ps[0:C, :], tbl[0:C, 0, :], ALU.add)
                kb.act(lw[0:C, :], lw[0:C, :], AF.Exp, scale=-1.0)
                kb.act(lw[0:C, :], lw[0:C, :], AF.Ln, bias=1.0)
                kb.act(lw[0:C, :], lw[0:C, :], AF.Exp, scale=-1.0, bias=-0.5)
                kb.op('act', lambda en: en.mul(out=lw[0:C, :], in_=lw[0:C, :], mul=-1.0), [lw[0:C, :]], [lw[0:C, :]])
                ps = self.psum()
                kb.mm(ps[0:C, :], [(adT[:, 0:C], lora[:, 1, :])])
                kb.tt('dve', av[0:C, :], ps[0:C, :], tbl[0:C, 1, :], ALU.add)
                kb.act(av[0:C, :], av[0:C, :], AF.Sigmoid)
                kb.tt('dve', kk[0:C, :], Kb[0:C, cs], tbl[0:C, 2, :], ALU.mult)
                kb.tt('dve', t1[0:C, :], kk[0:C, :], kk[0:C, :], ALU.mult)
                kb.op('dve', lambda en: en.tensor_reduce(out=st8[0:C, :], in_=h3(t1[0:C, :]), axis=AX.X, op=ALU.add), [t1[0:C, :]], [st8[0:C, :]])
                kb.act(st8[0:C, :], st8[0:C, :], AF.Sqrt)
                kb.ts('dve', st8[0:C, :], st8[0:C, :], 1e-12, ALU.max)
                kb.recip(st8[0:C, :], st8[0:C, :])
                kb.tt('dve', h3(kk[0:C, :]), h3(kk[0:C, :]), st8[0:C, :].unsqueeze(2).to_broadcast([C, 8, 64]), ALU.mult)
                kb.ts('dve', t1[0:C, :], av[0:C, :], -1.0, ALU.add)
                kb.tt('dve', t1[0:C, :], t1[0:C, :], tbl[0:C, 3, :], ALU.mult)
                kb.ts('dve', t1[0:C, :], t1[0:C, :], 1.0, ALU.add)
                kb.tt('dve', kp[0:C, :], Kb[0:C, cs], t1[0:C, :], ALU.mult)
                kb.tt('dve', t1[0:C, :], Rb[0:C, cs], kp[0:C, :], ALU.mult)
                kb.tt('dve', t1[0:C, :], t1[0:C, :], tbl[0:C, 6, :], ALU.mult)
                kb.op('dve', lambda en: en.tensor_reduce(out=bsum[0:C, :], in_=h3(t1[0:C, :]), axis=AX.X, op=ALU.add), [t1[0:C, :]], [bsum[0:C, :]])
                ps1 = self.psum(); ps2 = self.psum()
                kb.mm(ps1[0:C, :], [(G['mTi'][0:C, 0:C], lw[0:C, :])])
                kb.mm(ps2[0:C, :], [(G['ones'][0:C, 0:C], lw[0:C, :])])
                kb.copy('act', Lc[0:C, :], ps1[0:C, :]); kb.copy('dve', Ll[0:C, :], ps2[0:C, :])
                kb.tt('dve', t1[0:C, :], Lc[0:C, :], lw[0:C, :], ALU.subtract)
                kb.act(e1[0:C, :], t1[0:C, :], AF.Exp)
                kb.stt('dve', Atb[0:C, :], kk[0:C, :], -1.0, e1[0:C, :], ALU.mult, ALU.mult)
                kb.act(e1[0:C, :], Lc[0:C, :], AF.Exp)
                kb.tt('dve', Rtb[0:C, :], Rb[0:C, cs], e1[0:C, :], ALU.mult)
                kb.tt('dve', t2[0:C, :], kk[0:C, :], av[0:C, :], ALU.mult)
                kb.act(e1[0:C, :], Lc[0:C, :], AF.Exp, scale=-1.0)
                kb.tt('dve', Btb[0:C, :], t2[0:C, :], e1[0:C, :], ALU.mult)
                kb.tt('dve', Ktb[0:C, :], kp[0:C, :], e1[0:C, :], ALU.mult)
                kb.tt('dve', t1[0:C, :], Ll[0:C, :], Lc[0:C, :], ALU.subtract)
                kb.act(e1[0:C, :], t1[0:C, :], AF.Exp)
                kb.tt('dve', Bend[0:C, :], t2[0:C, :], e1[0:C, :], ALU.mult)
                kb.tt('dve', Kend[0:C, :], kp[0:C, :], e1[0:C, :], ALU.mult)
                ps = self.psum()
                for j in range(4):
                    kb.mm(ps[:, j * 16:j * 16 + nseq], [(lw[0:C, j * 128:(j + 1) * 128], G['seqmask'][0:C, 0:nseq])])
                kb.act(wlcol[:, :, 0:nseq], ps[:, 0:64].rearrange("p (c s) -> p c s", c=4)[:, :, 0:nseq], AF.Exp)
                for (srcb, dstv) in ((Btb, BtT[:, :, 0:C]), (Ktb, KtT[:, :, 0:C]), (Atb, ARt[:, :, 0, 0:C]), (Rtb, ARt[:, :, 1, 0:C])):
                    ps = self.psum(); pv = ps[:].bitcast(BF16)
                    for j in range(4):
                        kb.tr(pv[:, j * 128:j * 128 + C], srcb[0:C, j * 128:(j + 1) * 128], self.identb[0:C, 0:C])
                    kb.copy('act', dstv, pv[:, 0:512].rearrange("p (j t) -> p j t", j=4)[:, :, 0:C])
                import os as _os
                dlev = int(_os.environ.get('RWDBG_P' if gname == 'p' else 'RWDBG_S', '99'))
                if dlev <= 2: continue
                kb.memset('dve', ARz[:], 0.0)
                kb.copy('act', ARz[0:64, 0, :, :, 0:C], ARt[0:64, :, :, 0:C])
                kb.copy('dve', ARz[64:128, 1, :, :, 0:C], ARt[64:128, :, :, 0:C])
                yps = self.psum_long()
                for hl in range(8):
                    j = hl // 2; h2 = hl % 2; pb = h2 * 64; pair = hb * 4 + j
                    psl = slice(pb, pb + 64); hc = slice(hl * 64, (hl + 1) * 64)
                    np_ = kb.pool_rr('rwNP', NP); mq_ = kb.pool_rr('rwMQ', MQ)
                    ps = self.psum()
                    kb.mm(ps[0:C, 0:2 * C], [(BtT[:, j, 0:C], ARz[:, h2, j, :, 0:C])])
                    kb.tt('dve', np_[0:C, :, 0:C], ps[0:C, 0:2 * C].rearrange("p (a t) -> p a t", a=2), M2[gname][0:C, :, :], ALU.mult)
                    ps = self.psum()
                    kb.mm(ps[0:C, 0:2 * C], [(KtT[:, j, 0:C], ARz[:, h2, j, :, 0:C])])
                    kb.tt('dve', mq_[0:C, :, 0:C], ps[0:C, 0:2 * C].rearrange("p (a t) -> p a t", a=2), M2[gname][0:C, :, :], ALU.mult)
                    x0 = kb.pool_rr('rwX', Xn)
                    ps = self.psum()
                    kb.mm(ps[0:C, 0:C], [(ARz[:, h2, j, 0, 0:C], BtT[:, j, 0:C])])
                    kb.tt('dve', x0[0:C, 0:C], ps[0:C, 0:C], self.mlow[gname][0:C, 0:C], ALU.mult)
                    if dlev <= 3: continue
                    vsl = vb[0:C, hb * 512 + hl * 64:hb * 512 + (hl + 1) * 64]
                    prs = [(mq_[0:C, 0, 0:C], vsl)]
                    ypr = [(mq_[0:C, 1, 0:C], vsl)]
                    if gname == 'p':
                        if not first:
                            prs.append((ARz[:, h2, j, 0, 0:C], Hb2[:, pair, :]))
                            ypr.append((ARz[:, h2, j, 1, 0:C], Hb2[:, pair, :]))
                    else:
                        if h2 == 0:
                            for i in range(nseq):
                                si = kb.pool_rr('rwSin', Sin)
                                kb.dma('sp', si[:, :].rearrange("v (h k) -> v h k", h=2), io['state_rwkv_wkv'][i, 2 * pair:2 * pair + 2].rearrange("h v k -> v h k"))
                                tp = self.psum()
                                kb.tr(tp[:, 0:64], si[:, :], self.identf[0:64, 0:64])
                                kb.copy('dve', HsAll[:, i, :], tp[:, 0:64])
                                kb.copy('dve', HsbAll[:, i, :], tp[:, 0:64])
                        kb.memset('dve', amAll[:], 0.0)
                        for i in range(nseq):
                            kb.copy('dve', amAll[:, i, :, 4 * i:4 * i + 4], ARz[:, h2, j, :, 4 * i:4 * i + 4])
                        for i in range(nseq):
                            prs.append((amAll[:, i, 0, 0:C], HsbAll[:, i, :]))
                            ypr.append((amAll[:, i, 1, 0:C], HsbAll[:, i, :]))
                    ups = self.psum()
                    kb.mm(ups[0:C, 0:64], prs)
                    kb.copy('dve', U[0:C, hc], ups[0:C, 0:64])
                    kb.copy('dve', Ub[0:C, hc], ups[0:C, 0:64])
                    if dlev <= 4: continue
                    xt_cur = np_[0:C, 0, 0:C]; x_cur = x0[0:C, 0:C]
                    for lev in range(nlev):
                        aps = self.psum()
                        kb.mm(aps[0:C, 0:64], [(xt_cur, Ub[0:C, hc])])
                        kb.tt('dve', U[0:C, hc], U[0:C, hc], aps[0:C, 0:64], ALU.add)
                        kb.copy('dve', Ub[0:C, hc], U[0:C, hc])
                        if lev < nlev - 1:
                            xtn = kb.pool_rr('rwXT', XT)
                            p2 = self.psum()
                            kb.mm(p2[0:C, 0:C], [(x_cur, xt_cur)])
                            kb.copy('dve', xtn[0:C, 0:C], p2[0:C, 0:C])
                            if lev < nlev - 2:
                                xn_ = kb.pool_rr('rwX', Xn)
                                p3 = self.psum()
                                kb.mm(p3[0:C, 0:C], [(xt_cur, x_cur)])
                                kb.copy('dve', xn_[0:C, 0:C], p3[0:C, 0:C])
                                x_cur = xn_[0:C, 0:C]
                            xt_cur = xtn[0:C, 0:C]
                    if dlev <= 5: continue
                    ypr.append((np_[0:C, 1, 0:C], Ub[0:C, hc]))
                    kb.mm(yps[0:C, hc], ypr)
                    if dlev <= 6: continue
                    if h2 == 1:
                        pc = slice(j * 128, (j + 1) * 128)
                        vpc = vb[0:C, hb * 512 + j * 128:hb * 512 + (j + 1) * 128]
                        if gname == 'p':
                            hps = self.psum()
                            kb.mm(hps[:, 0:128], [(Bend[0:C, pc], Ub[0:C, pc]), (Kend[0:C, pc], vpc)])
                            for q in range(2):
                                qs = slice(q * 64, (q + 1) * 64)
                                if first:
                                    kb.copy('dve', H2[qs, pair, :], hps[qs, q * 64:(q + 1) * 64])
                                else:
                                    kb.stt('dve', H2[qs, pair, :], H2[qs, pair, :], wlcol[qs, j, 0:1], hps[qs, q * 64:(q + 1) * 64], ALU.mult, ALU.add)
                            kb.copy('dve', Hb2[:, pair, :], H2[:, pair, :])
                        else:
                            for i in range(nseq):
                                bk_ = kb.pool_rr('rwbkm', bkm)
                                kb.ts('dve', bk_[0:C, 0, :], Bend[0:C, pc], G['seqmask'][0:C, i:i + 1], ALU.mult)
                                kb.ts('dve', bk_[0:C, 1, :], Kend[0:C, pc], G['seqmask'][0:C, i:i + 1], ALU.mult)
                                hps = self.psum()
                                kb.mm(hps[:, 0:128], [(bk_[0:C, 0, :], Ub[0:C, pc]), (bk_[0:C, 1, :], vpc)])
                                for q in range(2):
                                    qs = slice(q * 64, (q + 1) * 64)
                                    kb.stt('dve', HsAll[qs, i, :], HsAll[qs, i, :], wlcol[qs, j, i:i + 1], hps[qs, q * 64:(q + 1) * 64], ALU.mult, ALU.add)
                                tp = self.psum()
                                kb.tr(tp[0:64, 0:128], HsAll[:, i, :], self.identf[:, :])
                                so = kb.pool_rr('rwSout', Sout)
                                kb.copy('dve', so[:, :], tp[0:64, 0:128])
                                kb.dma('sp', io['s_wkv'][i, 2 * pair:2 * pair + 2].rearrange("h v k -> v h k"), so[:, :].rearrange("v (h k) -> v h k", h=2))
                kb.copy('act', yb[0:C, :], yps[0:C, :])
                kb.op('dve', lambda en: en.tensor_reduce(out=st8[0:C, :], in_=h3(yb[0:C, :]), axis=AX.X, op=ALU.add), [yb[0:C, :]], [st8[0:C, :]])
                kb.ts('dve', st8[0:C, :], st8[0:C, :], 1.0 / 64, ALU.mult)
                kb.tt('dve', h3(yb[0:C, :]), h3(yb[0:C, :]), st8[0:C, :].unsqueeze(2).to_broadcast([C, 8, 64]), ALU.subtract)
                kb.tt('dve', t1[0:C, :], yb[0:C, :], yb[0:C, :], ALU.mult)
                kb.op('dve', lambda en: en.tensor_reduce(out=st8b[0:C, :], in_=h3(t1[0:C, :]), axis=AX.X, op=ALU.add), [t1[0:C, :]], [st8b[0:C, :]])
                kb.ts('dve', st8b[0:C, :], st8b[0:C, :], 1.0 / 64, ALU.mult, 64e-5, ALU.add)
                kb.act(st8b[0:C, :], st8b[0:C, :], AF.Sqrt)
                kb.recip(st8b[0:C, :], st8b[0:C, :])
                kb.tt('dve', h3(yb[0:C, :]), h3(yb[0:C, :]), st8b[0:C, :].unsqueeze(2).to_broadcast([C, 8, 64]), ALU.mult)
                kb.tt('dve', yb[0:C, :], yb[0:C, :], tbl[0:C, 4, :], ALU.mult)
                kb.tt('dve', yb[0:C, :], yb[0:C, :], tbl[0:C, 5, :], ALU.add)
                kb.tt('dve', h3(t1[0:C, :]), h3(vb[0:C, cs]), bsum[0:C, :].unsqueeze(2).to_broadcast([C, 8, 64]), ALU.mult)
                kb.tt('dve', yb[0:C, :], yb[0:C, :], t1[0:C, :], ALU.add)
                kb.tt('dve', yg[0:C, cs], yb[0:C, :], sg[0:C, cs], ALU.mult)
            if dlev <= 2:
                kb.dma('sp', dst, self.xt[0:C, :]); continue
            self.transpose_yg(yg, C)
            self.out_proj(C, dst, final)
            if gname == 'p' and ti == self.NTP - 1 and dlev > 8:
                for pair in range(32):
                    ps = self.psum()
                    kb.tr(ps[0:64, 0:128], H2[:, pair, :], self.identf[:, :])
                    so = kb.pool_rr('rwSout', Sout)
                    kb.copy('act' if pair % 2 else 'dve', so[:, :], ps[0:64, 0:128])
                    kb.dma('sp', io['p_wkv'][2 * pair:2 * pair + 2].rearrange("h v k -> v h k"), so[:, :].rearrange("v (h k) -> v h k", h=2))

    def layer_ssd(self, li, src_p, src_s, final):
        kb, io, les = self.kb, self.io, self.les
        NS = self.NS
        self.load_gain(li)
        sz = kb.sb('ssd_sz', [128, E], BF16, gran=512, es=les)
        xbc = kb.sb('ssd_xbc', [128, 6144], F32, gran=512, es=les)
        Bb = kb.sb('ssd_Bb', [128, 1024], BF16, es=les); Cb = kb.sb('ssd_Cb', [128, 1024], BF16, es=les)
        BT = kb.sb('ssd_BT', [128, 8, 128], BF16, es=les); CT = kb.sb('ssd_CT', [128, 8, 128], BF16, es=les)
        dtr = kb.sb('ssd_dtr', [128, 64], F32, es=les); dtv = kb.sb('ssd_dt', [128, 64], F32, es=les)
        la = kb.sb('ssd_la', [128, 64], F32, es=les); Lc = kb.sb('ssd_L', [128, 64], F32, es=les)
        Ll = kb.sb('ssd_Ll', [128, 64], F32, es=les); eL = kb.sb('ssd_eL', [128, 64], F32, es=les)
        wend = kb.sb('ssd_wend', [128, 64], F32, es=les); ellB = kb.sb('ssd_ellB', [128, 64], F32, es=les)
        dtb = kb.sb('ssd_dtb', [128, 64], F32, es=les); negA = kb.sb('ssd_negA', [128, 64], F32, es=les)
        Dt = kb.sb('ssd_Dtab', [128, 64], F32, es=les)
        kb.dma('sp', dtb[:], io['ssd_dt_bias'].partition_broadcast(128))
        kb.dma('sp', negA[:], io['ssd_A_log'].partition_broadcast(128))
        kb.dma('sp', Dt[:], io['ssd_D'].partition_broadcast(128))
        kb.act(negA[:], negA[:], AF.Exp)
        kb.op('act', lambda en: en.mul(out=negA[:], in_=negA[:], mul=-1.0), [negA[:]], [negA[:]])
        cw = [kb.sb(f'ssd_cw{i}', [128, 5, 512], F32, es=les) for i in range(2)]
        cblk = [kb.sb(f'ssd_cb{i}', [48, 512], F32, es=les) for i in range(2)]
        acc = kb.sb('ssd_acc', [128, 512], F32, es=les); tmpc = kb.sb('ssd_tmpc', [128, 512], F32, es=les)
        vv = kb.sb('ssd_v', [128, 64, 64], BF16, gran=512, es=les); vdt = [kb.sb(f'ssd_vd{i}', [128, 512], BF16, es=les) for i in range(2)]
        msc = kb.sb('ssd_msc', [128, 128], F32, es=les)
        drhs = [kb.sb(f'ssd_drhs{i}', [128, 128], F32, es=les) for i in range(2)]
        dseg = [kb.sb(f'ssd_dseg{i}', [128, 128], F32, es=les) for i in range(2)]
        AT = [kb.sb(f'ssd_AT{i}', [128, 128], BF16, es=les) for i in range(2)]
        ybuf = kb.sb('ssd_y', [128, E], F32, gran=512, es=les)
        yg = sz
        ss = kb.sb('ssd_ss', [128, 8], F32, es=les); rs = kb.sb('ssd_rs', [128, 8], F32, es=les)
        H = kb.sb('ssd_H', [128, 64, 64], F32, gran=512, es=les)
        Hbt = [kb.sb(f'ssd_Hb{i}', [128, 512], BF16, es=les) for i in range(2)]
        Hst = [kb.sb(f'ssd_Hs{i}', [128, 512], F32, es=les) for i in range(2)]
        cm = [kb.sb(f'ssd_cm{i}', [128, 128], BF16, es=les) for i in range(1)]
        bm = [kb.sb(f'ssd_bm{i}', [128, 128], BF16, es=les) for i in range(1)]
        ShT = {}; ScT = {}; selB = {}
        for gname in ('p', 's'):
            Cg = self.G[gname]['C']
            ShT[gname] = kb.sb(f'ssd_Sh{gname}', [128, 3, Cg], F32, es=les)
            kb.dma('sp', ShT[gname][:], io[f'c_{gname}_convSh'][:, :, :])
            ScT[gname] = kb.sb(f'ssd_Sc{gname}', [128, 3, Cg], F32, es=les)
            kb.dma('sp', ScT[gname][:], io[f'c_{gname}_convSc'][:, :, :])
        selB['s'] = kb.sb('ssd_selB', [128, NS, 128], F32, es=les)
        kb.dma('sp', selB['s'][:], io['c_s_selB'][:, :, :])
        carry = kb.dram('ssd_carry', [2, 3, 6144], F32)
        self.kb.reg['ssd_carry'] = ('dram', 3 * 6144)
        if final:
            self.fgain = kb.sb('fgain', [128, D], F32, es=les)
            kb.dma('sp', self.fgain[:], io['final_norm'].partition_broadcast(128))
        ntl = self.NTP + 1
        self.wplan(self.wblocks(io['ssd_w_in'], 10304, io['ssd_w_out']) * ntl)
        cwv = io['ssd_conv_w']; cbv = io['ssd_conv_b']
        for (gname, ti, src, dst, C) in self.tiles(src_p, src_s):
            G = self.G[gname]; nseq = G['nseq']
            first = (gname == 'p' and ti == 0)
            self.load_norm(src, C)
            self.make_xnT(C)
            def evac(bk, n, ps, C=C):
                e = 'act' if bk % 2 else 'dve'
                if bk < 8: kb.act(sz[0:C, bk * 512:(bk + 1) * 512], ps[0:C, :], AF.Silu)
                elif bk < 20: kb.copy(e, xbc[0:C, (bk - 8) * 512:(bk - 7) * 512], ps[0:C, :])
                else: kb.copy('dve', dtr[0:C, :], ps[0:C, 0:64])
            self.in_proj(C, 10304, None, evac=evac)
            if gname == 'p':
                kb.dma('sp', carry[(ti + 1) % 2], xbc[C - 3:C, :])
                if ti == self.NTP - 1:
                    kb.dma('sp', io['p_conv'][:, :], xbc[C - 3:C, :])
            else:
                for i in range(nseq):
                    kb.dma('sp', io['s_conv'][i], xbc[4 * i + 1:4 * i + 4, :])
            R = 3 if gname == 'p' else 3 * nseq
            for b in range(12):
                cs = slice(b * 512, (b + 1) * 512)
                w_ = kb.pool_rr('ssdcw', cw); cb_ = kb.pool_rr('ssdcb', cblk)
                kb.dma('sp', w_[:, 0:4, :], cwv[:, cs].partition_broadcast(128))
                kb.dma('sp', w_[:, 4, :], cbv[cs].partition_broadcast(128))
                has_c = not first
                if has_c:
                    if gname == 'p': kb.dma('sp', cb_[0:3, :], carry[ti % 2][:, cs])
                    else: kb.dma('sp', cb_[0:R, :], io['state_ssd_conv'][:, :, cs].rearrange("i r c -> (i r) c"))
                kb.tt('dve', acc[0:C, :], xbc[0:C, cs], w_[0:C, 3, :], ALU.mult)
                kb.tt('dve', acc[0:C, :], acc[0:C, :], w_[0:C, 4, :], ALU.add)
                for d in (1, 2, 3):
                    ps = self.psum()
                    prs = [(ShT[gname][0:C, d - 1, :], xbc[0:C, cs])]
                    if has_c: prs.append((ScT[gname][0:R, d - 1, :], cb_[0:R, :]))
                    kb.mm(ps[0:C, :], prs)
                    kb.tt('dve', tmpc[0:C, :], ps[0:C, :], w_[0:C, 3 - d, :], ALU.mult)
                    kb.tt('dve', acc[0:C, :], acc[0:C, :], tmpc[0:C, :], ALU.add)
                if b < 8: kb.act(xbc[0:C, cs], acc[0:C, :], AF.Silu)
                elif b < 10: kb.act(Bb[0:C, (b - 8) * 512:(b - 7) * 512], acc[0:C, :], AF.Silu)
                else: kb.act(Cb[0:C, (b - 10) * 512:(b - 9) * 512], acc[0:C, :], AF.Silu)
            xs3 = xbc[0:C, 0:4096].rearrange("p (h d) -> p h d", h=64)
            kb.tt('dve', dtv[0:C, :], dtr[0:C, :], dtb[0:C, :], ALU.add)
            kb.act(dtv[0:C, :], dtv[0:C, :], AF.Exp)
            kb.act(dtv[0:C, :], dtv[0:C, :], AF.Ln, bias=1.0)
            kb.tt('dve', la[0:C, :], dtv[0:C, :], negA[0:C, :], ALU.mult)
            ps1 = self.psum(); ps2 = self.psum()
            kb.mm(ps1[0:C, 0:64], [(G['mTi'][0:C, 0:C], la[0:C, :])])
            kb.mm(ps2[0:C, 0:64], [(G['ones'][0:C, 0:C], la[0:C, :])])
            kb.copy('act', Lc[0:C, :], ps1[0:C, 0:64]); kb.copy('dve', Ll[0:C, :], ps2[0:C, 0:64])
            kb.act(eL[0:C, :], Lc[0:C, :], AF.Exp)
            kb.tt('dve', wend[0:C, :], Ll[0:C, :], Lc[0:C, :], ALU.subtract)
            kb.act(wend[0:C, :], wend[0:C, :], AF.Exp)
            kb.tt('dve', vv[0:C], xs3, dtv[0:C, :].unsqueeze(2).to_broadcast([C, 64, 64]), ALU.mult)
            for (srcb, dstb) in ((Bb, BT), (Cb, CT)):
                ps = self.psum(); pv = ps[:].bitcast(BF16)
                for j in range(8):
                    kb.tr(pv[:, j * 128:j * 128 + C], srcb[0:C, j * 128:(j + 1) * 128], self.identb[0:C, 0:C])
                kb.copy('act', dstb[:, :, 0:C], pv.rearrange("p (j t) -> p j t", j=8)[:, :, 0:C])
            for g in range(8):
                ps = self.psum()
                kb.mm(ps[0:C, 0:C], [(BT[:, g, 0:C], CT[:, g, 0:C])])
                kb.tt('dve', msc[0:C, 0:C], ps[0:C, 0:C], G['mTi'][0:C, 0:C], ALU.mult)
                yps = self.psum_long()
                for hl in range(8):
                    h = g * 8 + hl
                    dr = kb.pool_rr('ssddr', drhs); ds_ = kb.pool_rr('ssdds', dseg); at = kb.pool_rr('ssdAT', AT)
                    kb.ts('dve', dr[0:C, 0:C], self.identf[0:C, 0:C], Lc[0:C, h:h + 1], ALU.mult)
                    psd = self.psum()
                    kb.mm(psd[0:C, 0:C], [(G['ones'][0:C, 0:C] if gname == 'p' else self.onesf[0:C, 0:C], dr[0:C, 0:C])])
                    kb.ts('dve', ds_[0:C, 0:C], psd[0:C, 0:C], Lc[0:C, h:h + 1], ALU.subtract, 0.0, ALU.min)
                    kb.act(ds_[0:C, 0:C], ds_[0:C, 0:C], AF.Exp)
                    kb.tt('dve', at[0:C, 0:C], ds_[0:C, 0:C], msc[0:C, 0:C], ALU.mult)
                    kb.mm(yps[0:C, hl * 64:(hl + 1) * 64], [(at[0:C, 0:C], vv[0:C, h, :])])
                kb.copy('act', ybuf[0:C, g * 512:(g + 1) * 512], yps[0:C, :])
                gsl = slice(g * 8, (g + 1) * 8)
                vdb = kb.pool_rr('ssdvd', vdt)
                kb.tt('dve', vdb[0:C, :].rearrange("p (h d) -> p h d", h=8), vv[0:C, gsl, :], wend[0:C, gsl].unsqueeze(2).to_broadcast([C, 8, 64]), ALU.mult)
                vdg = vdb[0:C, :]
                Hg = H[:, gsl, :].rearrange("p h d -> p (h d)")
                if gname == 'p':
                    if not first:
                        hb = kb.pool_rr('ssdHb', Hbt)
                        kb.copy('act', hb[:], Hg)
                        zps = self.psum()
                        kb.mm(zps[0:C, :], [(CT[:, g, 0:C], hb[:, :])])
                        kb.tt('dve', zps[0:C, :].rearrange("p (h d) -> p h d", h=8), zps[0:C, :].rearrange("p (h d) -> p h d", h=8),
                              eL[0:C, gsl].unsqueeze(2).to_broadcast([C, 8, 64]), ALU.mult) if False else None
                        ztmp = tmpc
                        kb.tt('dve', ztmp[0:C, :].rearrange("p (h d) -> p h d", h=8), zps[0:C, :].rearrange("p (h d) -> p h d", h=8),
                              eL[0:C, gsl].unsqueeze(2).to_broadcast([C, 8, 64]), ALU.mult)
                        kb.tt('dve', ybuf[0:C, g * 512:(g + 1) * 512], ybuf[0:C, g * 512:(g + 1) * 512], ztmp[0:C, :], ALU.add)
                    hps = self.psum()
                    kb.mm(hps[:, :], [(Bb[0:C, g * 128:(g + 1) * 128], vdg)])
                    if first:
                        kb.copy('act', Hg, hps[:, :])
                    else:
                        if g == 0:
                            psl = self.psum()
                            kb.mm(psl[:, 0:64], [(G['ones'][0:C, :], la[0:C, :])])
                            kb.act(ellB[:, :], psl[:, 0:64], AF.Exp)
                        kb.tt('dve', H[:, gsl, :], H[:, gsl, :], ellB[:, gsl].unsqueeze(2).to_broadcast([128, 8, 64]), ALU.mult)
                        kb.tt('dve', Hg, Hg, hps[:, :], ALU.add)
                else:
                    for i in range(nseq):
                        hs = kb.pool_rr('ssdHs', Hst); hb = kb.pool_rr('ssdHb', Hbt)
                        cmi = kb.pool_rr('ssdcm', cm); bmi = kb.pool_rr('ssdbm', bm)
                        kb.dma('sp', hs[:].rearrange("p (h d) -> p h d", h=8), io['state_ssd'][i, g * 8:(g + 1) * 8].rearrange("h n d -> n h d"))
                        kb.copy('act', hb[:], hs[:])
                        kb.memset('dve', cmi[:, 0:C], 0.0)
                        kb.copy('dve', cmi[:, 4 * i:4 * i + 4], CT[:, g, 4 * i:4 * i + 4])
                        zps = self.psum()
                        kb.mm(zps[0:C, :], [(cmi[:, 0:C], hb[:, :])])
                        ztmp = tmpc
                        kb.tt('dve', ztmp[0:C, :].rearrange("p (h d) -> p h d", h=8), zps[0:C, :].rearrange("p (h d) -> p h d", h=8),
                              eL[0:C, gsl].unsqueeze(2).to_broadcast([C, 8, 64]), ALU.mult)
                        kb.tt('dve', ybuf[0:C, g * 512:(g + 1) * 512], ybuf[0:C, g * 512:(g + 1) * 512], ztmp[0:C, :], ALU.add)
                        kb.ts('dve', bmi[0:C, :], Bb[0:C, g * 128:(g + 1) * 128], G['seqmask'][0:C, i:i + 1], ALU.mult)
                        hps = self.psum()
                        kb.mm(hps[:, :], [(bmi[0:C, :], vdg)])
                        psl = self.psum()
                        kb.mm(psl[:, 0:8], [(selB['s'][0:C, i, :], la[0:C, gsl])])
                        kb.act(ellB[:, 0:8], psl[:, 0:8], AF.Exp)
                        hs3 = hs[:].rearrange("p (h d) -> p h d", h=8)
                        kb.tt('dve', hs3, hs3, ellB[:, 0:8].unsqueeze(2).to_broadcast([128, 8, 64]), ALU.mult)
                        kb.tt('dve', hs[:], hs[:], hps[:, :], ALU.add)
                        kb.dma('sp', io['s_ssd'][i, g * 8:(g + 1) * 8].rearrange("h n d -> n h d"), hs[:].rearrange("p (h d) -> p h d", h=8))
            y3 = ybuf[0:C, :].rearrange("p (h d) -> p h d", h=64)
            for g in range(8):
                gsl = slice(g * 8, (g + 1) * 8)
                kb.tt('dve', tmpc[0:C, :].rearrange("p (h d) -> p h d", h=8), xs3[:, gsl, :], Dt[0:C, gsl].unsqueeze(2).to_broadcast([C, 8, 64]), ALU.mult)
                kb.tt('dve', ybuf[0:C, g * 512:(g + 1) * 512], ybuf[0:C, g * 512:(g + 1) * 512], tmpc[0:C, :], ALU.add)
            kb.tt('dve', ybuf[0:C, :], ybuf[0:C, :], sz[0:C, :], ALU.mult)
            for g in range(8):
                kb.act(self.sqj[0:C, 0:512], ybuf[0:C, g * 512:(g + 1) * 512], AF.Square, accum=ss[0:C, g:g + 1])
            kb.ts('dve', rs[0:C, :], ss[0:C, :], 1.0 / 512, ALU.mult, 1e-5, ALU.add)
            kb.act(rs[0:C, :], rs[0:C, :], AF.Sqrt)
            kb.recip(rs[0:C, :], rs[0:C, :])
            y8 = ybuf[0:C, :].rearrange("p (g v) -> p g v", g=8)
            kb.tt('dve', y8, y8, rs[0:C, :].unsqueeze(2).to_broadcast([C, 8, 512]), ALU.mult)
            for g in range(8):
                w_ = kb.pool_rr('ssdcw', cw)
                kb.dma('sp', w_[:, 0, :], io['ssd_norm_w'][g * 512:(g + 1) * 512].partition_broadcast(128))
                kb.tt('dve', yg[0:C, g * 512:(g + 1) * 512], ybuf[0:C, g * 512:(g + 1) * 512], w_[0:C, 0, :], ALU.mult)
            self.transpose_yg(yg, C)
            self.out_proj(C, dst, final)
            if gname == 'p' and ti == self.NTP - 1:
                kb.dma('sp', io['p_ssd'].rearrange("h n d -> n h d"), H[:])

    def layer_gla(self, li, src_p, src_s, final):
        kb, io, les = self.kb, self.io, self.les
        self.load_gain(li)
        QK = kb.sb('glaQK', [128, 2048], F32, gran=512, es=les)
        vb = kb.sb('gla_vb', [128, E], BF16, gran=512, es=les)
        sg = kb.sb('gla_sg', [128, E], BF16, gran=512, es=les)
        adf = kb.sb('gla_ad', [128, 16], F32, es=les)
        adT = kb.sb('gla_adT', [16, 128], F32, es=les)
        aup = kb.sb('gla_aup', [16, 1024], F32, es=les)
        abias = kb.sb('gla_abias', [128, 1024], F32, es=les)
        nw = kb.sb('gla_nw', [128, 1024], F32, es=les)
        kb.dma('sp', aup[:], io['gla_a_up'][:, :])
        kb.dma('sp', abias[:], io['gla_a_bias'].partition_broadcast(128))
        kb.dma('sp', nw[:], io['gla_norm_w'].partition_broadcast(128))
        la = kb.sb('gla_la', [128, 1024], F32, gran=512, es=les)
        Lc = kb.sb('gla_L', [128, 1024], F32, gran=512, es=les)
        Ll = kb.sb('gla_Ll', [128, 1024], F32, gran=512, es=les)
        tmp = kb.sb('gla_tmp', [128, 1024], F32, gran=512, es=les)
        qe = kb.sb('gla_qe', [128, 1024], BF16, es=les); ke = kb.sb('gla_ke', [128, 1024], BF16, es=les)
        kend = kb.sb('gla_kend', [128, 1024], BF16, es=les)
        qeT = kb.sb('gla_qeT', [128, 8, 128], BF16, es=les); keT = kb.sb('gla_keT', [128, 8, 128], BF16, es=les)
        gcol = kb.sb('gla_gcol', [128, 8, max(self.NS, 1)], F32, es=les)
        AT = [kb.sb(f'gla_AT{i}', [128, 128], BF16, es=les) for i in range(2)]
        ybuf = kb.sb('gla_y', [128, E], F32, gran=512, es=les)
        yg = vb
        ss = kb.sb('gla_ss', [128, 8], F32, es=les); rs = kb.sb('gla_rs', [128, 4], F32, es=les)
        H = kb.sb('gla_H', [128, 4, 2, 1024], F32, gran=1024, es=les)
        Hbt = [kb.sb(f'gla_Hb{i}', [128, 2, 1024], BF16, es=les) for i in range(2)]
        Hs = [H[:, i] for i in range(2)]; Hsb = Hbt
        qdm = [kb.sb(f'gla_qdm{i}', [128, 2, 128], BF16, es=les) for i in range(2)]
        kdm = [kb.sb(f'gla_kdm{i}', [128, 256], BF16, es=les) for i in range(2)]
        if final:
            self.fgain = kb.sb('fgain', [128, D], F32, es=les)
            kb.dma('sp', self.fgain[:], io['final_norm'].partition_broadcast(128))
        ntl = self.NTP + 1
        self.wplan(self.wblocks(io['gla_w_in'], 10256, io['gla_w_out']) * ntl)
        for (gname, ti, src, dst, C) in self.tiles(src_p, src_s):
            G = self.G[gname]; nseq = G['nseq']
            first = (gname == 'p' and ti == 0)
            self.load_norm(src, C)
            self.make_xnT(C)
            def evac(bk, n, ps, C=C):
                e = 'act' if bk % 2 else 'dve'
                if bk < 4: kb.copy(e, QK[0:C, bk * 512:(bk + 1) * 512], ps[0:C, :])
                elif bk < 12: kb.copy(e, vb[0:C, (bk - 4) * 512:(bk - 3) * 512], ps[0:C, :])
                elif bk < 20: kb.act(sg[0:C, (bk - 12) * 512:(bk - 11) * 512], ps[0:C, :], AF.Silu)
                else: kb.copy('dve', adf[0:C, :], ps[0:C, 0:16])
            self.in_proj(C, 10256, None, evac=evac)
            ps = self.psum()
            kb.tr(ps[0:16, 0:C], adf[0:C, 0:16], self.identf[0:C, 0:C])
            kb.copy('dve', adT[:, 0:C], ps[0:16, 0:C])
            for b in range(2):
                bs = slice(b * 512, (b + 1) * 512)
                ps = self.psum()
                kb.mm(ps[0:C, :], [(adT[:, 0:C], aup[:, bs])])
                kb.tt('dve', tmp[0:C, bs], ps[0:C, :], abias[0:C, bs], ALU.add)
                kb.act(tmp[0:C, bs], tmp[0:C, bs], AF.Exp, scale=-1.0)
                kb.act(tmp[0:C, bs], tmp[0:C, bs], AF.Ln, bias=1.0)
                kb.op('act', lambda en, bs=bs: en.mul(out=la[0:C, bs], in_=tmp[0:C, bs], mul=-1.0 / 16), [tmp[0:C, bs]], [la[0:C, bs]])
                ps1 = self.psum(); ps2 = self.psum()
                kb.mm(ps1[0:C, :], [(G['mTi'][0:C, 0:C], la[0:C, bs])])
                kb.mm(ps2[0:C, :], [(G['ones'][0:C, 0:C], la[0:C, bs])])
                kb.copy('act', Lc[0:C, bs], ps1[0:C, :])
                kb.copy('dve', Ll[0:C, bs], ps2[0:C, :])
                kb.act(tmp[0:C, bs], Lc[0:C, bs], AF.Exp)
                kb.stt('dve', qe[0:C, bs], QK[0:C, bs], 1.0 / 16, tmp[0:C, bs], ALU.mult, ALU.mult)
                kb.act(tmp[0:C, bs], Lc[0:C, bs], AF.Exp, scale=-1.0)
                kb.tt('dve', ke[0:C, bs], QK[0:C, 1024 + b * 512:1024 + (b + 1) * 512], tmp[0:C, bs], ALU.mult)
                kb.tt('dve', tmp[0:C, bs], Ll[0:C, bs], Lc[0:C, bs], ALU.subtract)
                kb.act(tmp[0:C, bs], tmp[0:C, bs], AF.Exp)
                kb.tt('dve', kend[0:C, bs], QK[0:C, 1024 + b * 512:1024 + (b + 1) * 512], tmp[0:C, bs], ALU.mult)
            ps = self.psum()
            for c8 in range(8):
                kb.mm(ps[:, c8 * 16:c8 * 16 + nseq], [(la[0:C, c8 * 128:(c8 + 1) * 128], G['seqmask'][0:C, 0:nseq])])
            kb.act(gcol[:, :, 0:nseq], ps[:, 0:128].rearrange("p (c s) -> p c s", c=8)[:, :, 0:nseq], AF.Exp)
            for (srcb, dstb) in ((qe, qeT), (ke, keT)):
                ps = self.psum(); pv = ps[:].bitcast(BF16)
                for j in range(8):
                    kb.tr(pv[:, j * 128:j * 128 + C], srcb[0:C, j * 128:(j + 1) * 128], self.identb[0:C, 0:C])
                kb.copy('act', dstb[:, :, 0:C], pv.rearrange("p (j t) -> p j t", j=8)[:, :, 0:C])
            for h in range(4):
                ps = self.psum()
                kb.mm(ps[0:C, 0:C], [(keT[:, h * 2 + kc, 0:C], qeT[:, h * 2 + kc, 0:C]) for kc in range(2)])
                at = kb.pool_rr('glaAT', AT)
                kb.tt('dve', at[0:C, 0:C], ps[0:C, 0:C], G['mTi'][0:C, 0:C], ALU.mult)
                if gname == 'p' and not first:
                    hbh = kb.pool_rr('glaHsb', Hbt)
                    kb.copy('act', hbh[:], H[:, h])
                for vblk in range(2):
                    vsl = vb[0:C, h * 1024 + vblk * 512:h * 1024 + (vblk + 1) * 512]
                    osl = slice(h * 1024 + vblk * 512, h * 1024 + (vblk + 1) * 512)
                    yps = self.psum_long()
                    if gname == 'p':
                        if first:
                            kb.mm(yps[0:C, :], [(at[0:C, 0:C], vsl)])
                        else:
                            kb.mm(yps[0:C, :], [(at[0:C, 0:C], vsl)] + [(qeT[:, h * 2 + kc, 0:C], hbh[:, kc, vblk * 512:(vblk + 1) * 512]) for kc in range(2)])
                        for kc in range(2):
                            hps = self.psum()
                            kb.mm(hps[:, :], [(kend[0:C, (h * 2 + kc) * 128:(h * 2 + kc + 1) * 128], vsl)])
                            hsl = H[:, h, kc, vblk * 512:(vblk + 1) * 512]
                            if first:
                                kb.copy('act', hsl, hps[:, :])
                            else:
                                kb.stt('dve', hsl, hsl, gcol[:, h * 2 + kc, 0:1], hps[:, :], ALU.mult, ALU.add)
                    else:
                        kb.mm(yps[0:C, :], [(at[0:C, 0:C], vsl)], stop=False)
                        for i in range(nseq):
                            hs = kb.pool_rr('glaHs', Hs); hsb = kb.pool_rr('glaHsb', Hsb)
                            qm = kb.pool_rr('glaqdm', qdm); km = kb.pool_rr('glakdm', kdm)
                            kb.dma('sp', hs[:, :, 0:512], io['state_gla'][i, h, :, vblk * 512:(vblk + 1) * 512].rearrange("(kc p) v -> p kc v", p=128))
                            kb.copy('act', hsb[:, :, 0:512], hs[:, :, 0:512])
                            kb.memset('dve', qm[:, :, 0:C], 0.0)
                            kb.copy('dve', qm[:, :, 4 * i:4 * i + 4], qeT[:, h * 2:h * 2 + 2, 4 * i:4 * i + 4])
                            kb.mm(yps[0:C, :], [(qm[:, kc, 0:C], hsb[:, kc, 0:512]) for kc in range(2)], start=False, stop=(i == nseq - 1))
                            kb.ts('dve', km[0:C, :], kend[0:C, h * 256:(h + 1) * 256], G['seqmask'][0:C, i:i + 1], ALU.mult)
                            for kc in range(2):
                                hps = self.psum()
                                kb.mm(hps[:, :], [(km[0:C, kc * 128:(kc + 1) * 128], vsl)])
                                kb.stt('dve', hs[:, kc, 0:512], hs[:, kc, 0:512], gcol[:, h * 2 + kc, i:i + 1], hps[:, :], ALU.mult, ALU.add)
                            kb.dma('sp', io['s_gla'][i, h, :, vblk * 512:(vblk + 1) * 512].rearrange("(kc p) v -> p kc v", p=128), hs[:, :, 0:512])
                    kb.copy('act', ybuf[0:C, osl], yps[0:C, :])
                    kb.act(self.sqj[0:C, 0:512], yps[0:C, :], AF.Square, accum=ss[0:C, h * 2 + vblk:h * 2 + vblk + 1])
            ss2 = ss[0:C, :].rearrange("p (h b) -> p h b", b=2)
            kb.tt('dve', rs[0:C, :], ss2[:, :, 0], ss2[:, :, 1], ALU.add)
            kb.ts('dve', rs[0:C, :], rs[0:C, :], 1.0 / 1024, ALU.mult, 1e-6, ALU.add)
            kb.act(rs[0:C, :], rs[0:C, :], AF.Sqrt)
            kb.recip(rs[0:C, :], rs[0:C, :])
            y3 = ybuf[0:C, :].rearrange("p (h v) -> p h v", h=4)
            kb.tt('dve', y3, y3, rs[0:C, :].unsqueeze(2).to_broadcast([C, 4, 1024]), ALU.mult)
            kb.tt('dve', y3, y3, nw[0:C, :].unsqueeze(1).to_broadcast([C, 4, 1024]), ALU.mult)
            kb.tt('dve', yg[0:C, :], ybuf[0:C, :], sg[0:C, :], ALU.mult)
            self.transpose_yg(yg, C)
            self.out_proj(C, dst, final)
            if gname == 'p' and ti == self.NTP - 1:
                kb.dma('sp', io['p_gla'].rearrange("h (kc p) v -> p h kc v", p=128), H[:])


_CACHE = {}

DBG = 99
def get_prog(TP, NS, layers=(0, 1, 2, 3), final=True):
    key = (TP, NS, tuple(layers), final, DBG)
    if key not in _CACHE:
        p = Prog(TP, NS, layers, final, DBG)
        p.nc_built = p.build()
        _CACHE[key] = p
    return _CACHE[key]


WEIGHT_NAMES = ['norm_gains', 'final_norm', 'rwkv_w_in', 'rwkv_mu', 'rwkv_w0', 'rwkv_w_up', 'rwkv_a0', 'rwkv_a_up',
                'rwkv_k_k', 'rwkv_k_a', 'rwkv_r_k', 'rwkv_ln_w', 'rwkv_ln_b', 'rwkv_w_out', 'ret_w_in', 'ret_w_out',
                'ssd_w_in', 'ssd_conv_w', 'ssd_conv_b', 'ssd_dt_bias', 'ssd_A_log', 'ssd_D', 'ssd_norm_w', 'ssd_w_out',
                'gla_w_in', 'gla_a_up', 'gla_a_bias', 'gla_norm_w', 'gla_w_out']
STATE_NAMES = ['state_rwkv_shift', 'state_rwkv_wkv', 'state_ret', 'state_ssd_conv', 'state_ssd', 'state_gla']


def run(inputs, n_cores=8, layers=(0, 1, 2, 3), final=True):
    xp = np.asarray(inputs['x_prompt'], np.float32); xs = np.asarray(inputs['x_sample'], np.float32)
    B, TP = xp.shape[0], xp.shape[1]
    DB = xs.shape[0]; NS = DB // n_cores
    prog = get_prog(TP, NS, layers, final)
    in_maps = []
    for c in range(n_cores):
        m = {}
        m['x_prompt'] = np.ascontiguousarray(xp[c % B])
        m['x_sample'] = np.ascontiguousarray(xs[c * NS:(c + 1) * NS].reshape(NS * 4, D))
        for nme in STATE_NAMES:
            if prog.need(nme): m[nme] = np.ascontiguousarray(np.asarray(inputs[nme], np.float32)[c * NS:(c + 1) * NS])
        for nme in WEIGHT_NAMES:
            if prog.need(nme): m[nme] = np.ascontiguousarray(np.asarray(inputs[nme], np.float32))
        for k, v in prog.consts.items(): m[k] = v
        in_maps.append(m)
    res = run_bass_kernel_spmd(prog.nc_built, in_maps, core_ids=list(range(n_cores)))
    R = res.results
    oshape = {'p_shift': (1, D), 'p_wkv': (64, 64, 64), 'p_ret': (8, 256, 512), 'p_conv': (3, 6144), 'p_ssd': (64, 128, 64),
              'p_gla': (4, 256, 1024), 's_shift': (NS, D), 's_wkv': (NS, 64, 64, 64), 's_ret': (NS, 8, 256, 512),
              's_conv': (NS, 3, 6144), 's_ssd': (NS, 64, 128, 64), 's_gla': (NS, 4, 256, 1024)}
    for r in R:
        for k, shp in oshape.items():
            if k not in r: r[k] = np.zeros(shp, np.float32)
    nb = min(B, n_cores)
    def cat(name, shape_tail=None):
        return np.concatenate([R[c][name] for c in range(n_cores)], axis=0)
    y_prompt = np.stack([R[c]['y_prompt'] for c in range(nb)], 0)
    y_sample = cat('y_sample').reshape(DB, 4, D)
    outs = [y_prompt, y_sample]
    outs.append(np.concatenate([R[c]['p_shift'] for c in range(nb)], 0))
    for nme in ('p_wkv', 'p_ret', 'p_conv', 'p_ssd', 'p_gla'):
        outs.append(np.stack([R[c][nme] for c in range(nb)], 0))
    for nme in ('s_shift', 's_wkv', 's_ret', 's_conv', 's_ssd', 's_gla'):
        outs.append(cat(nme))
    return tuple(outs)


def kernel(**inputs):
    return run(inputs, n_cores=8)
```

```python
import math, contextlib
import numpy as np
import concourse.bass as bass
import concourse.mybir as mybir
from concourse.bass_utils import run_bass_kernel_spmd

F32 = mybir.dt.float32; BF16 = mybir.dt.bfloat16
AF = mybir.ActivationFunctionType; ALU = mybir.AluOpType; AX = mybir.AxisListType

D = 2048; E = 4096
EPOCH = 3000


class KB:
    def __init__(self, nc, es, n_dma_sems=6):
        self.nc, self.es = nc, es
        self.eng = {'pe': nc.tensor, 'dve': nc.vector, 'act': nc.scalar, 'pool': nc.gpsimd, 'sp': nc.sync}
        self.sem = {}; self.cnt = {}; self.nsem = 0; self.allsems = []
        for e in self.eng: self._new_sem(e)
        self.waited = {}
        self.last_w = {}; self.readers = {}
        self.dsem = {}; self.n_dma_sems = n_dma_sems
        self.ninstr = 0
        self.reg = {}
        self.rr = {}

    def sb(self, name, shape, dt=F32, gran=None, es=None):
        t = (es or self.es).enter_context(self.nc.sbuf_tensor(name, list(shape), dt))
        self.reg[name] = ('sb', gran); return t
    def ps(self, name, shape, dt=F32, es=None):
        t = (es or self.es).enter_context(self.nc.psum_tensor(name, list(shape), dt))
        self.reg[name] = ('ps', None); return t
    def dram(self, name, shape, dt=F32, kind="Internal", gran=None):
        t = self.nc.dram_tensor(name, list(shape), dt, kind=kind).ap()
        self.reg[name] = ('dram', gran); return t

    def tokens(self, ap):
        name = ap.name
        kind, gran = self.reg.get(name, ('x', None))
        if gran is None: return [name]
        dims = ap.ap; off = ap.offset
        if kind == 'dram':
            lo = off; hi = off + sum((c - 1) * abs(s) for s, c in dims)
        else:
            pstep = dims[0][0]
            lo = off % pstep if pstep else off
            hi = lo + sum((c - 1) * abs(s) for s, c in dims[1:])
        return [(name, b) for b in range(lo // gran, hi // gran + 1)]

    def _new_sem(self, e):
        s = self.es.enter_context(self.nc.semaphore(f"s_{e}_{self.nsem}")); self.nsem += 1
        self.sem[e] = s; self.cnt[e] = 0; self.allsems.append(s)
    def _wait(self, e, deps):
        need = {}
        for ev in deps:
            if ev is None: continue
            s, v = ev
            if need.get(s.name, (None, 0))[1] < v: need[s.name] = (s, v)
        for name, (s, v) in need.items():
            if self.waited.get((e, name), 0) < v:
                self.eng[e].wait_ge(s, v); self.waited[(e, name)] = v; self.ninstr += 1
    def _deps(self, rt, wt):
        deps = []
        for t in rt:
            if t in self.last_w: deps.append(self.last_w[t])
        for t in wt:
            if t in self.last_w: deps.append(self.last_w[t])
            deps += self.readers.get(t, [])
        return deps
    def _record(self, ev, rt, wt):
        for t in rt: self.readers.setdefault(t, []).append(ev)
        for t in wt: self.last_w[t] = ev; self.readers[t] = []
    def _toks(self, aps):
        r = []
        for a in aps:
            if a is None or isinstance(a, (int, float)): continue
            r += self.tokens(a)
        return r
    def op(self, e, fn, ins, outs):
        rt = self._toks(ins); wt = self._toks(outs)
        self._wait(e, self._deps(rt, wt))
        ins_ = fn(self.eng[e]); self.ninstr += 1
        if self.cnt[e] >= EPOCH: self._new_sem(e)
        self.cnt[e] += 1
        ins_.then_inc(self.sem[e], 1)
        ev = (self.sem[e], self.cnt[e])
        self._record(ev, rt, wt)
        return ev
    def dma(self, q, out, in_, **kw):
        pool = self.dsem.setdefault(q, {'sems': [], 'vals': [], 'i': 0})
        if len(pool['sems']) < self.n_dma_sems:
            s = self.es.enter_context(self.nc.semaphore(f"d_{q}_{len(pool['sems'])}"))
            pool['sems'].append(s); pool['vals'].append(0)
        i = pool['i'] % self.n_dma_sems; pool['i'] += 1
        s = pool['sems'][i]
        rt = self._toks([in_]); wt = self._toks([out])
        deps = self._deps(rt, wt)
        if pool['vals'][i] > 0: deps.append((s, pool['vals'][i]))
        self._wait(q, deps)
        ins = self.eng[q].dma_start(out=out, in_=in_, **kw); self.ninstr += 1
        pool['vals'][i] += 16
        ins.then_inc(s, 16)
        ev = (s, pool['vals'][i])
        self._record(ev, rt, wt)
        return ev
    def all_events(self):
        evs = []
        for e in self.eng:
            if self.cnt[e] > 0: evs.append((self.sem[e], self.cnt[e]))
        for q, pool in self.dsem.items():
            for s, v in zip(pool['sems'], pool['vals']):
                if v > 0: evs.append((s, v))
        return evs
    def barrier(self, engines=('pe', 'dve', 'act', 'pool', 'sp')):
        evs = self.all_events()
        for e in engines: self._wait(e, evs)
        self.last_w = {}; self.readers = {}

    def tt(self, e, out, a, b, op):
        return self.op(e, lambda en: en.tensor_tensor(out=out, in0=a, in1=b, op=op), [a, b], [out])
    def ts(self, e, out, a, s1, op0, s2=None, op1=None, accum=None):
        kw = {}
        if op1 is not None: kw['op1'] = op1
        if accum is not None: kw['accum_out'] = accum
        return self.op(e, lambda en: en.tensor_scalar(out=out, in0=a, scalar1=s1, scalar2=s2, op0=op0, **kw),
                       [a, s1, s2], [out, accum])
    def stt(self, e, out, a, s, b, op0, op1):
        return self.op(e, lambda en: en.scalar_tensor_tensor(out=out, in0=a, scalar=s, in1=b, op0=op0, op1=op1),
                       [a, s, b], [out])
    def act(self, out, a, func, bias=None, scale=None, accum=None):
        kw = {}
        if bias is not None: kw['bias'] = bias
        if scale is not None: kw['scale'] = scale
        if accum is not None: kw['accum_out'] = accum
        return self.op('act', lambda en: en.activation(out=out, in_=a, func=func, **kw), [a, bias, scale], [out, accum])
    def copy(self, e, out, a):
        if e == 'act':
            return self.op(e, lambda en: en.copy(out=out, in_=a), [a], [out])
        return self.op(e, lambda en: en.tensor_copy(out=out, in_=a), [a], [out])
    def memset(self, e, out, v):
        return self.op(e, lambda en: en.memset(out, v), [], [out])
    def recip(self, out, a):
        return self.op('dve', lambda en: en.reciprocal(out=out, in_=a), [a], [out])
    def mm(self, out, pairs, start=True, stop=True):
        ins = []
        for l, r in pairs: ins += [l, r]
        n = len(pairs)
        def f(en):
            for i, (l, r) in enumerate(pairs):
                x = en.matmul(out, lhsT=l, rhs=r, start=(start and i == 0), stop=(stop and i == n - 1))
            return x
        return self.op('pe', f, ins + ([] if start else [out]), [out])
    def tr(self, out, a, ident):
        return self.op('pe', lambda en: en.transpose(out=out, in_=a, identity=ident), [a, ident], [out])
    def pool_rr(self, key, items):
        i = self.rr.get(key, 0); self.rr[key] = i + 1
        return items[i % len(items)]


def group_tables(C, nseq, slen, pos0s, ntiles):
    t = np.arange(C); seq = t // slen; pos = t % slen
    same = (seq[:, None] == seq[None, :])
    mT_incl = (same & (t[:, None] <= t[None, :])).astype(np.float32)
    mT_strict = (same & (t[:, None] < t[None, :])).astype(np.float32)
    tb = {}
    def pad(a, rows=128):
        o = np.zeros((rows,) + a.shape[1:], np.float32); o[:a.shape[0]] = a; return o
    tb['mTi'] = pad(mT_incl); tb['mTs'] = pad(mT_strict)
    tb['mi'] = pad(mT_incl.T.copy()); tb['ms'] = pad(mT_strict.T.copy())
    tb['ones'] = pad(same.astype(np.float32))
    sm = np.zeros((C, max(nseq, 1)), np.float32); sm[t, seq] = 1.0
    tb['seqmask'] = pad(sm)
    Sh = np.zeros((128, 3, C), np.float32); Sc = np.zeros((128, 3, C), np.float32)
    for d in (1, 2, 3):
        for tt_ in range(C):
            if pos[tt_] - d >= 0: Sh[tt_ - d, d - 1, tt_] = 1.0
            elif slen == 4: Sc[3 * seq[tt_] + (3 + pos[tt_] - d), d - 1, tt_] = 1.0
            else: Sc[3 + pos[tt_] - d, d - 1, tt_] = 1.0
    tb['convSh'] = Sh; tb['convSc'] = Sc
    selB = np.zeros((128, max(nseq, 1), 128), np.float32)
    selB[t, seq, :] = 1.0
    tb['selB'] = selB
    rwSh = np.zeros((128, C), np.float32); rwE0 = np.zeros((128, C), np.float32)
    for tt_ in range(C):
        if pos[tt_] >= 1: rwSh[tt_ - 1, tt_] = 1.0
        elif slen == 4: rwSh[C + seq[tt_], tt_] = 1.0
        else: rwE0[0, tt_] = 1.0
    tb['rwSh'] = rwSh; tb['rwE0'] = rwE0
    tb['mask2'] = np.concatenate([tb['mTs'][:, None, :], tb['mTi'][:, None, :]], axis=1).copy()
    H = 8
    lg = np.log1p(-(2.0 ** (-5.0 - np.arange(H, dtype=np.float32)))).astype(np.float32)
    diff = (t[None, :] - t[:, None]).astype(np.float32)
    DT = np.exp(diff[:, None, :] * lg[None, :, None]) * mT_incl[:, None, :]
    tb['retDT'] = pad(DT.astype(np.float32))
    gq = np.exp((pos[None, None, :] + 1.0) * lg[None, :, None])
    tb['retgq'] = np.broadcast_to(gq, (128, H, C)).astype(np.float32).copy()
    gk = np.exp((slen - 1.0 - pos[:, None]) * lg[None, :])
    tb['retgk'] = pad(gk.astype(np.float32))
    tb['retgC'] = np.exp(slen * lg).astype(np.float64)
    half = 128
    inv = (1.0 / (np.float32(10000.0) ** np.linspace(0.0, 1.0, half, dtype=np.float32))).astype(np.float32)
    cs = np.zeros((ntiles, 128, half), np.float32); sn = np.zeros((ntiles, 128, half), np.float32)
    for ti in range(ntiles):
        p = (pos0s[ti] + pos).astype(np.float32)
        ang = (p[:, None] * inv[None, :]).astype(np.float32)
        cs[ti, :C] = np.cos(ang); sn[ti, :C] = np.sin(ang)
    tb['cos'] = cs; tb['sin'] = sn
    return tb


class Prog:
    def __init__(self, TP, NS, layers=(0, 1, 2, 3), final=True, dbg=99):
        self.dbg = dbg
        self.TP, self.NS, self.layers, self.final = TP, NS, tuple(layers), final
        self.NTP = TP // 128
        self.CS = NS * 4
        self.consts = {}
        self.gp = group_tables(128, 1, 128, [128 * i for i in range(self.NTP)], self.NTP)
        self.gs = group_tables(self.CS, NS, 4, [16384], 1)

    def build(self):
        nc = bass.Bass("TRN2", target_bir_lowering=False)
        self.nc = nc
        es = contextlib.ExitStack()
        with es:
            kb = KB(nc, es); self.kb = kb
            self.declare_io()
            self.setup_common()
            src_p, src_s = self.io['x_prompt'], self.io['x_sample']
            for li in self.layers:
                last = (li == self.layers[-1])
                with contextlib.ExitStack() as les:
                    self.les = les
                    m = li % 4
                    fn = [self.layer_rwkv, self.layer_ret, self.layer_ssd, self.layer_gla][m]
                    fn(li, src_p, src_s, last and self.final)
                    kb.barrier()
                src_p, src_s = self.io['y_prompt'], self.io['y_sample']
            kb.barrier(engines=('sp',))
        return nc

    def declare_io(self):
        kb, nc = self.kb, self.nc
        TP, NS, CS = self.TP, self.NS, self.CS
        io = {}
        pref = {0: ('rwkv_', 'state_rwkv', 'p_shift', 'p_wkv', 's_shift', 's_wkv'), 1: ('ret_', 'state_ret', 'p_ret', 's_ret'),
                2: ('ssd_', 'state_ssd', 'p_conv', 'p_ssd', 's_conv', 's_ssd'), 3: ('gla_', 'state_gla', 'p_gla', 's_gla')}
        allp = sum(pref.values(), ())
        def need(name):
            if not name.startswith(allp): return True
            return any(name.startswith(pref[l % 4]) for l in self.layers)
        self.need = need
        def inp(name, shape, gran=None):
            if need(name): io[name] = kb.dram(name, shape, F32, kind="ExternalInput", gran=gran)
        def outp(name, shape, gran=None):
            if need(name): io[name] = kb.dram(name, shape, F32, kind="ExternalOutput", gran=gran)
        inp('x_prompt', [TP, D], gran=128 * D); inp('x_sample', [CS, D])
        inp('state_rwkv_shift', [NS, D]); inp('state_rwkv_wkv', [NS, 64, 64, 64])
        inp('state_ret', [NS, 8, 256, 512]); inp('state_ssd_conv', [NS, 3, 6144])
        inp('state_ssd', [NS, 64, 128, 64]); inp('state_gla', [NS, 4, 256, 1024])
        inp('norm_gains', [4, D]); inp('final_norm', [D])
        inp('rwkv_w_in', [D, 16576]); inp('rwkv_mu', [16576]); inp('rwkv_w0', [E]); inp('rwkv_w_up', [96, E])
        inp('rwkv_a0', [E]); inp('rwkv_a_up', [96, E]); inp('rwkv_k_k', [E]); inp('rwkv_k_a', [E])
        inp('rwkv_r_k', [64, 64]); inp('rwkv_ln_w', [E]); inp('rwkv_ln_b', [E]); inp('rwkv_w_out', [E, D])
        inp('ret_w_in', [D, 12288]); inp('ret_w_out', [E, D])
        inp('ssd_w_in', [D, 10304]); inp('ssd_conv_w', [4, 6144]); inp('ssd_conv_b', [6144]); inp('ssd_dt_bias', [64])
        inp('ssd_A_log', [64]); inp('ssd_D', [64]); inp('ssd_norm_w', [E]); inp('ssd_w_out', [E, D])
        inp('gla_w_in', [D, 10256]); inp('gla_a_up', [16, 1024]); inp('gla_a_bias', [1024]); inp('gla_norm_w', [1024])
        inp('gla_w_out', [E, D])
        outp('y_prompt', [TP, D], gran=128 * D); outp('y_sample', [CS, D])
        outp('p_shift', [1, D]); outp('p_wkv', [64, 64, 64]); outp('p_ret', [8, 256, 512]); outp('p_conv', [3, 6144])
        outp('p_ssd', [64, 128, 64]); outp('p_gla', [4, 256, 1024])
        outp('s_shift', [NS, D]); outp('s_wkv', [NS, 64, 64, 64]); outp('s_ret', [NS, 8, 256, 512])
        outp('s_conv', [NS, 3, 6144]); outp('s_ssd', [NS, 64, 128, 64]); outp('s_gla', [NS, 4, 256, 1024])
        for gname, g in (('p', self.gp), ('s', self.gs)):
            for k, v in g.items():
                if isinstance(v, np.ndarray) and v.dtype == np.float32:
                    nm = f"c_{gname}_{k}"
                    self.consts[nm] = v
                    io[nm] = kb.dram(nm, list(v.shape), F32, kind="ExternalInput")
        idn = np.eye(128, dtype=np.float32)
        self.consts['c_ident'] = idn
        io['c_ident'] = kb.dram('c_ident', [128, 128], F32, kind="ExternalInput")
        self.io = io

    def setup_common(self):
        kb = self.kb; io = self.io
        self.identf = kb.sb('identf', [128, 128], F32)
        self.identb = kb.sb('identb', [128, 128], BF16)
        kb.dma('sp', self.identf[:], io['c_ident'][:, :])
        kb.copy('dve', self.identb[:], self.identf[:])
        self.onesf = kb.sb('onesf', [128, 128], F32)
        kb.memset('dve', self.onesf[:], 1.0)
        self.psb = [kb.ps(f'psb{i}', [128, 512], F32) for i in range(8)]
        self.wbufs = [kb.sb(f'wbuf{i}', [128, 16, 512], BF16) for i in range(2)]
        self.wq = []
        self.wissued = 0; self.wused = 0
        self.G = {}
        for gname, g, C in (('p', self.gp, 128), ('s', self.gs, self.CS)):
            G = {'C': C, 'name': gname}
            for k in ('mTi', 'mTs', 'ones', 'seqmask', 'retgk'):
                v = g[k]
                t = kb.sb(f'g{gname}_{k}', list(v.shape), F32)
                kb.dma('sp', t[:], io[f'c_{gname}_{k}'][:, :])
                G[k] = t
            G['retgC'] = [float(x) for x in g['retgC']]
            self.G[gname] = G
        self.mlow = {}
        for gname in ('p', 's'):
            t = kb.sb(f'mlow_{gname}', [128, self.G[gname]['C']], F32)
            kb.dma('sp', t[:], io[f'c_{gname}_ms'][:, :]); self.mlow[gname] = t
        self.G['p']['nseq'] = 1; self.G['s']['nseq'] = self.NS
        self.G['p']['slen'] = 128; self.G['s']['slen'] = 4
        self.xt = kb.sb('xt', [128, D], F32, gran=512)
        self.xnb = kb.sb('xnb', [128, D], BF16)
        self.xnT = kb.sb('xnT', [128, 16, 128], BF16)
        self.stat = [kb.sb(f'stat{i}', [128, 8], F32) for i in range(4)]
        self.gain = kb.sb('gain', [128, D], F32)
        self.ygT = kb.sb('ygT', [128, 32, 128], BF16)

    def psum(self):
        return self.kb.pool_rr('psum', self.psb[0:6])
    def psum_long(self):
        return self.kb.pool_rr('psuml', self.psb[6:8])

    def wplan(self, blocks):
        self.wq = list(blocks); self.wissued = 0; self.wused = 0
    def _wissue(self):
        kb = self.kb
        ap, n = self.wq[self.wissued]
        buf = self.wbufs[self.wissued % 2]
        kb.dma('pool', buf[:, :, 0:n], ap.rearrange("(k p) c -> p k c", p=128))
        self.wissued += 1
    def wnext(self):
        while self.wissued < len(self.wq) and self.wissued < self.wused + 2:
            self._wissue()
        buf = self.wbufs[self.wused % 2]; self.wused += 1
        return buf

    def load_norm(self, src_ap, C, row0=0, xn32=None):
        kb = self.kb
        sl = slice(row0, row0 + C)
        kb.dma('sp', self.xt[sl, :], src_ap)
        st = kb.pool_rr('stat', self.stat)
        kb.act(self.xnb[sl, :], self.xt[sl, :], AF.Square, accum=st[sl, 0:1])
        kb.ts('dve', st[sl, 1:2], st[sl, 0:1], 1.0 / D, ALU.mult, 1e-6, ALU.add)
        kb.act(st[sl, 2:3], st[sl, 1:2], AF.Sqrt)
        kb.recip(st[sl, 3:4], st[sl, 2:3])
        kb.stt('dve', self.xnb[sl, :], self.xt[sl, :], st[sl, 3:4], self.gain[sl, :], ALU.mult, ALU.mult)
        if xn32 is not None:
            kb.stt('dve', xn32[sl, :], self.xt[sl, :], st[sl, 3:4], self.gain[sl, :], ALU.mult, ALU.mult)

    def make_xnT(self, R):
        kb = self.kb
        for half in range(2):
            ps = self.psum()
            pv = ps[:].bitcast(BF16)
            for j in range(8):
                k = half * 8 + j
                kb.tr(pv[:, j * 128:j * 128 + R], self.xnb[0:R, k * 128:(k + 1) * 128], self.identb[0:R, 0:R])
            src = pv.rearrange("p (j t) -> p j t", j=8)[:, :, 0:R]
            kb.copy('dve' if half == 0 else 'act', self.xnT[:, half * 8:(half + 1) * 8, 0:R], src)

    def in_proj(self, R, ncols, P, evac=None):
        kb = self.kb
        nblk = (ncols + 511) // 512
        for b in range(nblk):
            n = min(512, ncols - b * 512)
            w = self.wnext()
            ps = self.psum()
            kb.mm(ps[0:R, 0:n], [(self.xnT[:, k, 0:R], w[:, k, 0:n]) for k in range(16)])
            if evac is not None:
                evac(b, n, ps)
            else:
                kb.copy('act' if b % 2 == 0 else 'dve', P[0:R, b * 512:b * 512 + n], ps[0:R, 0:n])

    def transpose_yg(self, yg, R):
        kb = self.kb
        for q in range(4):
            ps = self.psum(); pv = ps[:].bitcast(BF16)
            for j in range(8):
                k = q * 8 + j
                kb.tr(pv[:, j * 128:j * 128 + R], yg[0:R, k * 128:(k + 1) * 128], self.identb[0:R, 0:R])
            src = pv.rearrange("p (j t) -> p j t", j=8)[:, :, 0:R]
            kb.copy('dve' if q % 2 == 0 else 'act', self.ygT[:, q * 8:(q + 1) * 8, 0:R], src)

    def out_proj(self, R, dst_ap, final):
        kb = self.kb
        pss = [self.psum() for _ in range(4)]
        for cb in range(4):
            for half in range(2):
                w = self.wnext()
                kb.mm(pss[cb][0:R, :], [(self.ygT[:, half * 16 + k, 0:R], w[:, k, :]) for k in range(16)],
                      start=(half == 0), stop=(half == 1))
            kb.tt('dve', self.xt[0:R, cb * 512:(cb + 1) * 512], pss[cb][0:R, :], self.xt[0:R, cb * 512:(cb + 1) * 512], ALU.add)
        if final:
            st = kb.pool_rr('stat', self.stat)
            kb.act(self.xnb[0:R, :], self.xt[0:R, :], AF.Square, accum=st[0:R, 0:1])
            kb.ts('dve', st[0:R, 1:2], st[0:R, 0:1], 1.0 / D, ALU.mult, 1e-6, ALU.add)
            kb.act(st[0:R, 2:3], st[0:R, 1:2], AF.Sqrt)
            kb.recip(st[0:R, 3:4], st[0:R, 2:3])
            kb.stt('dve', self.xt[0:R, :], self.xt[0:R, :], st[0:R, 3:4], self.fgain[0:R, :], ALU.mult, ALU.mult)
        kb.dma('sp', dst_ap, self.xt[0:R, :])

    def wblocks(self, w_in, ncols, w_out):
        bl = []
        for b in range((ncols + 511) // 512):
            n = min(512, ncols - b * 512)
            bl.append((w_in[:, b * 512:b * 512 + n], n))
        for cb in range(4):
            for half in range(2):
                bl.append((w_out[half * 2048:(half + 1) * 2048, cb * 512:(cb + 1) * 512], 512))
        return bl

    def tiles(self, src_p, src_s):
        io = self.io
        for ti in range(self.NTP):
            yield ('p', ti, src_p[ti * 128:(ti + 1) * 128, :], io['y_prompt'][ti * 128:(ti + 1) * 128, :], 128)
        yield ('s', 0, src_s[:, :], io['y_sample'][:, :], self.CS)

    def load_gain(self, li):
        self.kb.dma('sp', self.gain[:], self.io['norm_gains'][li].partition_broadcast(128))

    def layer_ret(self, li, src_p, src_s, final):
        kb, io, les = self.kb, self.io, self.les
        NS = self.NS
        self.load_gain(li)
        P = kb.sb('retP', [128, 4096], F32, gran=512, es=les)
        cosT = kb.sb('ret_cos', [128, 128], F32, es=les); sinT = kb.sb('ret_sin', [128, 128], F32, es=les)
        cosk = kb.sb('ret_cosk', [128, 128], F32, es=les); sink = kb.sb('ret_sink', [128, 128], F32, es=les)
        t1 = kb.sb('ret_t1', [128, 8, 128], F32, es=les); t2 = kb.sb('ret_t2', [128, 8, 128], F32, es=les)
        qr = kb.sb('ret_qr', [128, 8, 256], BF16, es=les); kr = kb.sb('ret_kr', [128, 8, 256], BF16, es=les)
        kd = kb.sb('ret_kd', [128, 8, 256], BF16, es=les)
        vb = kb.sb('ret_vb', [128, E], BF16, gran=512, es=les)
        sg = kb.sb('ret_sg', [128, E], BF16, gran=512, es=les)
        qT = kb.sb('ret_qT', [128, 8, 2, 128], BF16, es=les); qdT = kb.sb('ret_qdT', [128, 8, 2, 128], BF16, es=les)
        kT = kb.sb('ret_kT', [128, 8, 2, 128], BF16, es=les)
        AT = [kb.sb(f'ret_AT{i}', [128, 128], BF16, es=les) for i in range(2)]
        ybuf = P
        yg = vb
        ss = kb.sb('ret_ss', [128, 8], F32, es=les); rs = kb.sb('ret_rs', [128, 8], F32, es=les)
        H = kb.sb('ret_H', [128, 8, 2, 512], F32, gran=1024, es=les)
        Hb = kb.sb('ret_Hb', [128, 8, 2, 512], BF16, gran=1024, es=les)
        Hs = [H[:, i] for i in range(2)]
        Hsb = [Hb[:, i] for i in range(2)]
        qdm = [kb.sb(f'ret_qdm{i}', [128, 2, 128], BF16, es=les) for i in range(2)]
        kdm = [kb.sb(f'ret_kdm{i}', [128, 256], BF16, es=les) for i in range(2)]
        if final:
            self.fgain = kb.sb('fgain', [128, D], F32, es=les)
            kb.dma('sp', self.fgain[:], io['final_norm'].partition_broadcast(128))
        DTt = {}; gqt = {}
        for gname in ('p', 's'):
            DTt[gname] = kb.sb(f'ret_DT{gname}', [128, 8, self.G[gname]['C']], F32, es=les)
            kb.dma('sp', DTt[gname][:], io[f'c_{gname}_retDT'][:, :, :])
            gqt[gname] = kb.sb(f'ret_gq{gname}', [128, 8, self.G[gname]['C']], F32, es=les)
            kb.dma('sp', gqt[gname][:], io[f'c_{gname}_retgq'][:, :, :])
        ntl = self.NTP + 1
        self.wplan(self.wblocks(io['ret_w_in'], 12288, io['ret_w_out']) * ntl)
        for (gname, ti, src, dst, C) in self.tiles(src_p, src_s):
            G = self.G[gname]; nseq = G['nseq']
            first = (gname == 'p' and ti == 0)
            self.load_norm(src, C)
            self.make_xnT(C)
            def evac(bk, n, ps, C=C):
                if bk < 8: kb.copy('act' if bk % 2 else 'dve', P[0:C, bk * 512:(bk + 1) * 512], ps[0:C, :])
                elif bk < 16: kb.copy('act' if bk % 2 else 'dve', vb[0:C, (bk - 8) * 512:(bk - 7) * 512], ps[0:C, :])
                else: kb.act(sg[0:C, (bk - 16) * 512:(bk - 15) * 512], ps[0:C, :], AF.Silu)
            self.in_proj(C, 12288, P, evac=evac)
            if self.dbg <= 1:
                kb.dma('sp', dst, self.xt[0:C, :]); continue
            kb.dma('sp', cosT[:], io[f'c_{gname}_cos'][ti]); kb.dma('sp', sinT[:], io[f'c_{gname}_sin'][ti])
            kb.op('act', lambda en: en.mul(out=cosk[:], in_=cosT[:], mul=1.0 / 16), [cosT[:]], [cosk[:]])
            kb.op('act', lambda en: en.mul(out=sink[:], in_=sinT[:], mul=1.0 / 16), [sinT[:]], [sink[:]])
            for (base, outb, cs_, sn_) in ((0, qr, cosT, sinT), (2048, kr, cosk, sink)):
                x3 = P[0:C, base:base + 2048].rearrange("p (h d) -> p h d", h=8)
                x1 = x3[:, :, 0:128]; x2 = x3[:, :, 128:256]
                cb_ = cs_[0:C, :].unsqueeze(1).to_broadcast([C, 8, 128]); sb_ = sn_[0:C, :].unsqueeze(1).to_broadcast([C, 8, 128])
                kb.tt('dve', t1[0:C], x1, cb_, ALU.mult)
                kb.tt('dve', t2[0:C], x2, sb_, ALU.mult)
                kb.tt('dve', outb[0:C, :, 0:128], t1[0:C], t2[0:C], ALU.subtract)
                kb.tt('dve', t1[0:C], x1, sb_, ALU.mult)
                kb.tt('dve', t2[0:C], x2, cb_, ALU.mult)
                kb.tt('dve', outb[0:C, :, 128:256], t1[0:C], t2[0:C], ALU.add)
            kb.tt('dve', kd[0:C], kr[0:C], G['retgk'][0:C, :].unsqueeze(2).to_broadcast([C, 8, 256]), ALU.mult)
            for (srcb, dsts) in ((qr, (qT, qdT)), (kr, (kT,))):
                for half in range(2):
                    ps = self.psum(); pv = ps[:].bitcast(BF16)
                    for j in range(8):
                        h = half * 4 + j // 2; kc = j % 2
                        kb.tr(pv[:, j * 128:j * 128 + C], srcb[0:C, h, kc * 128:(kc + 1) * 128], self.identb[0:C, 0:C])
                    srcv = pv.rearrange("p (h k t) -> p h k t", h=4, k=2)[:, :, :, 0:C]
                    kb.copy('act', dsts[0][:, half * 4:(half + 1) * 4, :, 0:C], srcv)
                    if len(dsts) > 1:
                        gv = gqt[gname][:, half * 4:(half + 1) * 4, :].unsqueeze(2).to_broadcast([128, 4, 2, C])
                        kb.tt('dve', dsts[1][:, half * 4:(half + 1) * 4, :, 0:C], srcv, gv, ALU.mult)
            if self.dbg <= 2:
                kb.dma('sp', dst, self.xt[0:C, :]); continue
            for h in range(8):
                ps = self.psum()
                kb.mm(ps[0:C, 0:C], [(kT[:, h, kc, 0:C], qT[:, h, kc, 0:C]) for kc in range(2)])
                at = kb.pool_rr('retAT', AT)
                kb.tt('dve', at[0:C, 0:C], ps[0:C, 0:C], DTt[gname][0:C, h, :], ALU.mult)
                yps = self.psum_long()
                vsl = vb[0:C, h * 512:(h + 1) * 512]
                if gname == 'p':
                    if first:
                        kb.mm(yps[0:C, :], [(at[0:C, 0:C], vsl)])
                    else:
                        kb.mm(yps[0:C, :], [(at[0:C, 0:C], vsl)] + [(qdT[:, h, kc, 0:C], Hb[:, h, kc, :]) for kc in range(2)])
                    for kc in range(2):
                        hps = self.psum()
                        kb.mm(hps[:, :], [(kd[0:C, h, kc * 128:(kc + 1) * 128], vsl)])
                        if first:
                            kb.copy('act', H[:, h, kc, :], hps[:, :])
                        else:
                            kb.stt('dve', H[:, h, kc, :], H[:, h, kc, :], G['retgC'][h], hps[:, :], ALU.mult, ALU.add)
                        kb.copy('act', Hb[:, h, kc, :], H[:, h, kc, :])
                else:
                    kb.mm(yps[0:C, :], [(at[0:C, 0:C], vsl)], stop=False)
                    for i in range(nseq):
                        hs = kb.pool_rr('retHs', Hs); hsb = kb.pool_rr('retHsb', Hsb)
                        qm = kb.pool_rr('retqdm', qdm); km = kb.pool_rr('retkdm', kdm)
                        kb.dma('sp', hs[:], io['state_ret'][i, h].rearrange("(kc p) v -> p kc v", p=128))
                        kb.copy('act', hsb[:], hs[:])
                        kb.memset('dve', qm[:, :, 0:C], 0.0)
                        kb.copy('dve', qm[:, :, 4 * i:4 * i + 4], qdT[:, h, :, 4 * i:4 * i + 4])
                        kb.mm(yps[0:C, :], [(qm[:, kc, 0:C], hsb[:, kc, :]) for kc in range(2)], start=False, stop=(i == nseq - 1))
                        kb.ts('dve', km[0:C, :], kd[0:C, h, :], G['seqmask'][0:C, i:i + 1], ALU.mult)
                        for kc in range(2):
                            hps = self.psum()
                            kb.mm(hps[:, :], [(km[0:C, kc * 128:(kc + 1) * 128], vsl)])
                            kb.stt('dve', hs[:, kc, :], hs[:, kc, :], G['retgC'][h], hps[:, :], ALU.mult, ALU.add)
                        kb.dma('sp', io['s_ret'][i, h].rearrange("(kc p) v -> p kc v", p=128), hs[:])
                kb.copy('act', ybuf[0:C, h * 512:(h + 1) * 512], yps[0:C, :])
                kb.act(self.xnb[0:C, 0:512], yps[0:C, :], AF.Square, accum=ss[0:C, h:h + 1])
            if self.dbg <= 3:
                kb.dma('sp', dst, self.xt[0:C, :]); continue
            kb.ts('dve', rs[0:C, :], ss[0:C, :], 1.0 / 512, ALU.mult, 1e-6, ALU.add)
            kb.act(rs[0:C, :], rs[0:C, :], AF.Sqrt)
            kb.recip(rs[0:C, :], rs[0:C, :])
            y3 = ybuf[0:C, :].rearrange("p (h v) -> p h v", h=8)
            kb.tt('dve', y3, y3, rs[0:C, :].unsqueeze(2).to_broadcast([C, 8, 512]), ALU.mult)
            kb.tt('dve', yg[0:C, :], ybuf[0:C, :], sg[0:C, :], ALU.mult)
            self.transpose_yg(yg, C)
            self.out_proj(C, dst, final)
            if gname == 'p' and ti == self.NTP - 1:
                kb.dma('sp', io['p_ret'].rearrange("h (kc p) v -> p h kc v", p=128), H[:])


    def layer_rwkv(self, li, src_p, src_s, final):
        kb, io, les = self.kb, self.io, self.les
        NS = self.NS
        self.load_gain(li)
        sb = lambda n, shp, dt=F32, gran=None: kb.sb(n, shp, dt, gran=gran, es=les)
        Rb = sb('rw_R', [128, E], F32, 512); Kb = sb('rw_K', [128, E], F32, 512)
        vb = sb('rw_vb', [128, E], BF16, 512); sg = sb('rw_sg', [128, E], BF16, 512); yg = sg
        wa = sb('rw_wa', [128, 192], F32); twT = sb('rw_twT', [96, 128], F32); adT = sb('rw_adT', [96, 128], F32)
        pblk = [sb(f'rw_pblk{i}', [128, 512]) for i in range(2)]
        mub = [sb(f'rw_mu{i}', [128, 512]) for i in range(1)]
        cbl = [sb(f'rw_cb{i}', [1, 512]) for i in range(1)]
        tbl = sb('rw_tbl', [128, 7, 512]); lora = sb('rw_lora', [96, 2, 512])
        lw = sb('rw_lw', [128, 512]); Lc = sb('rw_Lc', [128, 512]); Ll = sb('rw_Ll', [128, 512])
        e1 = sb('rw_e1', [128, 512]); kk = sb('rw_kk', [128, 512]); av = sb('rw_a', [128, 512]); kp = sb('rw_kp', [128, 512])
        t1 = sb('rw_t1', [128, 512]); t2 = sb('rw_t2', [128, 512]); ltmp = t1
        st8 = sb('rw_st8', [128, 8]); st8b = sb('rw_st8b', [128, 8]); bsum = sb('rw_bsum', [128, 8])
        Atb = sb('rw_At', [128, 512], BF16); Btb = sb('rw_Bt', [128, 512], BF16); Ktb = sb('rw_Kt', [128, 512], BF16)
        Rtb = sb('rw_Rt', [128, 512], BF16); Bend = sb('rw_Bend', [128, 512], BF16); Kend = sb('rw_Kend', [128, 512], BF16)
        BtT = sb('rw_BtT', [128, 4, 128], BF16); KtT = sb('rw_KtT', [128, 4, 128], BF16); ARt = sb('rw_ARt', [128, 4, 2, 128], BF16)
        NP4 = sb('rw_NP4', [128, 4, 2, 128], BF16); MQ4 = sb('rw_MQ4', [128, 4, 2, 128], BF16)
        X4 = [sb(f'rw_X4{i}', [128, 4, 128], BF16) for i in range(2)]; XT4 = [sb(f'rw_XT4{i}', [128, 4, 128], BF16) for i in range(2)]
        NP = [NP4[:, 0]]; MQ = [MQ4[:, 0]]
        Xn = [X4[i][:, 0, :] for i in range(2)]; XT = [XT4[i][:, 0, :] for i in range(2)]
        U = Lc; Ub = sb('rw_Ub', [128, 512], BF16); ARz = sb('rw_ARz', [128, 2, 4, 2, 128], BF16)
        yb = e1
        wlcol = sb('rw_wl', [128, 4, max(NS, 1)])
        H2 = sb('rw_H2', [128, 32, 64], F32, 64); Hb2 = sb('rw_Hb2', [128, 32, 64], BF16, 64)
        HsAll = H2[:, 0:max(NS, 1), :]; HsbAll = Hb2[:, 0:max(NS, 1), :]
        amAll = sb('rw_amAll', [128, max(NS, 1), 2, 64], BF16)
        Sin = [sb(f'rw_Sin{i}', [64, 128]) for i in range(1)]; Sout = [sb(f'rw_Sout{i}', [64, 128]) for i in range(1)]
        bkm = [sb(f'rw_bkm{i}', [128, 2, 128], BF16) for i in range(1)]
        xn32 = self.ygT[:].rearrange("p a b -> p (a b)").bitcast(F32)
        Sh = {}; E0 = {}; M2 = {}
        for gname in ('p', 's'):
            Cg = self.G[gname]['C']
            Sh[gname] = sb(f'rw_Sh{gname}', [128, Cg]); kb.dma('sp', Sh[gname][:], io[f'c_{gname}_rwSh'][:, :])
            E0[gname] = sb(f'rw_E0{gname}', [128, Cg]); kb.dma('sp', E0[gname][:], io[f'c_{gname}_rwE0'][:, :])
            M2[gname] = sb(f'rw_M2{gname}', [128, 2, Cg]); kb.dma('sp', M2[gname][:], io[f'c_{gname}_mask2'][:, :, :])
        carry = kb.dram('rw_carry', [2, 16576], F32)
        kb.reg['rw_carry'] = ('dram', 16576)
        if final:
            self.fgain = sb('fgain', [128, D]); kb.dma('sp', self.fgain[:], io['final_norm'].partition_broadcast(128))
        ntl = self.NTP + 1
        self.wplan(self.wblocks(io['rwkv_w_in'], 16576, io['rwkv_w_out']) * ntl)
        tnames = ['rwkv_w0', 'rwkv_a0', 'rwkv_k_k', 'rwkv_k_a', 'rwkv_ln_w', 'rwkv_ln_b']
        rkflat = io['rwkv_r_k'].rearrange("h k -> (h k)")
        for (gname, ti, src, dst, C) in self.tiles(src_p, src_s):
            G = self.G[gname]; nseq = G['nseq']
            first = (gname == 'p' and ti == 0)
            nlev = 7 if gname == 'p' else 2
            self.load_norm(src, C, xn32=xn32)
            R = C
            if gname == 'p':
                if ti == self.NTP - 1: kb.dma('sp', io['p_shift'][:, :], xn32[C - 1:C, :])
            else:
                for i in range(nseq): kb.dma('sp', io['s_shift'][i:i + 1, :], xn32[4 * i + 3:4 * i + 4, :])
                R = C + nseq
                kb.dma('sp', self.xt[C:R, :], io['state_rwkv_shift'][:, :])
                kb.copy('act', self.xnb[C:R, :], self.xt[C:R, :])
            self.make_xnT(R)
            def evac(bk, n, ps, C=C, R=R, gname=gname, ti=ti, first=first):
                pb_ = kb.pool_rr('rwpblk', pblk); mu_ = kb.pool_rr('rwmu', mub)
                kb.copy('act', pb_[0:R, 0:n], ps[0:R, 0:n])
                kb.dma('sp', mu_[:, 0:n], io['rwkv_mu'][bk * 512:bk * 512 + n].partition_broadcast(128))
                prs = [(Sh[gname][0:R, :], pb_[0:R, 0:n])]
                if gname == 'p':
                    kb.dma('sp', carry[(ti + 1) % 2, bk * 512:bk * 512 + n], pb_[C - 1:C, 0:n])
                    if not first:
                        cb_ = kb.pool_rr('rwcb', cbl)
                        kb.dma('sp', cb_[0:1, 0:n], carry[ti % 2, bk * 512:bk * 512 + n])
                        prs.append((E0[gname][0:1, :], cb_[0:1, 0:n]))
                ps2 = self.psum()
                kb.mm(ps2[0:C, 0:n], prs)
                kb.tt('dve', ltmp[0:C, 0:n], ps2[0:C, 0:n], pb_[0:C, 0:n], ALU.subtract)
                kb.tt('dve', ltmp[0:C, 0:n], ltmp[0:C, 0:n], mu_[0:C, 0:n], ALU.mult)
                if bk < 8: kb.tt('dve', Rb[0:C, bk * 512:(bk + 1) * 512], pb_[0:C, :], ltmp[0:C, :], ALU.add)
                elif bk < 16: kb.tt('dve', Kb[0:C, (bk - 8) * 512:(bk - 7) * 512], pb_[0:C, :], ltmp[0:C, :], ALU.add)
                elif bk < 24: kb.tt('dve', vb[0:C, (bk - 16) * 512:(bk - 15) * 512], pb_[0:C, :], ltmp[0:C, :], ALU.add)
                elif bk < 32:
                    kb.tt('dve', ltmp[0:C, :], pb_[0:C, :], ltmp[0:C, :], ALU.add)
                    kb.act(sg[0:C, (bk - 24) * 512:(bk - 23) * 512], ltmp[0:C, :], AF.Silu)
                else: kb.tt('dve', wa[0:C, :], pb_[0:C, 0:192], ltmp[0:C, 0:192], ALU.add)
            self.in_proj(R, 16576, None, evac=evac)
            kb.act(wa[0:C, 0:96], wa[0:C, 0:96], AF.Tanh)
            for (o, dstT) in ((0, twT), (96, adT)):
                ps = self.psum()
                kb.tr(ps[0:96, 0:C], wa[0:C, o:o + 96], self.identf[0:C, 0:C])
                kb.copy('dve', dstT[:, 0:C], ps[0:96, 0:C])
            for hb in range(8):
                cs = slice(hb * 512, (hb + 1) * 512)
                for j, nm in enumerate(tnames):
                    kb.dma('sp', tbl[:, j, :], io[nm][cs].partition_broadcast(128))
                kb.dma('sp', tbl[:, 6, :], rkflat[cs].partition_broadcast(128))
                kb.dma('sp', lora[:, 0, :], io['rwkv_w_up'][:, cs]); kb.dma('sp', lora[:, 1, :], io['rwkv_a_up'][:, cs])
                h3 = lambda ap: ap.rearrange("p (h d) -> p h d", h=8)
                ps = self.psum()
                kb.mm(ps[0:C, :], [(twT[:, 0:C], lora[:, 0, :])])
                kb.tt('dve', lw[0:C, :], ps[0:C, :], tbl[0:C, 0, :], ALU.add)
                kb.act(lw[0:C, :], lw[0:C, :], AF.Exp, scale=-1.0)
                kb.act(lw[0:C, :], lw[0:C, :], AF.Ln, bias=1.0)
                kb.act(lw[0:C, :], lw[0:C, :], AF.Exp, scale=-1.0, bias=-0.5)
                kb.op('act', lambda en: en.mul(out=lw[0:C, :], in_=lw[0:C, :], mul=-1.0), [lw[0:C, :]], [lw[0:C, :]])
                ps = self.psum()
                kb.mm(ps[0:C, :], [(adT[:, 0:C], lora[:, 1, :])])
                kb.tt('dve', av[0:C, :], ps[0:C, :], tbl[0:C, 1, :], ALU.add)
                kb.act(av[0:C, :], av[0:C, :], AF.Sigmoid)
                kb.tt('dve', kk[0:C, :], Kb[0:C, cs], tbl[0:C, 2, :], ALU.mult)
                kb.tt('dve', t1[0:C, :], kk[0:C, :], kk[0:C, :], ALU.mult)
                kb.op('dve', lambda en: en.tensor_reduce(out=st8[0:C, :], in_=h3(t1[0:C, :]), axis=AX.X, op=ALU.add), [t1[0:C, :]], [st8[0:C, :]])
                kb.act(st8[0:C, :], st8[0:C, :], AF.Sqrt)
                kb.ts('dve', st8[0:C, :], st8[0:C, :], 1e-12, ALU.max)
                kb.recip(st8[0:C, :], st8[0:C, :])
                kb.tt('dve', h3(kk[0:C, :]), h3(kk[0:C, :]), st8[0:C, :].unsqueeze(2).to_broadcast([C, 8, 64]), ALU.mult)
                kb.ts('dve', t1[0:C, :], av[0:C, :], -1.0, ALU.add)
                kb.tt('dve', t1[0:C, :], t1[0:C, :], tbl[0:C, 3, :], ALU.mult)
                kb.ts('dve', t1[0:C, :], t1[0:C, :], 1.0, ALU.add)
                kb.tt('dve', kp[0:C, :], Kb[0:C, cs], t1[0:C, :], ALU.mult)
                kb.tt('dve', t1[0:C, :], Rb[0:C, cs], kp[0:C, :], ALU.mult)
                kb.tt('dve', t1[0:C, :], t1[0:C, :], tbl[0:C, 6, :], ALU.mult)
                kb.op('dve', lambda en: en.tensor_reduce(out=bsum[0:C, :], in_=h3(t1[0:C, :]), axis=AX.X, op=ALU.add), [t1[0:C, :]], [bsum[0:C, :]])
                ps1 = self.psum(); ps2 = self.psum()
                kb.mm(ps1[0:C, :], [(G['mTi'][0:C, 0:C], lw[0:C, :])])
                kb.mm(ps2[0:C, :], [(G['ones'][0:C, 0:C], lw[0:C, :])])
                kb.copy('act', Lc[0:C, :], ps1[0:C, :]); kb.copy('dve', Ll[0:C, :], ps2[0:C, :])
                kb.tt('dve', t1[0:C, :], Lc[0:C, :], lw[0:C, :], ALU.subtract)
                kb.act(e1[0:C, :], t1[0:C, :], AF.Exp)
                kb.stt('dve', Atb[0:C, :], kk[0:C, :], -1.0, e1[0:C, :], ALU.mult, ALU.mult)
                kb.act(e1[0:C, :], Lc[0:C, :], AF.Exp)
                kb.tt('dve', Rtb[0:C, :], Rb[0:C, cs], e1[0:C, :], ALU.mult)
                kb.tt('dve', t2[0:C, :], kk[0:C, :], av[0:C, :], ALU.mult)
                kb.act(e1[0:C, :], Lc[0:C, :], AF.Exp, scale=-1.0)
                kb.tt('dve', Btb[0:C, :], t2[0:C, :], e1[0:C, :], ALU.mult)
                kb.tt('dve', Ktb[0:C, :], kp[0:C, :], e1[0:C, :], ALU.mult)
                kb.tt('dve', t1[0:C, :], Ll[0:C, :], Lc[0:C, :], ALU.subtract)
                kb.act(e1[0:C, :], t1[0:C, :], AF.Exp)
                kb.tt('dve', Bend[0:C, :], t2[0:C, :], e1[0:C, :], ALU.mult)
                kb.tt('dve', Kend[0:C, :], kp[0:C, :], e1[0:C, :], ALU.mult)
                ps = self.psum()
                for j in range(4):
                    kb.mm(ps[:, j * 16:j * 16 + nseq], [(lw[0:C, j * 128:(j + 1) * 128], G['seqmask'][0:C, 0:nseq])])
                kb.act(wlcol[:, :, 0:nseq], ps[:, 0:64].rearrange("p (c s) -> p c s", c=4)[:, :, 0:nseq], AF.Exp)
                for (srcb, dstv) in ((Btb, BtT[:, :, 0:C]), (Ktb, KtT[:, :, 0:C]), (Atb, ARt[:, :, 0, 0:C]), (Rtb, ARt[:, :, 1, 0:C])):
                    ps = self.psum(); pv = ps[:].bitcast(BF16)
                    for j in range(4):
                        kb.tr(pv[:, j * 128:j * 128 + C], srcb[0:C, j * 128:(j + 1) * 128], self.identb[0:C, 0:C])
                    kb.copy('act', dstv, pv[:, 0:512].rearrange("p (j t) -> p j t", j=4)[:, :, 0:C])
                import os as _os
                dlev = int(_os.environ.get('RWDBG_P' if gname == 'p' else 'RWDBG_S', '99'))
                if dlev <= 2: continue
                kb.memset('dve', ARz[:], 0.0)
                kb.copy('act', ARz[0:64, 0, :, :, 0:C], ARt[0:64, :, :, 0:C])
                kb.copy('dve', ARz[64:128, 1, :, :, 0:C], ARt[64:128, :, :, 0:C])
                yps = self.psum_long()
                if gname == 'p':
                    for half in range(2):
                        hls = [4 * half + q for q in range(4)]
                        h4 = slice(half * 256, (half + 1) * 256)
                        for q, hl in enumerate(hls):
                            j = hl // 2; h2 = hl % 2
                            ps = self.psum()
                            kb.mm(ps[0:C, 0:2 * C], [(BtT[:, j, 0:C], ARz[:, h2, j, :, 0:C])])
                            kb.tt('dve', NP4[0:C, q, :, 0:C], ps[0:C, 0:2 * C].rearrange("p (a t) -> p a t", a=2), M2[gname][0:C, :, :], ALU.mult)
                            ps = self.psum()
                            kb.mm(ps[0:C, 0:2 * C], [(KtT[:, j, 0:C], ARz[:, h2, j, :, 0:C])])
                            kb.tt('dve', MQ4[0:C, q, :, 0:C], ps[0:C, 0:2 * C].rearrange("p (a t) -> p a t", a=2), M2[gname][0:C, :, :], ALU.mult)
                        xc = 0
                        psx = self.psum()
                        for q, hl in enumerate(hls):
                            j = hl // 2; h2 = hl % 2
                            kb.mm(psx[0:C, q * C:(q + 1) * C], [(ARz[:, h2, j, 0, 0:C], BtT[:, j, 0:C])])
                        kb.tt('dve', X4[xc][0:C, :, 0:C], psx[0:C, 0:4 * C].rearrange("p (q t) -> p q t", q=4),
                              self.mlow[gname][0:C, 0:C].unsqueeze(1).to_broadcast([C, 4, C]), ALU.mult)
                        ups = self.psum()
                        yprs = []
                        for q, hl in enumerate(hls):
                            j = hl // 2; h2 = hl % 2; pair = hb * 4 + j
                            vsl = vb[0:C, hb * 512 + hl * 64:hb * 512 + (hl + 1) * 64]
                            prs = [(MQ4[0:C, q, 0, 0:C], vsl)]
                            ypr = [(MQ4[0:C, q, 1, 0:C], vsl)]
                            if not first:
                                prs.append((ARz[:, h2, j, 0, 0:C], Hb2[:, pair, :]))
                                ypr.append((ARz[:, h2, j, 1, 0:C], Hb2[:, pair, :]))
                            yprs.append(ypr)
                            kb.mm(ups[0:C, q * 64:(q + 1) * 64], prs)
                        kb.copy('dve', U[0:C, h4], ups[0:C, 0:256])
                        kb.copy('act', Ub[0:C, h4], U[0:C, h4])
                        xts = [NP4[0:C, q, 0, 0:C] for q in range(4)]
                        xs_ = [X4[xc][0:C, q, 0:C] for q in range(4)]
                        tc_ = 0
                        for lev in range(nlev):
                            aps = self.psum()
                            for q, hl in enumerate(hls):
                                kb.mm(aps[0:C, q * 64:(q + 1) * 64], [(xts[q], Ub[0:C, hl * 64:(hl + 1) * 64])])
                            kb.tt('dve', U[0:C, h4], U[0:C, h4], aps[0:C, 0:256], ALU.add)
                            kb.copy('act', Ub[0:C, h4], U[0:C, h4])
                            if lev < nlev - 1:
                                p2 = self.psum()
                                for q in range(4):
                                    kb.mm(p2[0:C, q * C:(q + 1) * C], [(xs_[q], xts[q])])
                                if lev < nlev - 2:
                                    p3 = self.psum()
                                    for q in range(4):
                                        kb.mm(p3[0:C, q * C:(q + 1) * C], [(xts[q], xs_[q])])
                                xtn = XT4[tc_]; tc_ ^= 1
                                kb.copy('act', xtn[0:C, :, 0:C], p2[0:C, 0:4 * C].rearrange("p (q t) -> p q t", q=4))
                                if lev < nlev - 2:
                                    xc ^= 1
                                    kb.copy('dve', X4[xc][0:C, :, 0:C], p3[0:C, 0:4 * C].rearrange("p (q t) -> p q t", q=4))
                                    xs_ = [X4[xc][0:C, q, 0:C] for q in range(4)]
                                xts = [xtn[0:C, q, 0:C] for q in range(4)]
                        for q, hl in enumerate(hls):
                            hc = slice(hl * 64, (hl + 1) * 64)
                            kb.mm(yps[0:C, hc], yprs[q] + [(NP4[0:C, q, 1, 0:C], Ub[0:C, hc])])
                        for j in (2 * half, 2 * half + 1):
                            pair = hb * 4 + j
                            pc = slice(j * 128, (j + 1) * 128)
                            vpc = vb[0:C, hb * 512 + j * 128:hb * 512 + (j + 1) * 128]
                            hps = self.psum()
                            kb.mm(hps[:, 0:128], [(Bend[0:C, pc], Ub[0:C, pc]), (Kend[0:C, pc], vpc)])
                            for qq in range(2):
                                qs = slice(qq * 64, (qq + 1) * 64)
                                if first:
                                    kb.copy('dve', H2[qs, pair, :], hps[qs, qq * 64:(qq + 1) * 64])
                                else:
                                    kb.stt('dve', H2[qs, pair, :], H2[qs, pair, :], wlcol[qs, j, 0:1], hps[qs, qq * 64:(qq + 1) * 64], ALU.mult, ALU.add)
                            kb.copy('act', Hb2[:, pair, :], H2[:, pair, :])
                for hl in (range(8) if gname == 's' else []):
                    j = hl // 2; h2 = hl % 2; pb = h2 * 64; pair = hb * 4 + j
                    psl = slice(pb, pb + 64); hc = slice(hl * 64, (hl + 1) * 64)
                    np_ = kb.pool_rr('rwNP', NP); mq_ = kb.pool_rr('rwMQ', MQ)
                    ps = self.psum()
                    kb.mm(ps[0:C, 0:2 * C], [(BtT[:, j, 0:C], ARz[:, h2, j, :, 0:C])])
                    kb.tt('dve', np_[0:C, :, 0:C], ps[0:C, 0:2 * C].rearrange("p (a t) -> p a t", a=2), M2[gname][0:C, :, :], ALU.mult)
                    ps = self.psum()
                    kb.mm(ps[0:C, 0:2 * C], [(KtT[:, j, 0:C], ARz[:, h2, j, :, 0:C])])
                    kb.tt('dve', mq_[0:C, :, 0:C], ps[0:C, 0:2 * C].rearrange("p (a t) -> p a t", a=2), M2[gname][0:C, :, :], ALU.mult)
                    x0 = kb.pool_rr('rwX', Xn)
                    ps = self.psum()
                    kb.mm(ps[0:C, 0:C], [(ARz[:, h2, j, 0, 0:C], BtT[:, j, 0:C])])
                    kb.tt('dve', x0[0:C, 0:C], ps[0:C, 0:C], self.mlow[gname][0:C, 0:C], ALU.mult)
                    if dlev <= 3: continue
                    vsl = vb[0:C, hb * 512 + hl * 64:hb * 512 + (hl + 1) * 64]
                    prs = [(mq_[0:C, 0, 0:C], vsl)]
                    ypr = [(mq_[0:C, 1, 0:C], vsl)]
                    if gname == 'p':
                        if not first:
                            prs.append((ARz[:, h2, j, 0, 0:C], Hb2[:, pair, :]))
                            ypr.append((ARz[:, h2, j, 1, 0:C], Hb2[:, pair, :]))
                    else:
                        if h2 == 0:
                            for i in range(nseq):
                                si = kb.pool_rr('rwSin', Sin)
                                kb.dma('sp', si[:, :].rearrange("v (h k) -> v h k", h=2), io['state_rwkv_wkv'][i, 2 * pair:2 * pair + 2].rearrange("h v k -> v h k"))
                                tp = self.psum()
                                kb.tr(tp[:, 0:64], si[:, :], self.identf[0:64, 0:64])
                                kb.copy('dve', HsAll[:, i, :], tp[:, 0:64])
                                kb.copy('dve', HsbAll[:, i, :], tp[:, 0:64])
                        kb.memset('dve', amAll[:], 0.0)
                        for i in range(nseq):
                            kb.copy('dve', amAll[:, i, :, 4 * i:4 * i + 4], ARz[:, h2, j, :, 4 * i:4 * i + 4])
                        for i in range(nseq):
                            prs.append((amAll[:, i, 0, 0:C], HsbAll[:, i, :]))
                            ypr.append((amAll[:, i, 1, 0:C], HsbAll[:, i, :]))
                    ups = self.psum()
                    kb.mm(ups[0:C, 0:64], prs)
                    kb.copy('dve', U[0:C, hc], ups[0:C, 0:64])
                    kb.copy('dve', Ub[0:C, hc], ups[0:C, 0:64])
                    if dlev <= 4: continue
                    xt_cur = np_[0:C, 0, 0:C]; x_cur = x0[0:C, 0:C]
                    for lev in range(nlev):
                        aps = self.psum()
                        kb.mm(aps[0:C, 0:64], [(xt_cur, Ub[0:C, hc])])
                        kb.tt('dve', U[0:C, hc], U[0:C, hc], aps[0:C, 0:64], ALU.add)
                        kb.copy('dve', Ub[0:C, hc], U[0:C, hc])
                        if lev < nlev - 1:
                            xtn = kb.pool_rr('rwXT', XT)
                            p2 = self.psum()
                            kb.mm(p2[0:C, 0:C], [(x_cur, xt_cur)])
                            kb.copy('dve', xtn[0:C, 0:C], p2[0:C, 0:C])
                            if lev < nlev - 2:
                                xn_ = kb.pool_rr('rwX', Xn)
                                p3 = self.psum()
                                kb.mm(p3[0:C, 0:C], [(xt_cur, x_cur)])
                                kb.copy('dve', xn_[0:C, 0:C], p3[0:C, 0:C])
                                x_cur = xn_[0:C, 0:C]
                            xt_cur = xtn[0:C, 0:C]
                    if dlev <= 5: continue
                    ypr.append((np_[0:C, 1, 0:C], Ub[0:C, hc]))
                    kb.mm(yps[0:C, hc], ypr)
                    if dlev <= 6: continue
                    if h2 == 1:
                        pc = slice(j * 128, (j + 1) * 128)
                        vpc = vb[0:C, hb * 512 + j * 128:hb * 512 + (j + 1) * 128]
                        if gname == 'p':
                            hps = self.psum()
                            kb.mm(hps[:, 0:128], [(Bend[0:C, pc], Ub[0:C, pc]), (Kend[0:C, pc], vpc)])
                            for q in range(2):
                                qs = slice(q * 64, (q + 1) * 64)
                                if first:
                                    kb.copy('dve', H2[qs, pair, :], hps[qs, q * 64:(q + 1) * 64])
                                else:
                                    kb.stt('dve', H2[qs, pair, :], H2[qs, pair, :], wlcol[qs, j, 0:1], hps[qs, q * 64:(q + 1) * 64], ALU.mult, ALU.add)
                            kb.copy('dve', Hb2[:, pair, :], H2[:, pair, :])
                        else:
                            for i in range(nseq):
                                bk_ = kb.pool_rr('rwbkm', bkm)
                                kb.ts('dve', bk_[0:C, 0, :], Bend[0:C, pc], G['seqmask'][0:C, i:i + 1], ALU.mult)
                                kb.ts('dve', bk_[0:C, 1, :], Kend[0:C, pc], G['seqmask'][0:C, i:i + 1], ALU.mult)
                                hps = self.psum()
                                kb.mm(hps[:, 0:128], [(bk_[0:C, 0, :], Ub[0:C, pc]), (bk_[0:C, 1, :], vpc)])
                                for q in range(2):
                                    qs = slice(q * 64, (q + 1) * 64)
                                    kb.stt('dve', HsAll[qs, i, :], HsAll[qs, i, :], wlcol[qs, j, i:i + 1], hps[qs, q * 64:(q + 1) * 64], ALU.mult, ALU.add)
                                tp = self.psum()
                                kb.tr(tp[0:64, 0:128], HsAll[:, i, :], self.identf[:, :])
                                so = kb.pool_rr('rwSout', Sout)
                                kb.copy('dve', so[:, :], tp[0:64, 0:128])
                                kb.dma('sp', io['s_wkv'][i, 2 * pair:2 * pair + 2].rearrange("h v k -> v h k"), so[:, :].rearrange("v (h k) -> v h k", h=2))
                kb.copy('act', yb[0:C, :], yps[0:C, :])
                kb.op('dve', lambda en: en.tensor_reduce(out=st8[0:C, :], in_=h3(yb[0:C, :]), axis=AX.X, op=ALU.add), [yb[0:C, :]], [st8[0:C, :]])
                kb.ts('dve', st8[0:C, :], st8[0:C, :], 1.0 / 64, ALU.mult)
                kb.tt('dve', h3(yb[0:C, :]), h3(yb[0:C, :]), st8[0:C, :].unsqueeze(2).to_broadcast([C, 8, 64]), ALU.subtract)
                kb.tt('dve', t1[0:C, :], yb[0:C, :], yb[0:C, :], ALU.mult)
                kb.op('dve', lambda en: en.tensor_reduce(out=st8b[0:C, :], in_=h3(t1[0:C, :]), axis=AX.X, op=ALU.add), [t1[0:C, :]], [st8b[0:C, :]])
                kb.ts('dve', st8b[0:C, :], st8b[0:C, :], 1.0 / 64, ALU.mult, 64e-5, ALU.add)
                kb.act(st8b[0:C, :], st8b[0:C, :], AF.Sqrt)
                kb.recip(st8b[0:C, :], st8b[0:C, :])
                kb.tt('dve', h3(yb[0:C, :]), h3(yb[0:C, :]), st8b[0:C, :].unsqueeze(2).to_broadcast([C, 8, 64]), ALU.mult)
                kb.tt('dve', yb[0:C, :], yb[0:C, :], tbl[0:C, 4, :], ALU.mult)
                kb.tt('dve', yb[0:C, :], yb[0:C, :], tbl[0:C, 5, :], ALU.add)
                kb.tt('dve', h3(t1[0:C, :]), h3(vb[0:C, cs]), bsum[0:C, :].unsqueeze(2).to_broadcast([C, 8, 64]), ALU.mult)
                kb.tt('dve', yb[0:C, :], yb[0:C, :], t1[0:C, :], ALU.add)
                kb.tt('dve', yg[0:C, cs], yb[0:C, :], sg[0:C, cs], ALU.mult)
            if dlev <= 2:
                kb.dma('sp', dst, self.xt[0:C, :]); continue
            self.transpose_yg(yg, C)
            self.out_proj(C, dst, final)
            if gname == 'p' and ti == self.NTP - 1 and dlev > 8:
                for pair in range(32):
                    ps = self.psum()
                    kb.tr(ps[0:64, 0:128], H2[:, pair, :], self.identf[:, :])
                    so = kb.pool_rr('rwSout', Sout)
                    kb.copy('act' if pair % 2 else 'dve', so[:, :], ps[0:64, 0:128])
                    kb.dma('sp', io['p_wkv'][2 * pair:2 * pair + 2].rearrange("h v k -> v h k"), so[:, :].rearrange("v (h k) -> v h k", h=2))

    def layer_ssd(self, li, src_p, src_s, final):
        kb, io, les = self.kb, self.io, self.les
        NS = self.NS
        self.load_gain(li)
        sz = kb.sb('ssd_sz', [128, E], BF16, gran=512, es=les)
        xbc = kb.sb('ssd_xbc', [128, 6144], F32, gran=512, es=les)
        Bb = kb.sb('ssd_Bb', [128, 1024], BF16, es=les); Cb = kb.sb('ssd_Cb', [128, 1024], BF16, es=les)
        BT = kb.sb('ssd_BT', [128, 8, 128], BF16, es=les); CT = kb.sb('ssd_CT', [128, 8, 128], BF16, es=les)
        dtr = kb.sb('ssd_dtr', [128, 64], F32, es=les); dtv = kb.sb('ssd_dt', [128, 64], F32, es=les)
        la = kb.sb('ssd_la', [128, 64], F32, es=les); Lc = kb.sb('ssd_L', [128, 64], F32, es=les)
        Ll = kb.sb('ssd_Ll', [128, 64], F32, es=les); eL = kb.sb('ssd_eL', [128, 64], F32, es=les)
        wend = kb.sb('ssd_wend', [128, 64], F32, es=les); ellB = kb.sb('ssd_ellB', [128, 64], F32, es=les)
        dtb = kb.sb('ssd_dtb', [128, 64], F32, es=les); negA = kb.sb('ssd_negA', [128, 64], F32, es=les)
        Dt = kb.sb('ssd_Dtab', [128, 64], F32, es=les)
        kb.dma('sp', dtb[:], io['ssd_dt_bias'].partition_broadcast(128))
        kb.dma('sp', negA[:], io['ssd_A_log'].partition_broadcast(128))
        kb.dma('sp', Dt[:], io['ssd_D'].partition_broadcast(128))
        kb.act(negA[:], negA[:], AF.Exp)
        kb.op('act', lambda en: en.mul(out=negA[:], in_=negA[:], mul=-1.0), [negA[:]], [negA[:]])
        cw = [kb.sb(f'ssd_cw{i}', [128, 5, 512], F32, es=les) for i in range(2)]
        cblk = [kb.sb(f'ssd_cb{i}', [48, 512], F32, es=les) for i in range(2)]
        acc = kb.sb('ssd_acc', [128, 512], F32, es=les); tmpc = kb.sb('ssd_tmpc', [128, 512], F32, es=les)
        vv = kb.sb('ssd_v', [128, 64, 64], BF16, gran=512, es=les); vdt = [kb.sb(f'ssd_vd{i}', [128, 512], BF16, es=les) for i in range(2)]
        msc = kb.sb('ssd_msc', [128, 128], F32, es=les)
        drhs = [kb.sb(f'ssd_drhs{i}', [128, 128], F32, es=les) for i in range(2)]
        dseg = [kb.sb(f'ssd_dseg{i}', [128, 128], F32, es=les) for i in range(2)]
        AT = [kb.sb(f'ssd_AT{i}', [128, 128], BF16, es=les) for i in range(2)]
        ybuf = kb.sb('ssd_y', [128, E], F32, gran=512, es=les)
        yg = sz
        ss = kb.sb('ssd_ss', [128, 8], F32, es=les); rs = kb.sb('ssd_rs', [128, 8], F32, es=les)
        H = kb.sb('ssd_H', [128, 64, 64], F32, gran=512, es=les)
        Hbt = [kb.sb(f'ssd_Hb{i}', [128, 512], BF16, es=les) for i in range(2)]
        Hst = [kb.sb(f'ssd_Hs{i}', [128, 512], F32, es=les) for i in range(2)]
        cm = [kb.sb(f'ssd_cm{i}', [128, 128], BF16, es=les) for i in range(1)]
        bm = [kb.sb(f'ssd_bm{i}', [128, 128], BF16, es=les) for i in range(1)]
        ShT = {}; ScT = {}; selB = {}
        for gname in ('p', 's'):
            Cg = self.G[gname]['C']
            ShT[gname] = kb.sb(f'ssd_Sh{gname}', [128, 3, Cg], F32, es=les)
            kb.dma('sp', ShT[gname][:], io[f'c_{gname}_convSh'][:, :, :])
            ScT[gname] = kb.sb(f'ssd_Sc{gname}', [128, 3, Cg], F32, es=les)
            kb.dma('sp', ScT[gname][:], io[f'c_{gname}_convSc'][:, :, :])
        selB['s'] = kb.sb('ssd_selB', [128, NS, 128], F32, es=les)
        kb.dma('sp', selB['s'][:], io['c_s_selB'][:, :, :])
        carry = kb.dram('ssd_carry', [2, 3, 6144], F32)
        self.kb.reg['ssd_carry'] = ('dram', 3 * 6144)
        if final:
            self.fgain = kb.sb('fgain', [128, D], F32, es=les)
            kb.dma('sp', self.fgain[:], io['final_norm'].partition_broadcast(128))
        ntl = self.NTP + 1
        self.wplan(self.wblocks(io['ssd_w_in'], 10304, io['ssd_w_out']) * ntl)
        cwv = io['ssd_conv_w']; cbv = io['ssd_conv_b']
        for (gname, ti, src, dst, C) in self.tiles(src_p, src_s):
            G = self.G[gname]; nseq = G['nseq']
            first = (gname == 'p' and ti == 0)
            self.load_norm(src, C)
            self.make_xnT(C)
            def evac(bk, n, ps, C=C):
                e = 'act' if bk % 2 else 'dve'
                if bk < 8: kb.act(sz[0:C, bk * 512:(bk + 1) * 512], ps[0:C, :], AF.Silu)
                elif bk < 20: kb.copy(e, xbc[0:C, (bk - 8) * 512:(bk - 7) * 512], ps[0:C, :])
                else: kb.copy('dve', dtr[0:C, :], ps[0:C, 0:64])
            self.in_proj(C, 10304, None, evac=evac)
            if gname == 'p':
                kb.dma('sp', carry[(ti + 1) % 2], xbc[C - 3:C, :])
                if ti == self.NTP - 1:
                    kb.dma('sp', io['p_conv'][:, :], xbc[C - 3:C, :])
            else:
                for i in range(nseq):
                    kb.dma('sp', io['s_conv'][i], xbc[4 * i + 1:4 * i + 4, :])
            R = 3 if gname == 'p' else 3 * nseq
            for b in range(12):
                cs = slice(b * 512, (b + 1) * 512)
                w_ = kb.pool_rr('ssdcw', cw); cb_ = kb.pool_rr('ssdcb', cblk)
                kb.dma('sp', w_[:, 0:4, :], cwv[:, cs].partition_broadcast(128))
                kb.dma('sp', w_[:, 4, :], cbv[cs].partition_broadcast(128))
                has_c = not first
                if has_c:
                    if gname == 'p': kb.dma('sp', cb_[0:3, :], carry[ti % 2][:, cs])
                    else: kb.dma('sp', cb_[0:R, :], io['state_ssd_conv'][:, :, cs].rearrange("i r c -> (i r) c"))
                kb.tt('dve', acc[0:C, :], xbc[0:C, cs], w_[0:C, 3, :], ALU.mult)
                kb.tt('dve', acc[0:C, :], acc[0:C, :], w_[0:C, 4, :], ALU.add)
                for d in (1, 2, 3):
                    ps = self.psum()
                    prs = [(ShT[gname][0:C, d - 1, :], xbc[0:C, cs])]
                    if has_c: prs.append((ScT[gname][0:R, d - 1, :], cb_[0:R, :]))
                    kb.mm(ps[0:C, :], prs)
                    kb.tt('dve', tmpc[0:C, :], ps[0:C, :], w_[0:C, 3 - d, :], ALU.mult)
                    kb.tt('dve', acc[0:C, :], acc[0:C, :], tmpc[0:C, :], ALU.add)
                if b < 8: kb.act(xbc[0:C, cs], acc[0:C, :], AF.Silu)
                elif b < 10: kb.act(Bb[0:C, (b - 8) * 512:(b - 7) * 512], acc[0:C, :], AF.Silu)
                else: kb.act(Cb[0:C, (b - 10) * 512:(b - 9) * 512], acc[0:C, :], AF.Silu)
            xs3 = xbc[0:C, 0:4096].rearrange("p (h d) -> p h d", h=64)
            kb.tt('dve', dtv[0:C, :], dtr[0:C, :], dtb[0:C, :], ALU.add)
            kb.act(dtv[0:C, :], dtv[0:C, :], AF.Exp)
            kb.act(dtv[0:C, :], dtv[0:C, :], AF.Ln, bias=1.0)
            kb.tt('dve', la[0:C, :], dtv[0:C, :], negA[0:C, :], ALU.mult)
            ps1 = self.psum(); ps2 = self.psum()
            kb.mm(ps1[0:C, 0:64], [(G['mTi'][0:C, 0:C], la[0:C, :])])
            kb.mm(ps2[0:C, 0:64], [(G['ones'][0:C, 0:C], la[0:C, :])])
            kb.copy('act', Lc[0:C, :], ps1[0:C, 0:64]); kb.copy('dve', Ll[0:C, :], ps2[0:C, 0:64])
            kb.act(eL[0:C, :], Lc[0:C, :], AF.Exp)
            kb.tt('dve', wend[0:C, :], Ll[0:C, :], Lc[0:C, :], ALU.subtract)
            kb.act(wend[0:C, :], wend[0:C, :], AF.Exp)
            kb.tt('dve', vv[0:C], xs3, dtv[0:C, :].unsqueeze(2).to_broadcast([C, 64, 64]), ALU.mult)
            for (srcb, dstb) in ((Bb, BT), (Cb, CT)):
                ps = self.psum(); pv = ps[:].bitcast(BF16)
                for j in range(8):
                    kb.tr(pv[:, j * 128:j * 128 + C], srcb[0:C, j * 128:(j + 1) * 128], self.identb[0:C, 0:C])
                kb.copy('act', dstb[:, :, 0:C], pv.rearrange("p (j t) -> p j t", j=8)[:, :, 0:C])
            for g in range(8):
                ps = self.psum()
                kb.mm(ps[0:C, 0:C], [(BT[:, g, 0:C], CT[:, g, 0:C])])
                kb.tt('dve', msc[0:C, 0:C], ps[0:C, 0:C], G['mTi'][0:C, 0:C], ALU.mult)
                yps = self.psum_long()
                for hl in range(8):
                    h = g * 8 + hl
                    dr = kb.pool_rr('ssddr', drhs); ds_ = kb.pool_rr('ssdds', dseg); at = kb.pool_rr('ssdAT', AT)
                    kb.ts('dve', dr[0:C, 0:C], self.identf[0:C, 0:C], Lc[0:C, h:h + 1], ALU.mult)
                    psd = self.psum()
                    kb.mm(psd[0:C, 0:C], [(G['ones'][0:C, 0:C] if gname == 'p' else self.onesf[0:C, 0:C], dr[0:C, 0:C])])
                    kb.ts('dve', ds_[0:C, 0:C], psd[0:C, 0:C], Lc[0:C, h:h + 1], ALU.subtract, 0.0, ALU.min)
                    kb.act(ds_[0:C, 0:C], ds_[0:C, 0:C], AF.Exp)
                    kb.tt('dve', at[0:C, 0:C], ds_[0:C, 0:C], msc[0:C, 0:C], ALU.mult)
                    kb.mm(yps[0:C, hl * 64:(hl + 1) * 64], [(at[0:C, 0:C], vv[0:C, h, :])])
                kb.copy('act', ybuf[0:C, g * 512:(g + 1) * 512], yps[0:C, :])
                gsl = slice(g * 8, (g + 1) * 8)
                vdb = kb.pool_rr('ssdvd', vdt)
                kb.tt('dve', vdb[0:C, :].rearrange("p (h d) -> p h d", h=8), vv[0:C, gsl, :], wend[0:C, gsl].unsqueeze(2).to_broadcast([C, 8, 64]), ALU.mult)
                vdg = vdb[0:C, :]
                Hg = H[:, gsl, :].rearrange("p h d -> p (h d)")
                if gname == 'p':
                    if not first:
                        hb = kb.pool_rr('ssdHb', Hbt)
                        kb.copy('act', hb[:], Hg)
                        zps = self.psum()
                        kb.mm(zps[0:C, :], [(CT[:, g, 0:C], hb[:, :])])
                        kb.tt('dve', zps[0:C, :].rearrange("p (h d) -> p h d", h=8), zps[0:C, :].rearrange("p (h d) -> p h d", h=8),
                              eL[0:C, gsl].unsqueeze(2).to_broadcast([C, 8, 64]), ALU.mult) if False else None
                        ztmp = tmpc
                        kb.tt('dve', ztmp[0:C, :].rearrange("p (h d) -> p h d", h=8), zps[0:C, :].rearrange("p (h d) -> p h d", h=8),
                              eL[0:C, gsl].unsqueeze(2).to_broadcast([C, 8, 64]), ALU.mult)
                        kb.tt('dve', ybuf[0:C, g * 512:(g + 1) * 512], ybuf[0:C, g * 512:(g + 1) * 512], ztmp[0:C, :], ALU.add)
                    hps = self.psum()
                    kb.mm(hps[:, :], [(Bb[0:C, g * 128:(g + 1) * 128], vdg)])
                    if first:
                        kb.copy('act', Hg, hps[:, :])
                    else:
                        if g == 0:
                            psl = self.psum()
                            kb.mm(psl[:, 0:64], [(G['ones'][0:C, :], la[0:C, :])])
                            kb.act(ellB[:, :], psl[:, 0:64], AF.Exp)
                        kb.tt('dve', H[:, gsl, :], H[:, gsl, :], ellB[:, gsl].unsqueeze(2).to_broadcast([128, 8, 64]), ALU.mult)
                        kb.tt('dve', Hg, Hg, hps[:, :], ALU.add)
                else:
                    for i in range(nseq):
                        hs = kb.pool_rr('ssdHs', Hst); hb = kb.pool_rr('ssdHb', Hbt)
                        cmi = kb.pool_rr('ssdcm', cm); bmi = kb.pool_rr('ssdbm', bm)
                        kb.dma('sp', hs[:].rearrange("p (h d) -> p h d", h=8), io['state_ssd'][i, g * 8:(g + 1) * 8].rearrange("h n d -> n h d"))
                        kb.copy('act', hb[:], hs[:])
                        kb.memset('dve', cmi[:, 0:C], 0.0)
                        kb.copy('dve', cmi[:, 4 * i:4 * i + 4], CT[:, g, 4 * i:4 * i + 4])
                        zps = self.psum()
                        kb.mm(zps[0:C, :], [(cmi[:, 0:C], hb[:, :])])
                        ztmp = tmpc
                        kb.tt('dve', ztmp[0:C, :].rearrange("p (h d) -> p h d", h=8), zps[0:C, :].rearrange("p (h d) -> p h d", h=8),
                              eL[0:C, gsl].unsqueeze(2).to_broadcast([C, 8, 64]), ALU.mult)
                        kb.tt('dve', ybuf[0:C, g * 512:(g + 1) * 512], ybuf[0:C, g * 512:(g + 1) * 512], ztmp[0:C, :], ALU.add)
                        kb.ts('dve', bmi[0:C, :], Bb[0:C, g * 128:(g + 1) * 128], G['seqmask'][0:C, i:i + 1], ALU.mult)
                        hps = self.psum()
                        kb.mm(hps[:, :], [(bmi[0:C, :], vdg)])
                        psl = self.psum()
                        kb.mm(psl[:, 0:8], [(selB['s'][0:C, i, :], la[0:C, gsl])])
                        kb.act(ellB[:, 0:8], psl[:, 0:8], AF.Exp)
                        hs3 = hs[:].rearrange("p (h d) -> p h d", h=8)
                        kb.tt('dve', hs3, hs3, ellB[:, 0:8].unsqueeze(2).to_broadcast([128, 8, 64]), ALU.mult)
                        kb.tt('dve', hs[:], hs[:], hps[:, :], ALU.add)
                        kb.dma('sp', io['s_ssd'][i, g * 8:(g + 1) * 8].rearrange("h n d -> n h d"), hs[:].rearrange("p (h d) -> p h d", h=8))
            y3 = ybuf[0:C, :].rearrange("p (h d) -> p h d", h=64)
            for g in range(8):
                gsl = slice(g * 8, (g + 1) * 8)
                kb.tt('dve', tmpc[0:C, :].rearrange("p (h d) -> p h d", h=8), xs3[:, gsl, :], Dt[0:C, gsl].unsqueeze(2).to_broadcast([C, 8, 64]), ALU.mult)
                kb.tt('dve', ybuf[0:C, g * 512:(g + 1) * 512], ybuf[0:C, g * 512:(g + 1) * 512], tmpc[0:C, :], ALU.add)
            kb.tt('dve', ybuf[0:C, :], ybuf[0:C, :], sz[0:C, :], ALU.mult)
            for g in range(8):
                kb.act(self.xnb[0:C, 0:512], ybuf[0:C, g * 512:(g + 1) * 512], AF.Square, accum=ss[0:C, g:g + 1])
            kb.ts('dve', rs[0:C, :], ss[0:C, :], 1.0 / 512, ALU.mult, 1e-5, ALU.add)
            kb.act(rs[0:C, :], rs[0:C, :], AF.Sqrt)
            kb.recip(rs[0:C, :], rs[0:C, :])
            y8 = ybuf[0:C, :].rearrange("p (g v) -> p g v", g=8)
            kb.tt('dve', y8, y8, rs[0:C, :].unsqueeze(2).to_broadcast([C, 8, 512]), ALU.mult)
            for g in range(8):
                w_ = kb.pool_rr('ssdcw', cw)
                kb.dma('sp', w_[:, 0, :], io['ssd_norm_w'][g * 512:(g + 1) * 512].partition_broadcast(128))
                kb.tt('dve', yg[0:C, g * 512:(g + 1) * 512], ybuf[0:C, g * 512:(g + 1) * 512], w_[0:C, 0, :], ALU.mult)
            self.transpose_yg(yg, C)
            self.out_proj(C, dst, final)
            if gname == 'p' and ti == self.NTP - 1:
                kb.dma('sp', io['p_ssd'].rearrange("h n d -> n h d"), H[:])

    def layer_gla(self, li, src_p, src_s, final):
        kb, io, les = self.kb, self.io, self.les
        self.load_gain(li)
        QK = kb.sb('glaQK', [128, 2048], F32, gran=512, es=les)
        vb = kb.sb('gla_vb', [128, E], BF16, gran=512, es=les)
        sg = kb.sb('gla_sg', [128, E], BF16, gran=512, es=les)
        adf = kb.sb('gla_ad', [128, 16], F32, es=les)
        adT = kb.sb('gla_adT', [16, 128], F32, es=les)
        aup = kb.sb('gla_aup', [16, 1024], F32, es=les)
        abias = kb.sb('gla_abias', [128, 1024], F32, es=les)
        nw = kb.sb('gla_nw', [128, 1024], F32, es=les)
        kb.dma('sp', aup[:], io['gla_a_up'][:, :])
        kb.dma('sp', abias[:], io['gla_a_bias'].partition_broadcast(128))
        kb.dma('sp', nw[:], io['gla_norm_w'].partition_broadcast(128))
        la = kb.sb('gla_la', [128, 1024], F32, gran=512, es=les)
        Lc = kb.sb('gla_L', [128, 1024], F32, gran=512, es=les)
        Ll = kb.sb('gla_Ll', [128, 1024], F32, gran=512, es=les)
        tmp = kb.sb('gla_tmp', [128, 1024], F32, gran=512, es=les)
        qe = kb.sb('gla_qe', [128, 1024], BF16, es=les); ke = kb.sb('gla_ke', [128, 1024], BF16, es=les)
        kend = kb.sb('gla_kend', [128, 1024], BF16, es=les)
        qeT = kb.sb('gla_qeT', [128, 8, 128], BF16, es=les); keT = kb.sb('gla_keT', [128, 8, 128], BF16, es=les)
        gcol = kb.sb('gla_gcol', [128, 8, max(self.NS, 1)], F32, es=les)
        AT = [kb.sb(f'gla_AT{i}', [128, 128], BF16, es=les) for i in range(2)]
        ybuf = kb.sb('gla_y', [128, E], F32, gran=512, es=les)
        yg = vb
        ss = kb.sb('gla_ss', [128, 8], F32, es=les); rs = kb.sb('gla_rs', [128, 4], F32, es=les)
        H = kb.sb('gla_H', [128, 4, 2, 1024], F32, gran=1024, es=les)
        Hbt = [kb.sb(f'gla_Hb{i}', [128, 2, 1024], BF16, es=les) for i in range(2)]
        Hs = [H[:, i] for i in range(2)]; Hsb = Hbt
        qdm = [kb.sb(f'gla_qdm{i}', [128, 2, 128], BF16, es=les) for i in range(2)]
        kdm = [kb.sb(f'gla_kdm{i}', [128, 256], BF16, es=les) for i in range(2)]
        if final:
            self.fgain = kb.sb('fgain', [128, D], F32, es=les)
            kb.dma('sp', self.fgain[:], io['final_norm'].partition_broadcast(128))
        ntl = self.NTP + 1
        self.wplan(self.wblocks(io['gla_w_in'], 10256, io['gla_w_out']) * ntl)
        for (gname, ti, src, dst, C) in self.tiles(src_p, src_s):
            G = self.G[gname]; nseq = G['nseq']
            first = (gname == 'p' and ti == 0)
            self.load_norm(src, C)
            self.make_xnT(C)
            def evac(bk, n, ps, C=C):
                e = 'act' if bk % 2 else 'dve'
                if bk < 4: kb.copy(e, QK[0:C, bk * 512:(bk + 1) * 512], ps[0:C, :])
                elif bk < 12: kb.copy(e, vb[0:C, (bk - 4) * 512:(bk - 3) * 512], ps[0:C, :])
                elif bk < 20: kb.act(sg[0:C, (bk - 12) * 512:(bk - 11) * 512], ps[0:C, :], AF.Silu)
                else: kb.copy('dve', adf[0:C, :], ps[0:C, 0:16])
            self.in_proj(C, 10256, None, evac=evac)
            ps = self.psum()
            kb.tr(ps[0:16, 0:C], adf[0:C, 0:16], self.identf[0:C, 0:C])
            kb.copy('dve', adT[:, 0:C], ps[0:16, 0:C])
            for b in range(2):
                bs = slice(b * 512, (b + 1) * 512)
                ps = self.psum()
                kb.mm(ps[0:C, :], [(adT[:, 0:C], aup[:, bs])])
                kb.tt('dve', tmp[0:C, bs], ps[0:C, :], abias[0:C, bs], ALU.add)
                kb.act(tmp[0:C, bs], tmp[0:C, bs], AF.Exp, scale=-1.0)
                kb.act(tmp[0:C, bs], tmp[0:C, bs], AF.Ln, bias=1.0)
                kb.op('act', lambda en, bs=bs: en.mul(out=la[0:C, bs], in_=tmp[0:C, bs], mul=-1.0 / 16), [tmp[0:C, bs]], [la[0:C, bs]])
                ps1 = self.psum(); ps2 = self.psum()
                kb.mm(ps1[0:C, :], [(G['mTi'][0:C, 0:C], la[0:C, bs])])
                kb.mm(ps2[0:C, :], [(G['ones'][0:C, 0:C], la[0:C, bs])])
                kb.copy('act', Lc[0:C, bs], ps1[0:C, :])
                kb.copy('dve', Ll[0:C, bs], ps2[0:C, :])
                kb.act(tmp[0:C, bs], Lc[0:C, bs], AF.Exp)
                kb.stt('dve', qe[0:C, bs], QK[0:C, bs], 1.0 / 16, tmp[0:C, bs], ALU.mult, ALU.mult)
                kb.act(tmp[0:C, bs], Lc[0:C, bs], AF.Exp, scale=-1.0)
                kb.tt('dve', ke[0:C, bs], QK[0:C, 1024 + b * 512:1024 + (b + 1) * 512], tmp[0:C, bs], ALU.mult)
                kb.tt('dve', tmp[0:C, bs], Ll[0:C, bs], Lc[0:C, bs], ALU.subtract)
                kb.act(tmp[0:C, bs], tmp[0:C, bs], AF.Exp)
                kb.tt('dve', kend[0:C, bs], QK[0:C, 1024 + b * 512:1024 + (b + 1) * 512], tmp[0:C, bs], ALU.mult)
            ps = self.psum()
            for c8 in range(8):
                kb.mm(ps[:, c8 * 16:c8 * 16 + nseq], [(la[0:C, c8 * 128:(c8 + 1) * 128], G['seqmask'][0:C, 0:nseq])])
            kb.act(gcol[:, :, 0:nseq], ps[:, 0:128].rearrange("p (c s) -> p c s", c=8)[:, :, 0:nseq], AF.Exp)
            for (srcb, dstb) in ((qe, qeT), (ke, keT)):
                ps = self.psum(); pv = ps[:].bitcast(BF16)
                for j in range(8):
                    kb.tr(pv[:, j * 128:j * 128 + C], srcb[0:C, j * 128:(j + 1) * 128], self.identb[0:C, 0:C])
                kb.copy('act', dstb[:, :, 0:C], pv.rearrange("p (j t) -> p j t", j=8)[:, :, 0:C])
            for h in range(4):
                ps = self.psum()
                kb.mm(ps[0:C, 0:C], [(keT[:, h * 2 + kc, 0:C], qeT[:, h * 2 + kc, 0:C]) for kc in range(2)])
                at = kb.pool_rr('glaAT', AT)
                kb.tt('dve', at[0:C, 0:C], ps[0:C, 0:C], G['mTi'][0:C, 0:C], ALU.mult)
                if gname == 'p' and not first:
                    hbh = kb.pool_rr('glaHsb', Hbt)
                    kb.copy('act', hbh[:], H[:, h])
                for vblk in range(2):
                    vsl = vb[0:C, h * 1024 + vblk * 512:h * 1024 + (vblk + 1) * 512]
                    osl = slice(h * 1024 + vblk * 512, h * 1024 + (vblk + 1) * 512)
                    yps = self.psum_long()
                    if gname == 'p':
                        if first:
                            kb.mm(yps[0:C, :], [(at[0:C, 0:C], vsl)])
                        else:
                            kb.mm(yps[0:C, :], [(at[0:C, 0:C], vsl)] + [(qeT[:, h * 2 + kc, 0:C], hbh[:, kc, vblk * 512:(vblk + 1) * 512]) for kc in range(2)])
                        for kc in range(2):
                            hps = self.psum()
                            kb.mm(hps[:, :], [(kend[0:C, (h * 2 + kc) * 128:(h * 2 + kc + 1) * 128], vsl)])
                            hsl = H[:, h, kc, vblk * 512:(vblk + 1) * 512]
                            if first:
                                kb.copy('act', hsl, hps[:, :])
                            else:
                                kb.stt('dve', hsl, hsl, gcol[:, h * 2 + kc, 0:1], hps[:, :], ALU.mult, ALU.add)
                    else:
                        kb.mm(yps[0:C, :], [(at[0:C, 0:C], vsl)], stop=False)
                        for i in range(nseq):
                            hs = kb.pool_rr('glaHs', Hs); hsb = kb.pool_rr('glaHsb', Hsb)
                            qm = kb.pool_rr('glaqdm', qdm); km = kb.pool_rr('glakdm', kdm)
                            kb.dma('sp', hs[:, :, 0:512], io['state_gla'][i, h, :, vblk * 512:(vblk + 1) * 512].rearrange("(kc p) v -> p kc v", p=128))
                            kb.copy('act', hsb[:, :, 0:512], hs[:, :, 0:512])
                            kb.memset('dve', qm[:, :, 0:C], 0.0)
                            kb.copy('dve', qm[:, :, 4 * i:4 * i + 4], qeT[:, h * 2:h * 2 + 2, 4 * i:4 * i + 4])
                            kb.mm(yps[0:C, :], [(qm[:, kc, 0:C], hsb[:, kc, 0:512]) for kc in range(2)], start=False, stop=(i == nseq - 1))
                            kb.ts('dve', km[0:C, :], kend[0:C, h * 256:(h + 1) * 256], G['seqmask'][0:C, i:i + 1], ALU.mult)
                            for kc in range(2):
                                hps = self.psum()
                                kb.mm(hps[:, :], [(km[0:C, kc * 128:(kc + 1) * 128], vsl)])
                                kb.stt('dve', hs[:, kc, 0:512], hs[:, kc, 0:512], gcol[:, h * 2 + kc, i:i + 1], hps[:, :], ALU.mult, ALU.add)
                            kb.dma('sp', io['s_gla'][i, h, :, vblk * 512:(vblk + 1) * 512].rearrange("(kc p) v -> p kc v", p=128), hs[:, :, 0:512])
                    kb.copy('act', ybuf[0:C, osl], yps[0:C, :])
                    kb.act(self.xnb[0:C, 0:512], yps[0:C, :], AF.Square, accum=ss[0:C, h * 2 + vblk:h * 2 + vblk + 1])
            ss2 = ss[0:C, :].rearrange("p (h b) -> p h b", b=2)
            kb.tt('dve', rs[0:C, :], ss2[:, :, 0], ss2[:, :, 1], ALU.add)
            kb.ts('dve', rs[0:C, :], rs[0:C, :], 1.0 / 1024, ALU.mult, 1e-6, ALU.add)
            kb.act(rs[0:C, :], rs[0:C, :], AF.Sqrt)
            kb.recip(rs[0:C, :], rs[0:C, :])
            y3 = ybuf[0:C, :].rearrange("p (h v) -> p h v", h=4)
            kb.tt('dve', y3, y3, rs[0:C, :].unsqueeze(2).to_broadcast([C, 4, 1024]), ALU.mult)
            kb.tt('dve', y3, y3, nw[0:C, :].unsqueeze(1).to_broadcast([C, 4, 1024]), ALU.mult)
            kb.tt('dve', yg[0:C, :], ybuf[0:C, :], sg[0:C, :], ALU.mult)
            self.transpose_yg(yg, C)
            self.out_proj(C, dst, final)
            if gname == 'p' and ti == self.NTP - 1:
                kb.dma('sp', io['p_gla'].rearrange("h (kc p) v -> p h kc v", p=128), H[:])


_CACHE = {}

DBG = 99
def get_prog(TP, NS, layers=(0, 1, 2, 3), final=True):
    key = (TP, NS, tuple(layers), final, DBG)
    if key not in _CACHE:
        p = Prog(TP, NS, layers, final, DBG)
        p.nc_built = p.build()
        _CACHE[key] = p
    return _CACHE[key]


WEIGHT_NAMES = ['norm_gains', 'final_norm', 'rwkv_w_in', 'rwkv_mu', 'rwkv_w0', 'rwkv_w_up', 'rwkv_a0', 'rwkv_a_up',
                'rwkv_k_k', 'rwkv_k_a', 'rwkv_r_k', 'rwkv_ln_w', 'rwkv_ln_b', 'rwkv_w_out', 'ret_w_in', 'ret_w_out',
                'ssd_w_in', 'ssd_conv_w', 'ssd_conv_b', 'ssd_dt_bias', 'ssd_A_log', 'ssd_D', 'ssd_norm_w', 'ssd_w_out',
                'gla_w_in', 'gla_a_up', 'gla_a_bias', 'gla_norm_w', 'gla_w_out']
STATE_NAMES = ['state_rwkv_shift', 'state_rwkv_wkv', 'state_ret', 'state_ssd_conv', 'state_ssd', 'state_gla']


def run(inputs, n_cores=8, layers=(0, 1, 2, 3), final=True):
    xp = np.asarray(inputs['x_prompt'], np.float32); xs = np.asarray(inputs['x_sample'], np.float32)
    B, TP = xp.shape[0], xp.shape[1]
    DB = xs.shape[0]; NS = DB // n_cores
    prog = get_prog(TP, NS, layers, final)
    in_maps = []
    for c in range(n_cores):
        m = {}
        m['x_prompt'] = np.ascontiguousarray(xp[c % B])
        m['x_sample'] = np.ascontiguousarray(xs[c * NS:(c + 1) * NS].reshape(NS * 4, D))
        for nme in STATE_NAMES:
            if prog.need(nme): m[nme] = np.ascontiguousarray(np.asarray(inputs[nme], np.float32)[c * NS:(c + 1) * NS])
        for nme in WEIGHT_NAMES:
            if prog.need(nme): m[nme] = np.ascontiguousarray(np.asarray(inputs[nme], np.float32))
        for k, v in prog.consts.items(): m[k] = v
        in_maps.append(m)
    import os as _os
    if _os.environ.get('KTRACE'):
        res = run_bass_kernel_spmd(prog.nc_built, in_maps, core_ids=list(range(n_cores)), trace=True)
        print('KTRACE exec_time_ns', res.exec_time_ns)
    else:
        res = run_bass_kernel_spmd(prog.nc_built, in_maps, core_ids=list(range(n_cores)))
    R = res.results
    oshape = {'p_shift': (1, D), 'p_wkv': (64, 64, 64), 'p_ret': (8, 256, 512), 'p_conv': (3, 6144), 'p_ssd': (64, 128, 64),
              'p_gla': (4, 256, 1024), 's_shift': (NS, D), 's_wkv': (NS, 64, 64, 64), 's_ret': (NS, 8, 256, 512),
              's_conv': (NS, 3, 6144), 's_ssd': (NS, 64, 128, 64), 's_gla': (NS, 4, 256, 1024)}
    for r in R:
        for k, shp in oshape.items():
            if k not in r: r[k] = np.zeros(shp, np.float32)
    nb = min(B, n_cores)
    def cat(name, shape_tail=None):
        return np.concatenate([R[c][name] for c in range(n_cores)], axis=0)
    y_prompt = np.stack([R[c]['y_prompt'] for c in range(nb)], 0)
    y_sample = cat('y_sample').reshape(DB, 4, D)
    outs = [y_prompt, y_sample]
    outs.append(np.concatenate([R[c]['p_shift'] for c in range(nb)], 0))
    for nme in ('p_wkv', 'p_ret', 'p_conv', 'p_ssd', 'p_gla'):
        outs.append(np.stack([R[c][nme] for c in range(nb)], 0))
    for nme in ('s_shift', 's_wkv', 's_ret', 's_conv', 's_ssd', 's_gla'):
        outs.append(cat(nme))
    return tuple(outs)


def kernel(**inputs):
    return run(inputs, n_cores=8)
```

```python
import math, contextlib
import numpy as np
import concourse.bass as bass
import concourse.mybir as mybir
from concourse.bass_utils import run_bass_kernel_spmd

F32 = mybir.dt.float32; BF16 = mybir.dt.bfloat16
AF = mybir.ActivationFunctionType; ALU = mybir.AluOpType; AX = mybir.AxisListType

D = 2048; E = 4096
EPOCH = 3000


class KB:
    def __init__(self, nc, es, n_dma_sems=6):
        self.nc, self.es = nc, es
        self.eng = {'pe': nc.tensor, 'dve': nc.vector, 'act': nc.scalar, 'pool': nc.gpsimd, 'sp': nc.sync}
        self.sem = {}; self.cnt = {}; self.nsem = 0; self.allsems = []
        for e in self.eng: self._new_sem(e)
        self.waited = {}
        self.last_w = {}; self.readers = {}
        self.dsem = {}; self.n_dma_sems = n_dma_sems
        self.ninstr = 0
        self.reg = {}
        self.rr = {}

    def sb(self, name, shape, dt=F32, gran=None, es=None):
        t = (es or self.es).enter_context(self.nc.sbuf_tensor(name, list(shape), dt))
        self.reg[name] = ('sb', gran); return t
    def ps(self, name, shape, dt=F32, es=None):
        t = (es or self.es).enter_context(self.nc.psum_tensor(name, list(shape), dt))
        self.reg[name] = ('ps', None); return t
    def dram(self, name, shape, dt=F32, kind="Internal", gran=None):
        t = self.nc.dram_tensor(name, list(shape), dt, kind=kind).ap()
        self.reg[name] = ('dram', gran); return t

    def tokens(self, ap):
        name = ap.name
        kind, gran = self.reg.get(name, ('x', None))
        if gran is None: return [name]
        dims = ap.ap; off = ap.offset
        if kind == 'dram':
            lo = off; hi = off + sum((c - 1) * abs(s) for s, c in dims)
        else:
            pstep = dims[0][0]
            lo = off % pstep if pstep else off
            hi = lo + sum((c - 1) * abs(s) for s, c in dims[1:])
        return [(name, b) for b in range(lo // gran, hi // gran + 1)]

    def _new_sem(self, e):
        s = self.es.enter_context(self.nc.semaphore(f"s_{e}_{self.nsem}")); self.nsem += 1
        self.sem[e] = s; self.cnt[e] = 0; self.allsems.append(s)
    def _wait(self, e, deps):
        need = {}
        for ev in deps:
            if ev is None: continue
            s, v = ev
            if need.get(s.name, (None, 0))[1] < v: need[s.name] = (s, v)
        for name, (s, v) in need.items():
            if self.waited.get((e, name), 0) < v:
                self.eng[e].wait_ge(s, v); self.waited[(e, name)] = v; self.ninstr += 1
    def _deps(self, rt, wt):
        deps = []
        for t in rt:
            if t in self.last_w: deps.append(self.last_w[t])
        for t in wt:
            if t in self.last_w: deps.append(self.last_w[t])
            deps += self.readers.get(t, [])
        return deps
    def _record(self, ev, rt, wt):
        for t in rt: self.readers.setdefault(t, []).append(ev)
        for t in wt: self.last_w[t] = ev; self.readers[t] = []
    def _toks(self, aps):
        r = []
        for a in aps:
            if a is None or isinstance(a, (int, float)): continue
            r += self.tokens(a)
        return r
    def op(self, e, fn, ins, outs):
        rt = self._toks(ins); wt = self._toks(outs)
        self._wait(e, self._deps(rt, wt))
        ins_ = fn(self.eng[e]); self.ninstr += 1
        if self.cnt[e] >= EPOCH: self._new_sem(e)
        self.cnt[e] += 1
        ins_.then_inc(self.sem[e], 1)
        ev = (self.sem[e], self.cnt[e])
        self._record(ev, rt, wt)
        return ev
    def dma(self, q, out, in_, **kw):
        pool = self.dsem.setdefault(q, {'sems': [], 'vals': [], 'i': 0})
        if len(pool['sems']) < self.n_dma_sems:
            s = self.es.enter_context(self.nc.semaphore(f"d_{q}_{len(pool['sems'])}"))
            pool['sems'].append(s); pool['vals'].append(0)
        i = pool['i'] % self.n_dma_sems; pool['i'] += 1
        s = pool['sems'][i]
        rt = self._toks([in_]); wt = self._toks([out])
        deps = self._deps(rt, wt)
        if pool['vals'][i] > 0: deps.append((s, pool['vals'][i]))
        self._wait(q, deps)
        ins = self.eng[q].dma_start(out=out, in_=in_, **kw); self.ninstr += 1
        pool['vals'][i] += 16
        ins.then_inc(s, 16)
        ev = (s, pool['vals'][i])
        self._record(ev, rt, wt)
        return ev
    def all_events(self):
        evs = []
        for e in self.eng:
            if self.cnt[e] > 0: evs.append((self.sem[e], self.cnt[e]))
        for q, pool in self.dsem.items():
            for s, v in zip(pool['sems'], pool['vals']):
                if v > 0: evs.append((s, v))
        return evs
    def barrier(self, engines=('pe', 'dve', 'act', 'pool', 'sp')):
        evs = self.all_events()
        for e in engines: self._wait(e, evs)
        self.last_w = {}; self.readers = {}

    def tt(self, e, out, a, b, op):
        return self.op(e, lambda en: en.tensor_tensor(out=out, in0=a, in1=b, op=op), [a, b], [out])
    def ts(self, e, out, a, s1, op0, s2=None, op1=None, accum=None):
        kw = {}
        if op1 is not None: kw['op1'] = op1
        if accum is not None: kw['accum_out'] = accum
        return self.op(e, lambda en: en.tensor_scalar(out=out, in0=a, scalar1=s1, scalar2=s2, op0=op0, **kw),
                       [a, s1, s2], [out, accum])
    def stt(self, e, out, a, s, b, op0, op1):
        return self.op(e, lambda en: en.scalar_tensor_tensor(out=out, in0=a, scalar=s, in1=b, op0=op0, op1=op1),
                       [a, s, b], [out])
    def act(self, out, a, func, bias=None, scale=None, accum=None):
        kw = {}
        if bias is not None: kw['bias'] = bias
        if scale is not None: kw['scale'] = scale
        if accum is not None: kw['accum_out'] = accum
        return self.op('act', lambda en: en.activation(out=out, in_=a, func=func, **kw), [a, bias, scale], [out, accum])
    def copy(self, e, out, a):
        if e == 'act':
            return self.op(e, lambda en: en.copy(out=out, in_=a), [a], [out])
        return self.op(e, lambda en: en.tensor_copy(out=out, in_=a), [a], [out])
    def memset(self, e, out, v):
        return self.op(e, lambda en: en.memset(out, v), [], [out])
    def recip(self, out, a):
        return self.op('dve', lambda en: en.reciprocal(out=out, in_=a), [a], [out])
    def mm(self, out, pairs, start=True, stop=True):
        ins = []
        for l, r in pairs: ins += [l, r]
        n = len(pairs)
        def f(en):
            for i, (l, r) in enumerate(pairs):
                x = en.matmul(out, lhsT=l, rhs=r, start=(start and i == 0), stop=(stop and i == n - 1))
            return x
        return self.op('pe', f, ins + ([] if start else [out]), [out])
    def tr(self, out, a, ident):
        return self.op('pe', lambda en: en.transpose(out=out, in_=a, identity=ident), [a, ident], [out])
    def pool_rr(self, key, items):
        i = self.rr.get(key, 0); self.rr[key] = i + 1
        return items[i % len(items)]


def group_tables(C, nseq, slen, pos0s, ntiles):
    t = np.arange(C); seq = t // slen; pos = t % slen
    same = (seq[:, None] == seq[None, :])
    mT_incl = (same & (t[:, None] <= t[None, :])).astype(np.float32)
    mT_strict = (same & (t[:, None] < t[None, :])).astype(np.float32)
    tb = {}
    def pad(a, rows=128):
        o = np.zeros((rows,) + a.shape[1:], np.float32); o[:a.shape[0]] = a; return o
    tb['mTi'] = pad(mT_incl); tb['mTs'] = pad(mT_strict)
    tb['mi'] = pad(mT_incl.T.copy()); tb['ms'] = pad(mT_strict.T.copy())
    tb['ones'] = pad(same.astype(np.float32))
    sm = np.zeros((C, max(nseq, 1)), np.float32); sm[t, seq] = 1.0
    tb['seqmask'] = pad(sm)
    Sh = np.zeros((128, 3, C), np.float32); Sc = np.zeros((128, 3, C), np.float32)
    for d in (1, 2, 3):
        for tt_ in range(C):
            if pos[tt_] - d >= 0: Sh[tt_ - d, d - 1, tt_] = 1.0
            elif slen == 4: Sc[3 * seq[tt_] + (3 + pos[tt_] - d), d - 1, tt_] = 1.0
            else: Sc[3 + pos[tt_] - d, d - 1, tt_] = 1.0
    tb['convSh'] = Sh; tb['convSc'] = Sc
    selB = np.zeros((128, max(nseq, 1), 128), np.float32)
    selB[t, seq, :] = 1.0
    tb['selB'] = selB
    seqrow = np.zeros((128, max(nseq, 1), C), np.float32)
    seqrow[:, seq, t] = 1.0
    tb['seqrow'] = seqrow
    rwSh = np.zeros((128, C), np.float32); rwE0 = np.zeros((128, C), np.float32)
    for tt_ in range(C):
        if pos[tt_] >= 1: rwSh[tt_ - 1, tt_] = 1.0
        elif slen == 4: rwSh[C + seq[tt_], tt_] = 1.0
        else: rwE0[0, tt_] = 1.0
    tb['rwSh'] = rwSh; tb['rwE0'] = rwE0
    tb['mask2'] = np.concatenate([tb['mTs'][:, None, :], tb['mTi'][:, None, :]], axis=1).copy()
    H = 8
    lg = np.log1p(-(2.0 ** (-5.0 - np.arange(H, dtype=np.float32)))).astype(np.float32)
    diff = (t[None, :] - t[:, None]).astype(np.float32)
    DT = np.exp(diff[:, None, :] * lg[None, :, None]) * mT_incl[:, None, :]
    tb['retDT'] = pad(DT.astype(np.float32))
    gq = np.exp((pos[None, None, :] + 1.0) * lg[None, :, None])
    tb['retgq'] = np.broadcast_to(gq, (128, H, C)).astype(np.float32).copy()
    gk = np.exp((slen - 1.0 - pos[:, None]) * lg[None, :])
    tb['retgk'] = pad(gk.astype(np.float32))
    tb['retgC'] = np.exp(slen * lg).astype(np.float64)
    half = 128
    inv = (1.0 / (np.float32(10000.0) ** np.linspace(0.0, 1.0, half, dtype=np.float32))).astype(np.float32)
    cs = np.zeros((ntiles, 128, half), np.float32); sn = np.zeros((ntiles, 128, half), np.float32)
    for ti in range(ntiles):
        p = (pos0s[ti] + pos).astype(np.float32)
        ang = (p[:, None] * inv[None, :]).astype(np.float32)
        cs[ti, :C] = np.cos(ang); sn[ti, :C] = np.sin(ang)
    tb['cos'] = cs; tb['sin'] = sn
    return tb


class Prog:
    def __init__(self, TP, NS, layers=(0, 1, 2, 3), final=True, dbg=99):
        self.dbg = dbg
        self.TP, self.NS, self.layers, self.final = TP, NS, tuple(layers), final
        self.NTP = TP // 128
        self.CS = NS * 4
        self.consts = {}
        self.gp = group_tables(128, 1, 128, [128 * i for i in range(self.NTP)], self.NTP)
        self.gs = group_tables(self.CS, NS, 4, [16384], 1)

    def build(self):
        nc = bass.Bass("TRN2", target_bir_lowering=False)
        self.nc = nc
        es = contextlib.ExitStack()
        with es:
            kb = KB(nc, es); self.kb = kb
            self.declare_io()
            self.setup_common()
            src_p, src_s = self.io['x_prompt'], self.io['x_sample']
            for li in self.layers:
                last = (li == self.layers[-1])
                with contextlib.ExitStack() as les:
                    self.les = les
                    m = li % 4
                    fn = [self.layer_rwkv, self.layer_ret, self.layer_ssd, self.layer_gla][m]
                    fn(li, src_p, src_s, last and self.final)
                    kb.barrier()
                src_p, src_s = self.io['y_prompt'], self.io['y_sample']
            kb.barrier(engines=('sp',))
        return nc

    def declare_io(self):
        kb, nc = self.kb, self.nc
        TP, NS, CS = self.TP, self.NS, self.CS
        io = {}
        pref = {0: ('rwkv_', 'state_rwkv', 'p_shift', 'p_wkv', 's_shift', 's_wkv'), 1: ('ret_', 'state_ret', 'p_ret', 's_ret'),
                2: ('ssd_', 'state_ssd', 'p_conv', 'p_ssd', 's_conv', 's_ssd'), 3: ('gla_', 'state_gla', 'p_gla', 's_gla')}
        allp = sum(pref.values(), ())
        def need(name):
            if not name.startswith(allp): return True
            return any(name.startswith(pref[l % 4]) for l in self.layers)
        self.need = need
        def inp(name, shape, gran=None):
            if need(name): io[name] = kb.dram(name, shape, F32, kind="ExternalInput", gran=gran)
        def outp(name, shape, gran=None):
            if need(name): io[name] = kb.dram(name, shape, F32, kind="ExternalOutput", gran=gran)
        inp('x_prompt', [TP, D], gran=128 * D); inp('x_sample', [CS, D])
        inp('state_rwkv_shift', [NS, D]); inp('state_rwkv_wkv', [NS, 64, 64, 64])
        inp('state_ret', [NS, 8, 256, 512]); inp('state_ssd_conv', [NS, 3, 6144])
        inp('state_ssd', [NS, 64, 128, 64]); inp('state_gla', [NS, 4, 256, 1024])
        inp('norm_gains', [4, D]); inp('final_norm', [D])
        inp('rwkv_w_in', [D, 16576]); inp('rwkv_mu', [16576]); inp('rwkv_w0', [E]); inp('rwkv_w_up', [96, E])
        inp('rwkv_a0', [E]); inp('rwkv_a_up', [96, E]); inp('rwkv_k_k', [E]); inp('rwkv_k_a', [E])
        inp('rwkv_r_k', [64, 64]); inp('rwkv_ln_w', [E]); inp('rwkv_ln_b', [E]); inp('rwkv_w_out', [E, D])
        inp('ret_w_in', [D, 12288]); inp('ret_w_out', [E, D])
        inp('ssd_w_in', [D, 10304]); inp('ssd_conv_w', [4, 6144]); inp('ssd_conv_b', [6144]); inp('ssd_dt_bias', [64])
        inp('ssd_A_log', [64]); inp('ssd_D', [64]); inp('ssd_norm_w', [E]); inp('ssd_w_out', [E, D])
        inp('gla_w_in', [D, 10256]); inp('gla_a_up', [16, 1024]); inp('gla_a_bias', [1024]); inp('gla_norm_w', [1024])
        inp('gla_w_out', [E, D])
        outp('y_prompt', [TP, D], gran=128 * D); outp('y_sample', [CS, D])
        outp('p_shift', [1, D]); outp('p_wkv', [64, 64, 64]); outp('p_ret', [8, 256, 512]); outp('p_conv', [3, 6144])
        outp('p_ssd', [64, 128, 64]); outp('p_gla', [4, 256, 1024])
        outp('s_shift', [NS, D]); outp('s_wkv', [NS, 64, 64, 64]); outp('s_ret', [NS, 8, 256, 512])
        outp('s_conv', [NS, 3, 6144]); outp('s_ssd', [NS, 64, 128, 64]); outp('s_gla', [NS, 4, 256, 1024])
        for gname, g in (('p', self.gp), ('s', self.gs)):
            for k, v in g.items():
                if isinstance(v, np.ndarray) and v.dtype == np.float32:
                    nm = f"c_{gname}_{k}"
                    self.consts[nm] = v
                    io[nm] = kb.dram(nm, list(v.shape), F32, kind="ExternalInput")
        idn = np.eye(128, dtype=np.float32)
        self.consts['c_ident'] = idn
        io['c_ident'] = kb.dram('c_ident', [128, 128], F32, kind="ExternalInput")
        self.io = io

    def setup_common(self):
        kb = self.kb; io = self.io
        self.identf = kb.sb('identf', [128, 128], F32)
        self.identb = kb.sb('identb', [128, 128], BF16)
        kb.dma('sp', self.identf[:], io['c_ident'][:, :])
        kb.copy('dve', self.identb[:], self.identf[:])
        self.onesf = kb.sb('onesf', [128, 128], F32)
        kb.memset('dve', self.onesf[:], 1.0)
        self.psb = [kb.ps(f'psb{i}', [128, 512], F32) for i in range(8)]
        self.wbufs = [kb.sb(f'wbuf{i}', [128, 16, 512], BF16) for i in range(2)]
        self.wq = []
        self.wissued = 0; self.wused = 0
        self.G = {}
        for gname, g, C in (('p', self.gp, 128), ('s', self.gs, self.CS)):
            G = {'C': C, 'name': gname}
            for k in ('mTi', 'mTs', 'ones', 'seqmask', 'retgk'):
                v = g[k]
                t = kb.sb(f'g{gname}_{k}', list(v.shape), F32)
                kb.dma('sp', t[:], io[f'c_{gname}_{k}'][:, :])
                G[k] = t
            G['retgC'] = [float(x) for x in g['retgC']]
            self.G[gname] = G
        self.mlow = {}
        for gname in ('p', 's'):
            t = kb.sb(f'mlow_{gname}', [128, self.G[gname]['C']], F32)
            kb.dma('sp', t[:], io[f'c_{gname}_ms'][:, :]); self.mlow[gname] = t
        self.G['p']['nseq'] = 1; self.G['s']['nseq'] = self.NS
        self.G['p']['slen'] = 128; self.G['s']['slen'] = 4
        self.xt = kb.sb('xt', [128, D], F32, gran=512)
        self.xnb = kb.sb('xnb', [128, D], BF16)
        self.xnT = kb.sb('xnT', [128, 16, 128], BF16)
        self.stat = [kb.sb(f'stat{i}', [128, 8], F32) for i in range(4)]
        self.gain = kb.sb('gain', [128, D], F32)
        self.ygT = kb.sb('ygT', [128, 32, 128], BF16)

    def psum(self):
        return self.kb.pool_rr('psum', self.psb[0:6])
    def psum_long(self):
        return self.kb.pool_rr('psuml', self.psb[6:8])

    def wplan(self, blocks):
        self.wq = list(blocks); self.wissued = 0; self.wused = 0
    def _wissue(self):
        kb = self.kb
        ap, n = self.wq[self.wissued]
        buf = self.wbufs[self.wissued % 2]
        kb.dma('pool', buf[:, :, 0:n], ap.rearrange("(k p) c -> p k c", p=128))
        self.wissued += 1
    def wnext(self):
        while self.wissued < len(self.wq) and self.wissued < self.wused + 2:
            self._wissue()
        buf = self.wbufs[self.wused % 2]; self.wused += 1
        return buf

    def load_norm(self, src_ap, C, row0=0, xn32=None):
        kb = self.kb
        sl = slice(row0, row0 + C)
        kb.dma('sp', self.xt[sl, :], src_ap)
        st = kb.pool_rr('stat', self.stat)
        kb.act(self.xnb[sl, :], self.xt[sl, :], AF.Square, accum=st[sl, 0:1])
        kb.ts('dve', st[sl, 1:2], st[sl, 0:1], 1.0 / D, ALU.mult, 1e-6, ALU.add)
        kb.act(st[sl, 2:3], st[sl, 1:2], AF.Sqrt)
        kb.recip(st[sl, 3:4], st[sl, 2:3])
        kb.stt('dve', self.xnb[sl, :], self.xt[sl, :], st[sl, 3:4], self.gain[sl, :], ALU.mult, ALU.mult)
        if xn32 is not None:
            kb.stt('dve', xn32[sl, :], self.xt[sl, :], st[sl, 3:4], self.gain[sl, :], ALU.mult, ALU.mult)

    def make_xnT(self, R):
        kb = self.kb
        for half in range(2):
            ps = self.psum()
            pv = ps[:].bitcast(BF16)
            for j in range(8):
                k = half * 8 + j
                kb.tr(pv[:, j * 128:j * 128 + R], self.xnb[0:R, k * 128:(k + 1) * 128], self.identb[0:R, 0:R])
            src = pv.rearrange("p (j t) -> p j t", j=8)[:, :, 0:R]
            kb.copy('dve' if half == 0 else 'act', self.xnT[:, half * 8:(half + 1) * 8, 0:R], src)

    def in_proj(self, R, ncols, P, evac=None):
        kb = self.kb
        nblk = (ncols + 511) // 512
        for b in range(nblk):
            n = min(512, ncols - b * 512)
            w = self.wnext()
            ps = self.psum()
            kb.mm(ps[0:R, 0:n], [(self.xnT[:, k, 0:R], w[:, k, 0:n]) for k in range(16)])
            if evac is not None:
                evac(b, n, ps)
            else:
                kb.copy('act' if b % 2 == 0 else 'dve', P[0:R, b * 512:b * 512 + n], ps[0:R, 0:n])

    def transpose_yg(self, yg, R):
        kb = self.kb
        for q in range(4):
            ps = self.psum(); pv = ps[:].bitcast(BF16)
            for j in range(8):
                k = q * 8 + j
                kb.tr(pv[:, j * 128:j * 128 + R], yg[0:R, k * 128:(k + 1) * 128], self.identb[0:R, 0:R])
            src = pv.rearrange("p (j t) -> p j t", j=8)[:, :, 0:R]
            kb.copy('dve' if q % 2 == 0 else 'act', self.ygT[:, q * 8:(q + 1) * 8, 0:R], src)

    def out_proj(self, R, dst_ap, final):
        kb = self.kb
        pss = [self.psum() for _ in range(4)]
        for cb in range(4):
            for half in range(2):
                w = self.wnext()
                kb.mm(pss[cb][0:R, :], [(self.ygT[:, half * 16 + k, 0:R], w[:, k, :]) for k in range(16)],
                      start=(half == 0), stop=(half == 1))
            kb.tt('dve', self.xt[0:R, cb * 512:(cb + 1) * 512], pss[cb][0:R, :], self.xt[0:R, cb * 512:(cb + 1) * 512], ALU.add)
        if final:
            st = kb.pool_rr('stat', self.stat)
            kb.act(self.xnb[0:R, :], self.xt[0:R, :], AF.Square, accum=st[0:R, 0:1])
            kb.ts('dve', st[0:R, 1:2], st[0:R, 0:1], 1.0 / D, ALU.mult, 1e-6, ALU.add)
            kb.act(st[0:R, 2:3], st[0:R, 1:2], AF.Sqrt)
            kb.recip(st[0:R, 3:4], st[0:R, 2:3])
            kb.stt('dve', self.xt[0:R, :], self.xt[0:R, :], st[0:R, 3:4], self.fgain[0:R, :], ALU.mult, ALU.mult)
        kb.dma('sp', dst_ap, self.xt[0:R, :])

    def wblocks(self, w_in, ncols, w_out):
        bl = []
        for b in range((ncols + 511) // 512):
            n = min(512, ncols - b * 512)
            bl.append((w_in[:, b * 512:b * 512 + n], n))
        for cb in range(4):
            for half in range(2):
                bl.append((w_out[half * 2048:(half + 1) * 2048, cb * 512:(cb + 1) * 512], 512))
        return bl

    def tiles(self, src_p, src_s):
        io = self.io
        for ti in range(self.NTP):
            yield ('p', ti, src_p[ti * 128:(ti + 1) * 128, :], io['y_prompt'][ti * 128:(ti + 1) * 128, :], 128)
        yield ('s', 0, src_s[:, :], io['y_sample'][:, :], self.CS)

    def load_gain(self, li):
        self.kb.dma('sp', self.gain[:], self.io['norm_gains'][li].partition_broadcast(128))

    def layer_ret(self, li, src_p, src_s, final):
        kb, io, les = self.kb, self.io, self.les
        NS = self.NS
        self.load_gain(li)
        P = kb.sb('retP', [128, 4096], F32, gran=512, es=les)
        cosT = kb.sb('ret_cos', [128, 128], F32, es=les); sinT = kb.sb('ret_sin', [128, 128], F32, es=les)
        cosk = kb.sb('ret_cosk', [128, 128], F32, es=les); sink = kb.sb('ret_sink', [128, 128], F32, es=les)
        t1 = kb.sb('ret_t1', [128, 8, 128], F32, es=les); t2 = kb.sb('ret_t2', [128, 8, 128], F32, es=les)
        qr = kb.sb('ret_qr', [128, 8, 256], BF16, es=les); kr = kb.sb('ret_kr', [128, 8, 256], BF16, es=les)
        kd = kb.sb('ret_kd', [128, 8, 256], BF16, es=les)
        vb = kb.sb('ret_vb', [128, E], BF16, gran=512, es=les)
        sg = kb.sb('ret_sg', [128, E], BF16, gran=512, es=les)
        qT = kb.sb('ret_qT', [128, 8, 2, 128], BF16, es=les); qdT = kb.sb('ret_qdT', [128, 8, 2, 128], BF16, es=les)
        kT = kb.sb('ret_kT', [128, 8, 2, 128], BF16, es=les)
        AT = [kb.sb(f'ret_AT{i}', [128, 128], BF16, es=les) for i in range(2)]
        ybuf = P
        yg = vb
        ss = kb.sb('ret_ss', [128, 8], F32, es=les); rs = kb.sb('ret_rs', [128, 8], F32, es=les)
        H = kb.sb('ret_H', [128, 8, 2, 512], F32, gran=1024, es=les)
        Hb = kb.sb('ret_Hb', [128, 8, 2, 512], BF16, gran=1024, es=les)
        Hs = [H[:, i] for i in range(2)]
        Hsb = [Hb[:, i] for i in range(2)]
        qdm = [kb.sb(f'ret_qdm{i}', [128, 2, 128], BF16, es=les) for i in range(2)]
        kdm = [kb.sb(f'ret_kdm{i}', [128, 256], BF16, es=les) for i in range(2)]
        if final:
            self.fgain = kb.sb('fgain', [128, D], F32, es=les)
            kb.dma('sp', self.fgain[:], io['final_norm'].partition_broadcast(128))
        DTt = {}; gqt = {}
        for gname in ('p', 's'):
            DTt[gname] = kb.sb(f'ret_DT{gname}', [128, 8, self.G[gname]['C']], F32, es=les)
            kb.dma('sp', DTt[gname][:], io[f'c_{gname}_retDT'][:, :, :])
            gqt[gname] = kb.sb(f'ret_gq{gname}', [128, 8, self.G[gname]['C']], F32, es=les)
            kb.dma('sp', gqt[gname][:], io[f'c_{gname}_retgq'][:, :, :])
        ntl = self.NTP + 1
        self.wplan(self.wblocks(io['ret_w_in'], 12288, io['ret_w_out']) * ntl)
        for (gname, ti, src, dst, C) in self.tiles(src_p, src_s):
            G = self.G[gname]; nseq = G['nseq']
            first = (gname == 'p' and ti == 0)
            self.load_norm(src, C)
            self.make_xnT(C)
            def evac(bk, n, ps, C=C):
                if bk < 8: kb.copy('act' if bk % 2 else 'dve', P[0:C, bk * 512:(bk + 1) * 512], ps[0:C, :])
                elif bk < 16: kb.copy('act' if bk % 2 else 'dve', vb[0:C, (bk - 8) * 512:(bk - 7) * 512], ps[0:C, :])
                else: kb.act(sg[0:C, (bk - 16) * 512:(bk - 15) * 512], ps[0:C, :], AF.Silu)
            self.in_proj(C, 12288, P, evac=evac)
            if self.dbg <= 1:
                kb.dma('sp', dst, self.xt[0:C, :]); continue
            kb.dma('sp', cosT[:], io[f'c_{gname}_cos'][ti]); kb.dma('sp', sinT[:], io[f'c_{gname}_sin'][ti])
            kb.op('act', lambda en: en.mul(out=cosk[:], in_=cosT[:], mul=1.0 / 16), [cosT[:]], [cosk[:]])
            kb.op('act', lambda en: en.mul(out=sink[:], in_=sinT[:], mul=1.0 / 16), [sinT[:]], [sink[:]])
            for (base, outb, cs_, sn_) in ((0, qr, cosT, sinT), (2048, kr, cosk, sink)):
                x3 = P[0:C, base:base + 2048].rearrange("p (h d) -> p h d", h=8)
                x1 = x3[:, :, 0:128]; x2 = x3[:, :, 128:256]
                cb_ = cs_[0:C, :].unsqueeze(1).to_broadcast([C, 8, 128]); sb_ = sn_[0:C, :].unsqueeze(1).to_broadcast([C, 8, 128])
                kb.tt('dve', t1[0:C], x1, cb_, ALU.mult)
                kb.tt('dve', t2[0:C], x2, sb_, ALU.mult)
                kb.tt('dve', outb[0:C, :, 0:128], t1[0:C], t2[0:C], ALU.subtract)
                kb.tt('dve', t1[0:C], x1, sb_, ALU.mult)
                kb.tt('dve', t2[0:C], x2, cb_, ALU.mult)
                kb.tt('dve', outb[0:C, :, 128:256], t1[0:C], t2[0:C], ALU.add)
            kb.tt('dve', kd[0:C], kr[0:C], G['retgk'][0:C, :].unsqueeze(2).to_broadcast([C, 8, 256]), ALU.mult)
            for (srcb, dsts) in ((qr, (qT, qdT)), (kr, (kT,))):
                for half in range(2):
                    ps = self.psum(); pv = ps[:].bitcast(BF16)
                    for j in range(8):
                        h = half * 4 + j // 2; kc = j % 2
                        kb.tr(pv[:, j * 128:j * 128 + C], srcb[0:C, h, kc * 128:(kc + 1) * 128], self.identb[0:C, 0:C])
                    srcv = pv.rearrange("p (h k t) -> p h k t", h=4, k=2)[:, :, :, 0:C]
                    kb.copy('act', dsts[0][:, half * 4:(half + 1) * 4, :, 0:C], srcv)
                    if len(dsts) > 1:
                        gv = gqt[gname][:, half * 4:(half + 1) * 4, :].unsqueeze(2).to_broadcast([128, 4, 2, C])
                        kb.tt('dve', dsts[1][:, half * 4:(half + 1) * 4, :, 0:C], srcv, gv, ALU.mult)
            if self.dbg <= 2:
                kb.dma('sp', dst, self.xt[0:C, :]); continue
            for h in range(8):
                ps = self.psum()
                kb.mm(ps[0:C, 0:C], [(kT[:, h, kc, 0:C], qT[:, h, kc, 0:C]) for kc in range(2)])
                at = kb.pool_rr('retAT', AT)
                kb.tt('dve', at[0:C, 0:C], ps[0:C, 0:C], DTt[gname][0:C, h, :], ALU.mult)
                yps = self.psum_long()
                vsl = vb[0:C, h * 512:(h + 1) * 512]
                if gname == 'p':
                    if first:
                        kb.mm(yps[0:C, :], [(at[0:C, 0:C], vsl)])
                    else:
                        kb.mm(yps[0:C, :], [(at[0:C, 0:C], vsl)] + [(qdT[:, h, kc, 0:C], Hb[:, h, kc, :]) for kc in range(2)])
                    for kc in range(2):
                        hps = self.psum()
                        kb.mm(hps[:, :], [(kd[0:C, h, kc * 128:(kc + 1) * 128], vsl)])
                        if first:
                            kb.copy('act', H[:, h, kc, :], hps[:, :])
                        else:
                            kb.stt('dve', H[:, h, kc, :], H[:, h, kc, :], G['retgC'][h], hps[:, :], ALU.mult, ALU.add)
                        kb.copy('act', Hb[:, h, kc, :], H[:, h, kc, :])
                else:
                    kb.mm(yps[0:C, :], [(at[0:C, 0:C], vsl)], stop=False)
                    for i in range(nseq):
                        hs = kb.pool_rr('retHs', Hs); hsb = kb.pool_rr('retHsb', Hsb)
                        qm = kb.pool_rr('retqdm', qdm); km = kb.pool_rr('retkdm', kdm)
                        kb.dma('sp', hs[:], io['state_ret'][i, h].rearrange("(kc p) v -> p kc v", p=128))
                        kb.copy('act', hsb[:], hs[:])
                        kb.memset('dve', qm[:, :, 0:C], 0.0)
                        kb.copy('dve', qm[:, :, 4 * i:4 * i + 4], qdT[:, h, :, 4 * i:4 * i + 4])
                        kb.mm(yps[0:C, :], [(qm[:, kc, 0:C], hsb[:, kc, :]) for kc in range(2)], start=False, stop=(i == nseq - 1))
                        kb.ts('dve', km[0:C, :], kd[0:C, h, :], G['seqmask'][0:C, i:i + 1], ALU.mult)
                        for kc in range(2):
                            hps = self.psum()
                            kb.mm(hps[:, :], [(km[0:C, kc * 128:(kc + 1) * 128], vsl)])
                            kb.stt('dve', hs[:, kc, :], hs[:, kc, :], G['retgC'][h], hps[:, :], ALU.mult, ALU.add)
                        kb.dma('sp', io['s_ret'][i, h].rearrange("(kc p) v -> p kc v", p=128), hs[:])
                kb.copy('act', ybuf[0:C, h * 512:(h + 1) * 512], yps[0:C, :])
                kb.act(self.xnb[0:C, 0:512], yps[0:C, :], AF.Square, accum=ss[0:C, h:h + 1])
            if self.dbg <= 3:
                kb.dma('sp', dst, self.xt[0:C, :]); continue
            kb.ts('dve', rs[0:C, :], ss[0:C, :], 1.0 / 512, ALU.mult, 1e-6, ALU.add)
            kb.act(rs[0:C, :], rs[0:C, :], AF.Sqrt)
            kb.recip(rs[0:C, :], rs[0:C, :])
            y3 = ybuf[0:C, :].rearrange("p (h v) -> p h v", h=8)
            kb.tt('dve', y3, y3, rs[0:C, :].unsqueeze(2).to_broadcast([C, 8, 512]), ALU.mult)
            kb.tt('dve', yg[0:C, :], ybuf[0:C, :], sg[0:C, :], ALU.mult)
            self.transpose_yg(yg, C)
            self.out_proj(C, dst, final)
            if gname == 'p' and ti == self.NTP - 1:
                kb.dma('sp', io['p_ret'].rearrange("h (kc p) v -> p h kc v", p=128), H[:])


    def layer_rwkv(self, li, src_p, src_s, final):
        kb, io, les = self.kb, self.io, self.les
        NS = self.NS
        self.load_gain(li)
        sb = lambda n, shp, dt=F32, gran=None: kb.sb(n, shp, dt, gran=gran, es=les)
        Rb = sb('rw_R', [128, E], F32, 512); Kb = sb('rw_K', [128, E], F32, 512)
        vb = sb('rw_vb', [128, E], BF16, 512); sg = sb('rw_sg', [128, E], BF16, 512); yg = sg
        wa = sb('rw_wa', [128, 192], F32); twT = sb('rw_twT', [96, 128], F32); adT = sb('rw_adT', [96, 128], F32)
        pblk = [sb(f'rw_pblk{i}', [128, 512]) for i in range(1)]
        mub = [sb(f'rw_mu{i}', [128, 512]) for i in range(1)]
        cbl = [sb(f'rw_cb{i}', [1, 512]) for i in range(1)]
        tbl = sb('rw_tbl', [128, 7, 512]); lora = sb('rw_lora', [96, 2, 512])
        lw = sb('rw_lw', [128, 512]); Lc = sb('rw_Lc', [128, 512]); Ll = sb('rw_Ll', [128, 512])
        e1 = sb('rw_e1', [128, 512]); kk = sb('rw_kk', [128, 512]); av = sb('rw_a', [128, 512]); kp = sb('rw_kp', [128, 512])
        t1 = sb('rw_t1', [128, 512]); t2 = sb('rw_t2', [128, 512]); ltmp = t1
        st8 = sb('rw_st8', [128, 8]); st8b = sb('rw_st8b', [128, 8]); bsum = sb('rw_bsum', [128, 8])
        Atb = sb('rw_At', [128, 512], BF16); Btb = sb('rw_Bt', [128, 512], BF16); Ktb = sb('rw_Kt', [128, 512], BF16)
        Rtb = sb('rw_Rt', [128, 512], BF16); Bend = sb('rw_Bend', [128, 512], BF16); Kend = sb('rw_Kend', [128, 512], BF16)
        BtT = sb('rw_BtT', [128, 4, 128], BF16); KtT = sb('rw_KtT', [128, 4, 128], BF16); ARt = sb('rw_ARt', [128, 4, 2, 128], BF16)
        NP4 = sb('rw_NP4', [128, 4, 2, 128], BF16); MQ4 = sb('rw_MQ4', [128, 4, 2, 128], BF16)
        X4 = [sb(f'rw_X4{i}', [128, 4, 128], BF16) for i in range(2)]; XT4 = [sb(f'rw_XT4{i}', [128, 4, 128], BF16) for i in range(2)]
        NP = [NP4[:, 0]]; MQ = [MQ4[:, 0]]
        Xn = [X4[i][:, 0, :] for i in range(2)]; XT = [XT4[i][:, 0, :] for i in range(2)]
        U = Lc; Ub = sb('rw_Ub', [128, 512], BF16); ARz = sb('rw_ARz', [128, 2, 4, 2, 128], BF16)
        yb = e1
        wlcol = sb('rw_wl', [128, 4, max(NS, 1)])
        H2 = sb('rw_H2', [128, 32, 64], F32, 64); Hb2 = sb('rw_Hb2', [128, 32, 64], BF16, 64)
        HsAll = H2[:, 0:max(NS, 1), :]; HsbAll = Hb2[:, 0:max(NS, 1), :]
        amAll = sb('rw_amAll', [128, max(NS, 1), 2, 64], BF16)
        Sin = [sb(f'rw_Sin{i}', [64, 128]) for i in range(1)]; Sout = [sb(f'rw_Sout{i}', [64, 128]) for i in range(1)]
        bkm = [sb(f'rw_bkm{i}', [128, 2, 128], BF16) for i in range(1)]
        xn32 = self.ygT[:].rearrange("p a b -> p (a b)").bitcast(F32)
        Sh = {}; E0 = {}; M2 = {}
        for gname in ('p', 's'):
            Cg = self.G[gname]['C']
            Sh[gname] = sb(f'rw_Sh{gname}', [128, Cg]); kb.dma('sp', Sh[gname][:], io[f'c_{gname}_rwSh'][:, :])
            E0[gname] = sb(f'rw_E0{gname}', [128, Cg]); kb.dma('sp', E0[gname][:], io[f'c_{gname}_rwE0'][:, :])
            M2[gname] = sb(f'rw_M2{gname}', [128, 2, Cg]); kb.dma('sp', M2[gname][:], io[f'c_{gname}_mask2'][:, :, :])
        seqrow = sb('rw_seqrow', [128, max(NS, 1), self.CS], BF16)
        kb.dma('pool', seqrow[:], io['c_s_seqrow'][:, :, :])
        carry = kb.dram('rw_carry', [2, 16576], F32)
        kb.reg['rw_carry'] = ('dram', 16576)
        if final:
            self.fgain = sb('fgain', [128, D]); kb.dma('sp', self.fgain[:], io['final_norm'].partition_broadcast(128))
        ntl = self.NTP + 1
        self.wplan(self.wblocks(io['rwkv_w_in'], 16576, io['rwkv_w_out']) * ntl)
        tnames = ['rwkv_w0', 'rwkv_a0', 'rwkv_k_k', 'rwkv_k_a', 'rwkv_ln_w', 'rwkv_ln_b']
        rkflat = io['rwkv_r_k'].rearrange("h k -> (h k)")
        for (gname, ti, src, dst, C) in self.tiles(src_p, src_s):
            G = self.G[gname]; nseq = G['nseq']
            first = (gname == 'p' and ti == 0)
            nlev = 7 if gname == 'p' else 2
            self.load_norm(src, C, xn32=xn32)
            R = C
            if gname == 'p':
                if ti == self.NTP - 1: kb.dma('sp', io['p_shift'][:, :], xn32[C - 1:C, :])
            else:
                for i in range(nseq): kb.dma('sp', io['s_shift'][i:i + 1, :], xn32[4 * i + 3:4 * i + 4, :])
                R = C + nseq
                kb.dma('sp', self.xt[C:R, :], io['state_rwkv_shift'][:, :])
                kb.copy('act', self.xnb[C:R, :], self.xt[C:R, :])
            self.make_xnT(R)
            def evac(bk, n, ps, C=C, R=R, gname=gname, ti=ti, first=first):
                pb_ = kb.pool_rr('rwpblk', pblk); mu_ = kb.pool_rr('rwmu', mub)
                kb.copy('act', pb_[0:R, 0:n], ps[0:R, 0:n])
                kb.dma('sp', mu_[:, 0:n], io['rwkv_mu'][bk * 512:bk * 512 + n].partition_broadcast(128))
                prs = [(Sh[gname][0:R, :], pb_[0:R, 0:n])]
                if gname == 'p':
                    kb.dma('sp', carry[(ti + 1) % 2, bk * 512:bk * 512 + n], pb_[C - 1:C, 0:n])
                    if not first:
                        cb_ = kb.pool_rr('rwcb', cbl)
                        kb.dma('sp', cb_[0:1, 0:n], carry[ti % 2, bk * 512:bk * 512 + n])
                        prs.append((E0[gname][0:1, :], cb_[0:1, 0:n]))
                ps2 = self.psum()
                kb.mm(ps2[0:C, 0:n], prs)
                kb.tt('dve', ltmp[0:C, 0:n], ps2[0:C, 0:n], pb_[0:C, 0:n], ALU.subtract)
                kb.tt('dve', ltmp[0:C, 0:n], ltmp[0:C, 0:n], mu_[0:C, 0:n], ALU.mult)
                if bk < 8: kb.tt('dve', Rb[0:C, bk * 512:(bk + 1) * 512], pb_[0:C, :], ltmp[0:C, :], ALU.add)
                elif bk < 16: kb.tt('dve', Kb[0:C, (bk - 8) * 512:(bk - 7) * 512], pb_[0:C, :], ltmp[0:C, :], ALU.add)
                elif bk < 24: kb.tt('dve', vb[0:C, (bk - 16) * 512:(bk - 15) * 512], pb_[0:C, :], ltmp[0:C, :], ALU.add)
                elif bk < 32:
                    kb.tt('dve', ltmp[0:C, :], pb_[0:C, :], ltmp[0:C, :], ALU.add)
                    kb.act(sg[0:C, (bk - 24) * 512:(bk - 23) * 512], ltmp[0:C, :], AF.Silu)
                else: kb.tt('dve', wa[0:C, :], pb_[0:C, 0:192], ltmp[0:C, 0:192], ALU.add)
            self.in_proj(R, 16576, None, evac=evac)
            kb.act(wa[0:C, 0:96], wa[0:C, 0:96], AF.Tanh)
            for (o, dstT) in ((0, twT), (96, adT)):
                ps = self.psum()
                kb.tr(ps[0:96, 0:C], wa[0:C, o:o + 96], self.identf[0:C, 0:C])
                kb.copy('dve', dstT[:, 0:C], ps[0:96, 0:C])
            for hb in range(8):
                cs = slice(hb * 512, (hb + 1) * 512)
                for j, nm in enumerate(tnames):
                    kb.dma('sp', tbl[:, j, :], io[nm][cs].partition_broadcast(128))
                kb.dma('sp', tbl[:, 6, :], rkflat[cs].partition_broadcast(128))
                kb.dma('sp', lora[:, 0, :], io['rwkv_w_up'][:, cs]); kb.dma('sp', lora[:, 1, :], io['rwkv_a_up'][:, cs])
                h3 = lambda ap: ap.rearrange("p (h d) -> p h d", h=8)
                ps = self.psum()
                kb.mm(ps[0:C, :], [(twT[:, 0:C], lora[:, 0, :])])
                kb.tt('dve', lw[0:C, :], ps[0:C, :], tbl[0:C, 0, :], ALU.add)
                kb.act(lw[0:C, :], lw[0:C, :], AF.Exp, scale=-1.0)
                kb.act(lw[0:C, :], lw[0:C, :], AF.Ln, bias=1.0)
                kb.act(lw[0:C, :], lw[0:C, :], AF.Exp, scale=-1.0, bias=-0.5)
                kb.op('act', lambda en: en.mul(out=lw[0:C, :], in_=lw[0:C, :], mul=-1.0), [lw[0:C, :]], [lw[0:C, :]])
                ps = self.psum()
                kb.mm(ps[0:C, :], [(adT[:, 0:C], lora[:, 1, :])])
                kb.tt('dve', av[0:C, :], ps[0:C, :], tbl[0:C, 1, :], ALU.add)
                kb.act(av[0:C, :], av[0:C, :], AF.Sigmoid)
                kb.tt('dve', kk[0:C, :], Kb[0:C, cs], tbl[0:C, 2, :], ALU.mult)
                kb.tt('dve', t1[0:C, :], kk[0:C, :], kk[0:C, :], ALU.mult)
                kb.op('dve', lambda en: en.tensor_reduce(out=st8[0:C, :], in_=h3(t1[0:C, :]), axis=AX.X, op=ALU.add), [t1[0:C, :]], [st8[0:C, :]])
                kb.act(st8[0:C, :], st8[0:C, :], AF.Sqrt)
                kb.ts('dve', st8[0:C, :], st8[0:C, :], 1e-12, ALU.max)
                kb.recip(st8[0:C, :], st8[0:C, :])
                kb.tt('dve', h3(kk[0:C, :]), h3(kk[0:C, :]), st8[0:C, :].unsqueeze(2).to_broadcast([C, 8, 64]), ALU.mult)
                kb.ts('dve', t1[0:C, :], av[0:C, :], -1.0, ALU.add)
                kb.tt('dve', t1[0:C, :], t1[0:C, :], tbl[0:C, 3, :], ALU.mult)
                kb.ts('dve', t1[0:C, :], t1[0:C, :], 1.0, ALU.add)
                kb.tt('dve', kp[0:C, :], Kb[0:C, cs], t1[0:C, :], ALU.mult)
                kb.tt('dve', t1[0:C, :], Rb[0:C, cs], kp[0:C, :], ALU.mult)
                kb.tt('dve', t1[0:C, :], t1[0:C, :], tbl[0:C, 6, :], ALU.mult)
                kb.op('dve', lambda en: en.tensor_reduce(out=bsum[0:C, :], in_=h3(t1[0:C, :]), axis=AX.X, op=ALU.add), [t1[0:C, :]], [bsum[0:C, :]])
                ps1 = self.psum(); ps2 = self.psum()
                kb.mm(ps1[0:C, :], [(G['mTi'][0:C, 0:C], lw[0:C, :])])
                kb.mm(ps2[0:C, :], [(G['ones'][0:C, 0:C], lw[0:C, :])])
                kb.copy('act', Lc[0:C, :], ps1[0:C, :]); kb.copy('dve', Ll[0:C, :], ps2[0:C, :])
                kb.tt('dve', t1[0:C, :], Lc[0:C, :], lw[0:C, :], ALU.subtract)
                kb.act(e1[0:C, :], t1[0:C, :], AF.Exp)
                kb.stt('dve', Atb[0:C, :], kk[0:C, :], -1.0, e1[0:C, :], ALU.mult, ALU.mult)
                kb.act(e1[0:C, :], Lc[0:C, :], AF.Exp)
                kb.tt('dve', Rtb[0:C, :], Rb[0:C, cs], e1[0:C, :], ALU.mult)
                kb.tt('dve', t2[0:C, :], kk[0:C, :], av[0:C, :], ALU.mult)
                kb.act(e1[0:C, :], Lc[0:C, :], AF.Exp, scale=-1.0)
                kb.tt('dve', Btb[0:C, :], t2[0:C, :], e1[0:C, :], ALU.mult)
                kb.tt('dve', Ktb[0:C, :], kp[0:C, :], e1[0:C, :], ALU.mult)
                kb.tt('dve', t1[0:C, :], Ll[0:C, :], Lc[0:C, :], ALU.subtract)
                kb.act(e1[0:C, :], t1[0:C, :], AF.Exp)
                kb.tt('dve', Bend[0:C, :], t2[0:C, :], e1[0:C, :], ALU.mult)
                kb.tt('dve', Kend[0:C, :], kp[0:C, :], e1[0:C, :], ALU.mult)
                ps = self.psum()
                for j in range(4):
                    kb.mm(ps[:, j * 16:j * 16 + nseq], [(lw[0:C, j * 128:(j + 1) * 128], G['seqmask'][0:C, 0:nseq])])
                kb.act(wlcol[:, :, 0:nseq], ps[:, 0:64].rearrange("p (c s) -> p c s", c=4)[:, :, 0:nseq], AF.Exp)
                for (srcb, dstv) in ((Btb, BtT[:, :, 0:C]), (Ktb, KtT[:, :, 0:C]), (Atb, ARt[:, :, 0, 0:C]), (Rtb, ARt[:, :, 1, 0:C])):
                    ps = self.psum(); pv = ps[:].bitcast(BF16)
                    for j in range(4):
                        kb.tr(pv[:, j * 128:j * 128 + C], srcb[0:C, j * 128:(j + 1) * 128], self.identb[0:C, 0:C])
                    kb.copy('act', dstv, pv[:, 0:512].rearrange("p (j t) -> p j t", j=4)[:, :, 0:C])
                import os as _os
                dlev = int(_os.environ.get('RWDBG_P' if gname == 'p' else 'RWDBG_S', '99'))
                if dlev <= 2: continue
                kb.memset('dve', ARz[:], 0.0)
                kb.copy('act', ARz[0:64, 0, :, :, 0:C], ARt[0:64, :, :, 0:C])
                kb.copy('dve', ARz[64:128, 1, :, :, 0:C], ARt[64:128, :, :, 0:C])
                yps = self.psum_long()
                if gname == 'p':
                    for half in range(2):
                        hls = [4 * half + q for q in range(4)]
                        h4 = slice(half * 256, (half + 1) * 256)
                        for q, hl in enumerate(hls):
                            j = hl // 2; h2 = hl % 2
                            ps = self.psum()
                            kb.mm(ps[0:C, 0:2 * C], [(BtT[:, j, 0:C], ARz[:, h2, j, :, 0:C])])
                            kb.tt('dve', NP4[0:C, q, :, 0:C], ps[0:C, 0:2 * C].rearrange("p (a t) -> p a t", a=2), M2[gname][0:C, :, :], ALU.mult)
                            ps = self.psum()
                            kb.mm(ps[0:C, 0:2 * C], [(KtT[:, j, 0:C], ARz[:, h2, j, :, 0:C])])
                            kb.tt('dve', MQ4[0:C, q, :, 0:C], ps[0:C, 0:2 * C].rearrange("p (a t) -> p a t", a=2), M2[gname][0:C, :, :], ALU.mult)
                        xc = 0
                        psx = self.psum()
                        for q, hl in enumerate(hls):
                            j = hl // 2; h2 = hl % 2
                            kb.mm(psx[0:C, q * C:(q + 1) * C], [(ARz[:, h2, j, 0, 0:C], BtT[:, j, 0:C])])
                        kb.tt('dve', X4[xc][0:C, :, 0:C], psx[0:C, 0:4 * C].rearrange("p (q t) -> p q t", q=4),
                              self.mlow[gname][0:C, 0:C].unsqueeze(1).to_broadcast([C, 4, C]), ALU.mult)
                        ups = self.psum()
                        yprs = []
                        for q, hl in enumerate(hls):
                            j = hl // 2; h2 = hl % 2; pair = hb * 4 + j
                            vsl = vb[0:C, hb * 512 + hl * 64:hb * 512 + (hl + 1) * 64]
                            prs = [(MQ4[0:C, q, 0, 0:C], vsl)]
                            ypr = [(MQ4[0:C, q, 1, 0:C], vsl)]
                            if not first:
                                prs.append((ARz[:, h2, j, 0, 0:C], Hb2[:, pair, :]))
                                ypr.append((ARz[:, h2, j, 1, 0:C], Hb2[:, pair, :]))
                            yprs.append(ypr)
                            kb.mm(ups[0:C, q * 64:(q + 1) * 64], prs)
                        kb.copy('dve', U[0:C, h4], ups[0:C, 0:256])
                        kb.copy('act', Ub[0:C, h4], U[0:C, h4])
                        xts = [NP4[0:C, q, 0, 0:C] for q in range(4)]
                        xs_ = [X4[xc][0:C, q, 0:C] for q in range(4)]
                        tc_ = 0
                        for lev in range(nlev):
                            aps = self.psum()
                            for q, hl in enumerate(hls):
                                kb.mm(aps[0:C, q * 64:(q + 1) * 64], [(xts[q], Ub[0:C, hl * 64:(hl + 1) * 64])])
                            kb.tt('dve', U[0:C, h4], U[0:C, h4], aps[0:C, 0:256], ALU.add)
                            kb.copy('act', Ub[0:C, h4], U[0:C, h4])
                            if lev < nlev - 1:
                                p2 = self.psum()
                                for q in range(4):
                                    kb.mm(p2[0:C, q * C:(q + 1) * C], [(xs_[q], xts[q])])
                                if lev < nlev - 2:
                                    p3 = self.psum()
                                    for q in range(4):
                                        kb.mm(p3[0:C, q * C:(q + 1) * C], [(xts[q], xs_[q])])
                                xtn = XT4[tc_]; tc_ ^= 1
                                kb.copy('act', xtn[0:C, :, 0:C], p2[0:C, 0:4 * C].rearrange("p (q t) -> p q t", q=4))
                                if lev < nlev - 2:
                                    xc ^= 1
                                    kb.copy('dve', X4[xc][0:C, :, 0:C], p3[0:C, 0:4 * C].rearrange("p (q t) -> p q t", q=4))
                                    xs_ = [X4[xc][0:C, q, 0:C] for q in range(4)]
                                xts = [xtn[0:C, q, 0:C] for q in range(4)]
                        for q, hl in enumerate(hls):
                            hc = slice(hl * 64, (hl + 1) * 64)
                            kb.mm(yps[0:C, hc], yprs[q] + [(NP4[0:C, q, 1, 0:C], Ub[0:C, hc])])
                        for j in (2 * half, 2 * half + 1):
                            pair = hb * 4 + j
                            pc = slice(j * 128, (j + 1) * 128)
                            vpc = vb[0:C, hb * 512 + j * 128:hb * 512 + (j + 1) * 128]
                            hps = self.psum()
                            kb.mm(hps[:, 0:128], [(Bend[0:C, pc], Ub[0:C, pc]), (Kend[0:C, pc], vpc)])
                            for qq in range(2):
                                qs = slice(qq * 64, (qq + 1) * 64)
                                if first:
                                    kb.copy('dve', H2[qs, pair, :], hps[qs, qq * 64:(qq + 1) * 64])
                                else:
                                    kb.stt('dve', H2[qs, pair, :], H2[qs, pair, :], wlcol[qs, j, 0:1], hps[qs, qq * 64:(qq + 1) * 64], ALU.mult, ALU.add)
                            kb.copy('act', Hb2[:, pair, :], H2[:, pair, :])
                for hl in (range(8) if gname == 's' else []):
                    j = hl // 2; h2 = hl % 2; pb = h2 * 64; pair = hb * 4 + j
                    psl = slice(pb, pb + 64); hc = slice(hl * 64, (hl + 1) * 64)
                    np_ = kb.pool_rr('rwNP', NP); mq_ = kb.pool_rr('rwMQ', MQ)
                    ps = self.psum()
                    kb.mm(ps[0:C, 0:2 * C], [(BtT[:, j, 0:C], ARz[:, h2, j, :, 0:C])])
                    kb.tt('dve', np_[0:C, :, 0:C], ps[0:C, 0:2 * C].rearrange("p (a t) -> p a t", a=2), M2[gname][0:C, :, :], ALU.mult)
                    ps = self.psum()
                    kb.mm(ps[0:C, 0:2 * C], [(KtT[:, j, 0:C], ARz[:, h2, j, :, 0:C])])
                    kb.tt('dve', mq_[0:C, :, 0:C], ps[0:C, 0:2 * C].rearrange("p (a t) -> p a t", a=2), M2[gname][0:C, :, :], ALU.mult)
                    x0 = kb.pool_rr('rwX', Xn)
                    ps = self.psum()
                    kb.mm(ps[0:C, 0:C], [(ARz[:, h2, j, 0, 0:C], BtT[:, j, 0:C])])
                    kb.tt('dve', x0[0:C, 0:C], ps[0:C, 0:C], self.mlow[gname][0:C, 0:C], ALU.mult)
                    if dlev <= 3: continue
                    vsl = vb[0:C, hb * 512 + hl * 64:hb * 512 + (hl + 1) * 64]
                    prs = [(mq_[0:C, 0, 0:C], vsl)]
                    ypr = [(mq_[0:C, 1, 0:C], vsl)]
                    if gname == 'p':
                        if not first:
                            prs.append((ARz[:, h2, j, 0, 0:C], Hb2[:, pair, :]))
                            ypr.append((ARz[:, h2, j, 1, 0:C], Hb2[:, pair, :]))
                    else:
                        if h2 == 0:
                            for i in range(nseq):
                                si = kb.pool_rr('rwSin', Sin)
                                kb.dma('sp', si[:, :].rearrange("v (h k) -> v h k", h=2), io['state_rwkv_wkv'][i, 2 * pair:2 * pair + 2].rearrange("h v k -> v h k"))
                                tp = self.psum()
                                kb.tr(tp[:, 0:64], si[:, :], self.identf[0:64, 0:64])
                                kb.copy('dve', HsAll[:, i, :], tp[:, 0:64])
                                kb.copy('dve', HsbAll[:, i, :], tp[:, 0:64])
                        kb.tt('dve', amAll[:, :, :, 0:C], ARz[:, h2, j, :, 0:C].unsqueeze(1).to_broadcast([128, nseq, 2, C]),
                              seqrow[:, :, 0:C].unsqueeze(2).to_broadcast([128, nseq, 2, C]), ALU.mult)
                        for i in range(nseq):
                            prs.append((amAll[:, i, 0, 0:C], HsbAll[:, i, :]))
                            ypr.append((amAll[:, i, 1, 0:C], HsbAll[:, i, :]))
                    ups = self.psum()
                    kb.mm(ups[0:C, 0:64], prs)
                    kb.copy('dve', U[0:C, hc], ups[0:C, 0:64])
                    kb.copy('dve', Ub[0:C, hc], ups[0:C, 0:64])
                    if dlev <= 4: continue
                    xt_cur = np_[0:C, 0, 0:C]; x_cur = x0[0:C, 0:C]
                    for lev in range(nlev):
                        aps = self.psum()
                        kb.mm(aps[0:C, 0:64], [(xt_cur, Ub[0:C, hc])])
                        kb.tt('dve', U[0:C, hc], U[0:C, hc], aps[0:C, 0:64], ALU.add)
                        kb.copy('dve', Ub[0:C, hc], U[0:C, hc])
                        if lev < nlev - 1:
                            xtn = kb.pool_rr('rwXT', XT)
                            p2 = self.psum()
                            kb.mm(p2[0:C, 0:C], [(x_cur, xt_cur)])
                            kb.copy('dve', xtn[0:C, 0:C], p2[0:C, 0:C])
                            if lev < nlev - 2:
                                xn_ = kb.pool_rr('rwX', Xn)
                                p3 = self.psum()
                                kb.mm(p3[0:C, 0:C], [(xt_cur, x_cur)])
                                kb.copy('dve', xn_[0:C, 0:C], p3[0:C, 0:C])
                                x_cur = xn_[0:C, 0:C]
                            xt_cur = xtn[0:C, 0:C]
                    if dlev <= 5: continue
                    ypr.append((np_[0:C, 1, 0:C], Ub[0:C, hc]))
                    kb.mm(yps[0:C, hc], ypr)
                    if dlev <= 6: continue
                    if h2 == 1:
                        pc = slice(j * 128, (j + 1) * 128)
                        vpc = vb[0:C, hb * 512 + j * 128:hb * 512 + (j + 1) * 128]
                        if gname == 'p':
                            hps = self.psum()
                            kb.mm(hps[:, 0:128], [(Bend[0:C, pc], Ub[0:C, pc]), (Kend[0:C, pc], vpc)])
                            for q in range(2):
                                qs = slice(q * 64, (q + 1) * 64)
                                if first:
                                    kb.copy('dve', H2[qs, pair, :], hps[qs, q * 64:(q + 1) * 64])
                                else:
                                    kb.stt('dve', H2[qs, pair, :], H2[qs, pair, :], wlcol[qs, j, 0:1], hps[qs, q * 64:(q + 1) * 64], ALU.mult, ALU.add)
                            kb.copy('dve', Hb2[:, pair, :], H2[:, pair, :])
                        else:
                            for i in range(nseq):
                                bk_ = kb.pool_rr('rwbkm', bkm)
                                kb.ts('dve', bk_[0:C, 0, :], Bend[0:C, pc], G['seqmask'][0:C, i:i + 1], ALU.mult)
                                kb.ts('dve', bk_[0:C, 1, :], Kend[0:C, pc], G['seqmask'][0:C, i:i + 1], ALU.mult)
                                hps = self.psum()
                                kb.mm(hps[:, 0:128], [(bk_[0:C, 0, :], Ub[0:C, pc]), (bk_[0:C, 1, :], vpc)])
                                for q in range(2):
                                    qs = slice(q * 64, (q + 1) * 64)
                                    kb.stt('dve', HsAll[qs, i, :], HsAll[qs, i, :], wlcol[qs, j, i:i + 1], hps[qs, q * 64:(q + 1) * 64], ALU.mult, ALU.add)
                                tp = self.psum()
                                kb.tr(tp[0:64, 0:128], HsAll[:, i, :], self.identf[:, :])
                                so = kb.pool_rr('rwSout', Sout)
                                kb.copy('dve', so[:, :], tp[0:64, 0:128])
                                kb.dma('sp', io['s_wkv'][i, 2 * pair:2 * pair + 2].rearrange("h v k -> v h k"), so[:, :].rearrange("v (h k) -> v h k", h=2))
                kb.copy('act', yb[0:C, :], yps[0:C, :])
                kb.op('dve', lambda en: en.tensor_reduce(out=st8[0:C, :], in_=h3(yb[0:C, :]), axis=AX.X, op=ALU.add), [yb[0:C, :]], [st8[0:C, :]])
                kb.ts('dve', st8[0:C, :], st8[0:C, :], 1.0 / 64, ALU.mult)
                kb.tt('dve', h3(yb[0:C, :]), h3(yb[0:C, :]), st8[0:C, :].unsqueeze(2).to_broadcast([C, 8, 64]), ALU.subtract)
                kb.tt('dve', t1[0:C, :], yb[0:C, :], yb[0:C, :], ALU.mult)
                kb.op('dve', lambda en: en.tensor_reduce(out=st8b[0:C, :], in_=h3(t1[0:C, :]), axis=AX.X, op=ALU.add), [t1[0:C, :]], [st8b[0:C, :]])
                kb.ts('dve', st8b[0:C, :], st8b[0:C, :], 1.0 / 64, ALU.mult, 64e-5, ALU.add)
                kb.act(st8b[0:C, :], st8b[0:C, :], AF.Sqrt)
                kb.recip(st8b[0:C, :], st8b[0:C, :])
                kb.tt('dve', h3(yb[0:C, :]), h3(yb[0:C, :]), st8b[0:C, :].unsqueeze(2).to_broadcast([C, 8, 64]), ALU.mult)
                kb.tt('dve', yb[0:C, :], yb[0:C, :], tbl[0:C, 4, :], ALU.mult)
                kb.tt('dve', yb[0:C, :], yb[0:C, :], tbl[0:C, 5, :], ALU.add)
                kb.tt('dve', h3(t1[0:C, :]), h3(vb[0:C, cs]), bsum[0:C, :].unsqueeze(2).to_broadcast([C, 8, 64]), ALU.mult)
                kb.tt('dve', yb[0:C, :], yb[0:C, :], t1[0:C, :], ALU.add)
                kb.tt('dve', yg[0:C, cs], yb[0:C, :], sg[0:C, cs], ALU.mult)
            if dlev <= 2:
                kb.dma('sp', dst, self.xt[0:C, :]); continue
            self.transpose_yg(yg, C)
            self.out_proj(C, dst, final)
            if gname == 'p' and ti == self.NTP - 1 and dlev > 8:
                for pair in range(32):
                    ps = self.psum()
                    kb.tr(ps[0:64, 0:128], H2[:, pair, :], self.identf[:, :])
                    so = kb.pool_rr('rwSout', Sout)
                    kb.copy('act' if pair % 2 else 'dve', so[:, :], ps[0:64, 0:128])
                    kb.dma('sp', io['p_wkv'][2 * pair:2 * pair + 2].rearrange("h v k -> v h k"), so[:, :].rearrange("v (h k) -> v h k", h=2))

    def layer_ssd(self, li, src_p, src_s, final):
        kb, io, les = self.kb, self.io, self.les
        NS = self.NS
        self.load_gain(li)
        sz = kb.sb('ssd_sz', [128, E], BF16, gran=512, es=les)
        xbc = kb.sb('ssd_xbc', [128, 6144], F32, gran=512, es=les)
        Bb = kb.sb('ssd_Bb', [128, 1024], BF16, es=les); Cb = kb.sb('ssd_Cb', [128, 1024], BF16, es=les)
        BT = kb.sb('ssd_BT', [128, 8, 128], BF16, es=les); CT = kb.sb('ssd_CT', [128, 8, 128], BF16, es=les)
        dtr = kb.sb('ssd_dtr', [128, 64], F32, es=les); dtv = kb.sb('ssd_dt', [128, 64], F32, es=les)
        la = kb.sb('ssd_la', [128, 64], F32, es=les); Lc = kb.sb('ssd_L', [128, 64], F32, es=les)
        Ll = kb.sb('ssd_Ll', [128, 64], F32, es=les); eL = kb.sb('ssd_eL', [128, 64], F32, es=les)
        wend = kb.sb('ssd_wend', [128, 64], F32, es=les); ellB = kb.sb('ssd_ellB', [128, 64], F32, es=les)
        dtb = kb.sb('ssd_dtb', [128, 64], F32, es=les); negA = kb.sb('ssd_negA', [128, 64], F32, es=les)
        Dt = kb.sb('ssd_Dtab', [128, 64], F32, es=les)
        kb.dma('sp', dtb[:], io['ssd_dt_bias'].partition_broadcast(128))
        kb.dma('sp', negA[:], io['ssd_A_log'].partition_broadcast(128))
        kb.dma('sp', Dt[:], io['ssd_D'].partition_broadcast(128))
        kb.act(negA[:], negA[:], AF.Exp)
        kb.op('act', lambda en: en.mul(out=negA[:], in_=negA[:], mul=-1.0), [negA[:]], [negA[:]])
        cw = [kb.sb(f'ssd_cw{i}', [128, 5, 512], F32, es=les) for i in range(2)]
        cblk = [kb.sb(f'ssd_cb{i}', [48, 512], F32, es=les) for i in range(2)]
        acc = kb.sb('ssd_acc', [128, 512], F32, es=les); tmpc = kb.sb('ssd_tmpc', [128, 512], F32, es=les)
        vv = kb.sb('ssd_v', [128, 64, 64], BF16, gran=512, es=les); vdt = [kb.sb(f'ssd_vd{i}', [128, 512], BF16, es=les) for i in range(2)]
        msc = kb.sb('ssd_msc', [128, 128], F32, es=les)
        drhs = [kb.sb(f'ssd_drhs{i}', [128, 128], F32, es=les) for i in range(2)]
        dseg = [kb.sb(f'ssd_dseg{i}', [128, 128], F32, es=les) for i in range(2)]
        AT = [kb.sb(f'ssd_AT{i}', [128, 128], BF16, es=les) for i in range(2)]
        ybuf = kb.sb('ssd_y', [128, E], F32, gran=512, es=les)
        yg = sz
        ss = kb.sb('ssd_ss', [128, 8], F32, es=les); rs = kb.sb('ssd_rs', [128, 8], F32, es=les)
        H = kb.sb('ssd_H', [128, 64, 64], F32, gran=512, es=les)
        Hbt = [kb.sb(f'ssd_Hb{i}', [128, 512], BF16, es=les) for i in range(2)]
        Hst = [kb.sb(f'ssd_Hs{i}', [128, 512], F32, es=les) for i in range(2)]
        cm = [kb.sb(f'ssd_cm{i}', [128, 128], BF16, es=les) for i in range(1)]
        bm = [kb.sb(f'ssd_bm{i}', [128, 128], BF16, es=les) for i in range(1)]
        ShT = {}; ScT = {}; selB = {}
        for gname in ('p', 's'):
            Cg = self.G[gname]['C']
            ShT[gname] = kb.sb(f'ssd_Sh{gname}', [128, 3, Cg], F32, es=les)
            kb.dma('sp', ShT[gname][:], io[f'c_{gname}_convSh'][:, :, :])
            ScT[gname] = kb.sb(f'ssd_Sc{gname}', [128, 3, Cg], F32, es=les)
            kb.dma('sp', ScT[gname][:], io[f'c_{gname}_convSc'][:, :, :])
        selB['s'] = kb.sb('ssd_selB', [128, NS, 128], F32, es=les)
        kb.dma('sp', selB['s'][:], io['c_s_selB'][:, :, :])
        carry = kb.dram('ssd_carry', [2, 3, 6144], F32)
        self.kb.reg['ssd_carry'] = ('dram', 3 * 6144)
        if final:
            self.fgain = kb.sb('fgain', [128, D], F32, es=les)
            kb.dma('sp', self.fgain[:], io['final_norm'].partition_broadcast(128))
        ntl = self.NTP + 1
        self.wplan(self.wblocks(io['ssd_w_in'], 10304, io['ssd_w_out']) * ntl)
        cwv = io['ssd_conv_w']; cbv = io['ssd_conv_b']
        for (gname, ti, src, dst, C) in self.tiles(src_p, src_s):
            G = self.G[gname]; nseq = G['nseq']
            first = (gname == 'p' and ti == 0)
            self.load_norm(src, C)
            self.make_xnT(C)
            def evac(bk, n, ps, C=C):
                e = 'act' if bk % 2 else 'dve'
                if bk < 8: kb.act(sz[0:C, bk * 512:(bk + 1) * 512], ps[0:C, :], AF.Silu)
                elif bk < 20: kb.copy(e, xbc[0:C, (bk - 8) * 512:(bk - 7) * 512], ps[0:C, :])
                else: kb.copy('dve', dtr[0:C, :], ps[0:C, 0:64])
            self.in_proj(C, 10304, None, evac=evac)
            if gname == 'p':
                kb.dma('sp', carry[(ti + 1) % 2], xbc[C - 3:C, :])
                if ti == self.NTP - 1:
                    kb.dma('sp', io['p_conv'][:, :], xbc[C - 3:C, :])
            else:
                for i in range(nseq):
                    kb.dma('sp', io['s_conv'][i], xbc[4 * i + 1:4 * i + 4, :])
            R = 3 if gname == 'p' else 3 * nseq
            for b in range(12):
                cs = slice(b * 512, (b + 1) * 512)
                w_ = kb.pool_rr('ssdcw', cw); cb_ = kb.pool_rr('ssdcb', cblk)
                kb.dma('sp', w_[:, 0:4, :], cwv[:, cs].partition_broadcast(128))
                kb.dma('sp', w_[:, 4, :], cbv[cs].partition_broadcast(128))
                has_c = not first
                if has_c:
                    if gname == 'p': kb.dma('sp', cb_[0:3, :], carry[ti % 2][:, cs])
                    else: kb.dma('sp', cb_[0:R, :], io['state_ssd_conv'][:, :, cs].rearrange("i r c -> (i r) c"))
                kb.tt('dve', acc[0:C, :], xbc[0:C, cs], w_[0:C, 3, :], ALU.mult)
                kb.tt('dve', acc[0:C, :], acc[0:C, :], w_[0:C, 4, :], ALU.add)
                for d in (1, 2, 3):
                    ps = self.psum()
                    prs = [(ShT[gname][0:C, d - 1, :], xbc[0:C, cs])]
                    if has_c: prs.append((ScT[gname][0:R, d - 1, :], cb_[0:R, :]))
                    kb.mm(ps[0:C, :], prs)
                    kb.tt('dve', tmpc[0:C, :], ps[0:C, :], w_[0:C, 3 - d, :], ALU.mult)
                    kb.tt('dve', acc[0:C, :], acc[0:C, :], tmpc[0:C, :], ALU.add)
                if b < 8: kb.act(xbc[0:C, cs], acc[0:C, :], AF.Silu)
                elif b < 10: kb.act(Bb[0:C, (b - 8) * 512:(b - 7) * 512], acc[0:C, :], AF.Silu)
                else: kb.act(Cb[0:C, (b - 10) * 512:(b - 9) * 512], acc[0:C, :], AF.Silu)
            xs3 = xbc[0:C, 0:4096].rearrange("p (h d) -> p h d", h=64)
            kb.tt('dve', dtv[0:C, :], dtr[0:C, :], dtb[0:C, :], ALU.add)
            kb.act(dtv[0:C, :], dtv[0:C, :], AF.Exp)
            kb.act(dtv[0:C, :], dtv[0:C, :], AF.Ln, bias=1.0)
            kb.tt('dve', la[0:C, :], dtv[0:C, :], negA[0:C, :], ALU.mult)
            ps1 = self.psum(); ps2 = self.psum()
            kb.mm(ps1[0:C, 0:64], [(G['mTi'][0:C, 0:C], la[0:C, :])])
            kb.mm(ps2[0:C, 0:64], [(G['ones'][0:C, 0:C], la[0:C, :])])
            kb.copy('act', Lc[0:C, :], ps1[0:C, 0:64]); kb.copy('dve', Ll[0:C, :], ps2[0:C, 0:64])
            kb.act(eL[0:C, :], Lc[0:C, :], AF.Exp)
            kb.tt('dve', wend[0:C, :], Ll[0:C, :], Lc[0:C, :], ALU.subtract)
            kb.act(wend[0:C, :], wend[0:C, :], AF.Exp)
            kb.tt('dve', vv[0:C], xs3, dtv[0:C, :].unsqueeze(2).to_broadcast([C, 64, 64]), ALU.mult)
            for (srcb, dstb) in ((Bb, BT), (Cb, CT)):
                ps = self.psum(); pv = ps[:].bitcast(BF16)
                for j in range(8):
                    kb.tr(pv[:, j * 128:j * 128 + C], srcb[0:C, j * 128:(j + 1) * 128], self.identb[0:C, 0:C])
                kb.copy('act', dstb[:, :, 0:C], pv.rearrange("p (j t) -> p j t", j=8)[:, :, 0:C])
            for g in range(8):
                ps = self.psum()
                kb.mm(ps[0:C, 0:C], [(BT[:, g, 0:C], CT[:, g, 0:C])])
                kb.tt('dve', msc[0:C, 0:C], ps[0:C, 0:C], G['mTi'][0:C, 0:C], ALU.mult)
                yps = self.psum_long()
                for hl in range(8):
                    h = g * 8 + hl
                    dr = kb.pool_rr('ssddr', drhs); ds_ = kb.pool_rr('ssdds', dseg); at = kb.pool_rr('ssdAT', AT)
                    kb.ts('dve', dr[0:C, 0:C], self.identf[0:C, 0:C], Lc[0:C, h:h + 1], ALU.mult)
                    psd = self.psum()
                    kb.mm(psd[0:C, 0:C], [(G['ones'][0:C, 0:C] if gname == 'p' else self.onesf[0:C, 0:C], dr[0:C, 0:C])])
                    kb.ts('dve', ds_[0:C, 0:C], psd[0:C, 0:C], Lc[0:C, h:h + 1], ALU.subtract, 0.0, ALU.min)
                    kb.act(ds_[0:C, 0:C], ds_[0:C, 0:C], AF.Exp)
                    kb.tt('dve', at[0:C, 0:C], ds_[0:C, 0:C], msc[0:C, 0:C], ALU.mult)
                    kb.mm(yps[0:C, hl * 64:(hl + 1) * 64], [(at[0:C, 0:C], vv[0:C, h, :])])
                kb.copy('act', ybuf[0:C, g * 512:(g + 1) * 512], yps[0:C, :])
                gsl = slice(g * 8, (g + 1) * 8)
                vdb = kb.pool_rr('ssdvd', vdt)
                kb.tt('dve', vdb[0:C, :].rearrange("p (h d) -> p h d", h=8), vv[0:C, gsl, :], wend[0:C, gsl].unsqueeze(2).to_broadcast([C, 8, 64]), ALU.mult)
                vdg = vdb[0:C, :]
                Hg = H[:, gsl, :].rearrange("p h d -> p (h d)")
                if gname == 'p':
                    if not first:
                        hb = kb.pool_rr('ssdHb', Hbt)
                        kb.copy('act', hb[:], Hg)
                        zps = self.psum()
                        kb.mm(zps[0:C, :], [(CT[:, g, 0:C], hb[:, :])])
                        kb.tt('dve', zps[0:C, :].rearrange("p (h d) -> p h d", h=8), zps[0:C, :].rearrange("p (h d) -> p h d", h=8),
                              eL[0:C, gsl].unsqueeze(2).to_broadcast([C, 8, 64]), ALU.mult) if False else None
                        ztmp = tmpc
                        kb.tt('dve', ztmp[0:C, :].rearrange("p (h d) -> p h d", h=8), zps[0:C, :].rearrange("p (h d) -> p h d", h=8),
                              eL[0:C, gsl].unsqueeze(2).to_broadcast([C, 8, 64]), ALU.mult)
                        kb.tt('dve', ybuf[0:C, g * 512:(g + 1) * 512], ybuf[0:C, g * 512:(g + 1) * 512], ztmp[0:C, :], ALU.add)
                    hps = self.psum()
                    kb.mm(hps[:, :], [(Bb[0:C, g * 128:(g + 1) * 128], vdg)])
                    if first:
                        kb.copy('act', Hg, hps[:, :])
                    else:
                        if g == 0:
                            psl = self.psum()
                            kb.mm(psl[:, 0:64], [(G['ones'][0:C, :], la[0:C, :])])
                            kb.act(ellB[:, :], psl[:, 0:64], AF.Exp)
                        kb.tt('dve', H[:, gsl, :], H[:, gsl, :], ellB[:, gsl].unsqueeze(2).to_broadcast([128, 8, 64]), ALU.mult)
                        kb.tt('dve', Hg, Hg, hps[:, :], ALU.add)
                else:
                    for i in range(nseq):
                        hs = kb.pool_rr('ssdHs', Hst); hb = kb.pool_rr('ssdHb', Hbt)
                        cmi = kb.pool_rr('ssdcm', cm); bmi = kb.pool_rr('ssdbm', bm)
                        kb.dma('sp', hs[:].rearrange("p (h d) -> p h d", h=8), io['state_ssd'][i, g * 8:(g + 1) * 8].rearrange("h n d -> n h d"))
                        kb.copy('act', hb[:], hs[:])
                        kb.memset('dve', cmi[:, 0:C], 0.0)
                        kb.copy('dve', cmi[:, 4 * i:4 * i + 4], CT[:, g, 4 * i:4 * i + 4])
                        zps = self.psum()
                        kb.mm(zps[0:C, :], [(cmi[:, 0:C], hb[:, :])])
                        ztmp = tmpc
                        kb.tt('dve', ztmp[0:C, :].rearrange("p (h d) -> p h d", h=8), zps[0:C, :].rearrange("p (h d) -> p h d", h=8),
                              eL[0:C, gsl].unsqueeze(2).to_broadcast([C, 8, 64]), ALU.mult)
                        kb.tt('dve', ybuf[0:C, g * 512:(g + 1) * 512], ybuf[0:C, g * 512:(g + 1) * 512], ztmp[0:C, :], ALU.add)
                        kb.ts('dve', bmi[0:C, :], Bb[0:C, g * 128:(g + 1) * 128], G['seqmask'][0:C, i:i + 1], ALU.mult)
                        hps = self.psum()
                        kb.mm(hps[:, :], [(bmi[0:C, :], vdg)])
                        psl = self.psum()
                        kb.mm(psl[:, 0:8], [(selB['s'][0:C, i, :], la[0:C, gsl])])
                        kb.act(ellB[:, 0:8], psl[:, 0:8], AF.Exp)
                        hs3 = hs[:].rearrange("p (h d) -> p h d", h=8)
                        kb.tt('dve', hs3, hs3, ellB[:, 0:8].unsqueeze(2).to_broadcast([128, 8, 64]), ALU.mult)
                        kb.tt('dve', hs[:], hs[:], hps[:, :], ALU.add)
                        kb.dma('sp', io['s_ssd'][i, g * 8:(g + 1) * 8].rearrange("h n d -> n h d"), hs[:].rearrange("p (h d) -> p h d", h=8))
            y3 = ybuf[0:C, :].rearrange("p (h d) -> p h d", h=64)
            for g in range(8):
                gsl = slice(g * 8, (g + 1) * 8)
                kb.tt('dve', tmpc[0:C, :].rearrange("p (h d) -> p h d", h=8), xs3[:, gsl, :], Dt[0:C, gsl].unsqueeze(2).to_broadcast([C, 8, 64]), ALU.mult)
                kb.tt('dve', ybuf[0:C, g * 512:(g + 1) * 512], ybuf[0:C, g * 512:(g + 1) * 512], tmpc[0:C, :], ALU.add)
            kb.tt('dve', ybuf[0:C, :], ybuf[0:C, :], sz[0:C, :], ALU.mult)
            for g in range(8):
                kb.act(self.xnb[0:C, 0:512], ybuf[0:C, g * 512:(g + 1) * 512], AF.Square, accum=ss[0:C, g:g + 1])
            kb.ts('dve', rs[0:C, :], ss[0:C, :], 1.0 / 512, ALU.mult, 1e-5, ALU.add)
            kb.act(rs[0:C, :], rs[0:C, :], AF.Sqrt)
            kb.recip(rs[0:C, :], rs[0:C, :])
            y8 = ybuf[0:C, :].rearrange("p (g v) -> p g v", g=8)
            kb.tt('dve', y8, y8, rs[0:C, :].unsqueeze(2).to_broadcast([C, 8, 512]), ALU.mult)
            for g in range(8):
                w_ = kb.pool_rr('ssdcw', cw)
                kb.dma('sp', w_[:, 0, :], io['ssd_norm_w'][g * 512:(g + 1) * 512].partition_broadcast(128))
                kb.tt('dve', yg[0:C, g * 512:(g + 1) * 512], ybuf[0:C, g * 512:(g + 1) * 512], w_[0:C, 0, :], ALU.mult)
            self.transpose_yg(yg, C)
            self.out_proj(C, dst, final)
            if gname == 'p' and ti == self.NTP - 1:
                kb.dma('sp', io['p_ssd'].rearrange("h n d -> n h d"), H[:])

    def layer_gla(self, li, src_p, src_s, final):
        kb, io, les = self.kb, self.io, self.les
        self.load_gain(li)
        QK = kb.sb('glaQK', [128, 2048], F32, gran=512, es=les)
        vb = kb.sb('gla_vb', [128, E], BF16, gran=512, es=les)
        sg = kb.sb('gla_sg', [128, E], BF16, gran=512, es=les)
        adf = kb.sb('gla_ad', [128, 16], F32, es=les)
        adT = kb.sb('gla_adT', [16, 128], F32, es=les)
        aup = kb.sb('gla_aup', [16, 1024], F32, es=les)
        abias = kb.sb('gla_abias', [128, 1024], F32, es=les)
        nw = kb.sb('gla_nw', [128, 1024], F32, es=les)
        kb.dma('sp', aup[:], io['gla_a_up'][:, :])
        kb.dma('sp', abias[:], io['gla_a_bias'].partition_broadcast(128))
        kb.dma('sp', nw[:], io['gla_norm_w'].partition_broadcast(128))
        la = kb.sb('gla_la', [128, 1024], F32, gran=512, es=les)
        Lc = kb.sb('gla_L', [128, 1024], F32, gran=512, es=les)
        Ll = kb.sb('gla_Ll', [128, 1024], F32, gran=512, es=les)
        tmp = kb.sb('gla_tmp', [128, 1024], F32, gran=512, es=les)
        qe = kb.sb('gla_qe', [128, 1024], BF16, es=les); ke = kb.sb('gla_ke', [128, 1024], BF16, es=les)
        kend = kb.sb('gla_kend', [128, 1024], BF16, es=les)
        qeT = kb.sb('gla_qeT', [128, 8, 128], BF16, es=les); keT = kb.sb('gla_keT', [128, 8, 128], BF16, es=les)
        gcol = kb.sb('gla_gcol', [128, 8, max(self.NS, 1)], F32, es=les)
        AT = [kb.sb(f'gla_AT{i}', [128, 128], BF16, es=les) for i in range(2)]
        ybuf = kb.sb('gla_y', [128, E], F32, gran=512, es=les)
        yg = vb
        ss = kb.sb('gla_ss', [128, 8], F32, es=les); rs = kb.sb('gla_rs', [128, 4], F32, es=les)
        H = kb.sb('gla_H', [128, 4, 2, 1024], F32, gran=1024, es=les)
        Hbt = [kb.sb(f'gla_Hb{i}', [128, 2, 1024], BF16, es=les) for i in range(2)]
        Hs = [H[:, i] for i in range(2)]; Hsb = Hbt
        qdm = [kb.sb(f'gla_qdm{i}', [128, 2, 128], BF16, es=les) for i in range(2)]
        kdm = [kb.sb(f'gla_kdm{i}', [128, 256], BF16, es=les) for i in range(2)]
        if final:
            self.fgain = kb.sb('fgain', [128, D], F32, es=les)
            kb.dma('sp', self.fgain[:], io['final_norm'].partition_broadcast(128))
        ntl = self.NTP + 1
        self.wplan(self.wblocks(io['gla_w_in'], 10256, io['gla_w_out']) * ntl)
        for (gname, ti, src, dst, C) in self.tiles(src_p, src_s):
            G = self.G[gname]; nseq = G['nseq']
            first = (gname == 'p' and ti == 0)
            self.load_norm(src, C)
            self.make_xnT(C)
            def evac(bk, n, ps, C=C):
                e = 'act' if bk % 2 else 'dve'
                if bk < 4: kb.copy(e, QK[0:C, bk * 512:(bk + 1) * 512], ps[0:C, :])
                elif bk < 12: kb.copy(e, vb[0:C, (bk - 4) * 512:(bk - 3) * 512], ps[0:C, :])
                elif bk < 20: kb.act(sg[0:C, (bk - 12) * 512:(bk - 11) * 512], ps[0:C, :], AF.Silu)
                else: kb.copy('dve', adf[0:C, :], ps[0:C, 0:16])
            self.in_proj(C, 10256, None, evac=evac)
            ps = self.psum()
            kb.tr(ps[0:16, 0:C], adf[0:C, 0:16], self.identf[0:C, 0:C])
            kb.copy('dve', adT[:, 0:C], ps[0:16, 0:C])
            for b in range(2):
                bs = slice(b * 512, (b + 1) * 512)
                ps = self.psum()
                kb.mm(ps[0:C, :], [(adT[:, 0:C], aup[:, bs])])
                kb.tt('dve', tmp[0:C, bs], ps[0:C, :], abias[0:C, bs], ALU.add)
                kb.act(tmp[0:C, bs], tmp[0:C, bs], AF.Exp, scale=-1.0)
                kb.act(tmp[0:C, bs], tmp[0:C, bs], AF.Ln, bias=1.0)
                kb.op('act', lambda en, bs=bs: en.mul(out=la[0:C, bs], in_=tmp[0:C, bs], mul=-1.0 / 16), [tmp[0:C, bs]], [la[0:C, bs]])
                ps1 = self.psum(); ps2 = self.psum()
                kb.mm(ps1[0:C, :], [(G['mTi'][0:C, 0:C], la[0:C, bs])])
                kb.mm(ps2[0:C, :], [(G['ones'][0:C, 0:C], la[0:C, bs])])
                kb.copy('act', Lc[0:C, bs], ps1[0:C, :])
                kb.copy('dve', Ll[0:C, bs], ps2[0:C, :])
                kb.act(tmp[0:C, bs], Lc[0:C, bs], AF.Exp)
                kb.stt('dve', qe[0:C, bs], QK[0:C, bs], 1.0 / 16, tmp[0:C, bs], ALU.mult, ALU.mult)
                kb.act(tmp[0:C, bs], Lc[0:C, bs], AF.Exp, scale=-1.0)
                kb.tt('dve', ke[0:C, bs], QK[0:C, 1024 + b * 512:1024 + (b + 1) * 512], tmp[0:C, bs], ALU.mult)
                kb.tt('dve', tmp[0:C, bs], Ll[0:C, bs], Lc[0:C, bs], ALU.subtract)
                kb.act(tmp[0:C, bs], tmp[0:C, bs], AF.Exp)
                kb.tt('dve', kend[0:C, bs], QK[0:C, 1024 + b * 512:1024 + (b + 1) * 512], tmp[0:C, bs], ALU.mult)
            ps = self.psum()
            for c8 in range(8):
                kb.mm(ps[:, c8 * 16:c8 * 16 + nseq], [(la[0:C, c8 * 128:(c8 + 1) * 128], G['seqmask'][0:C, 0:nseq])])
            kb.act(gcol[:, :, 0:nseq], ps[:, 0:128].rearrange("p (c s) -> p c s", c=8)[:, :, 0:nseq], AF.Exp)
            for (srcb, dstb) in ((qe, qeT), (ke, keT)):
                ps = self.psum(); pv = ps[:].bitcast(BF16)
                for j in range(8):
                    kb.tr(pv[:, j * 128:j * 128 + C], srcb[0:C, j * 128:(j + 1) * 128], self.identb[0:C, 0:C])
                kb.copy('act', dstb[:, :, 0:C], pv.rearrange("p (j t) -> p j t", j=8)[:, :, 0:C])
            for h in range(4):
                ps = self.psum()
                kb.mm(ps[0:C, 0:C], [(keT[:, h * 2 + kc, 0:C], qeT[:, h * 2 + kc, 0:C]) for kc in range(2)])
                at = kb.pool_rr('glaAT', AT)
                kb.tt('dve', at[0:C, 0:C], ps[0:C, 0:C], G['mTi'][0:C, 0:C], ALU.mult)
                if gname == 'p' and not first:
                    hbh = kb.pool_rr('glaHsb', Hbt)
                    kb.copy('act', hbh[:], H[:, h])
                for vblk in range(2):
                    vsl = vb[0:C, h * 1024 + vblk * 512:h * 1024 + (vblk + 1) * 512]
                    osl = slice(h * 1024 + vblk * 512, h * 1024 + (vblk + 1) * 512)
                    yps = self.psum_long()
                    if gname == 'p':
                        if first:
                            kb.mm(yps[0:C, :], [(at[0:C, 0:C], vsl)])
                        else:
                            kb.mm(yps[0:C, :], [(at[0:C, 0:C], vsl)] + [(qeT[:, h * 2 + kc, 0:C], hbh[:, kc, vblk * 512:(vblk + 1) * 512]) for kc in range(2)])
                        for kc in range(2):
                            hps = self.psum()
                            kb.mm(hps[:, :], [(kend[0:C, (h * 2 + kc) * 128:(h * 2 + kc + 1) * 128], vsl)])
                            hsl = H[:, h, kc, vblk * 512:(vblk + 1) * 512]
                            if first:
                                kb.copy('act', hsl, hps[:, :])
                            else:
                                kb.stt('dve', hsl, hsl, gcol[:, h * 2 + kc, 0:1], hps[:, :], ALU.mult, ALU.add)
                    else:
                        kb.mm(yps[0:C, :], [(at[0:C, 0:C], vsl)], stop=False)
                        for i in range(nseq):
                            hs = kb.pool_rr('glaHs', Hs); hsb = kb.pool_rr('glaHsb', Hsb)
                            qm = kb.pool_rr('glaqdm', qdm); km = kb.pool_rr('glakdm', kdm)
                            kb.dma('sp', hs[:, :, 0:512], io['state_gla'][i, h, :, vblk * 512:(vblk + 1) * 512].rearrange("(kc p) v -> p kc v", p=128))
                            kb.copy('act', hsb[:, :, 0:512], hs[:, :, 0:512])
                            kb.memset('dve', qm[:, :, 0:C], 0.0)
                            kb.copy('dve', qm[:, :, 4 * i:4 * i + 4], qeT[:, h * 2:h * 2 + 2, 4 * i:4 * i + 4])
                            kb.mm(yps[0:C, :], [(qm[:, kc, 0:C], hsb[:, kc, 0:512]) for kc in range(2)], start=False, stop=(i == nseq - 1))
                            kb.ts('dve', km[0:C, :], kend[0:C, h * 256:(h + 1) * 256], G['seqmask'][0:C, i:i + 1], ALU.mult)
                            for kc in range(2):
                                hps = self.psum()
                                kb.mm(hps[:, :], [(km[0:C, kc * 128:(kc + 1) * 128], vsl)])
                                kb.stt('dve', hs[:, kc, 0:512], hs[:, kc, 0:512], gcol[:, h * 2 + kc, i:i + 1], hps[:, :], ALU.mult, ALU.add)
                            kb.dma('sp', io['s_gla'][i, h, :, vblk * 512:(vblk + 1) * 512].rearrange("(kc p) v -> p kc v", p=128), hs[:, :, 0:512])
                    kb.copy('act', ybuf[0:C, osl], yps[0:C, :])
                    kb.act(self.xnb[0:C, 0:512], yps[0:C, :], AF.Square, accum=ss[0:C, h * 2 + vblk:h * 2 + vblk + 1])
            ss2 = ss[0:C, :].rearrange("p (h b) -> p h b", b=2)
            kb.tt('dve', rs[0:C, :], ss2[:, :, 0], ss2[:, :, 1], ALU.add)
            kb.ts('dve', rs[0:C, :], rs[0:C, :], 1.0 / 1024, ALU.mult, 1e-6, ALU.add)
            kb.act(rs[0:C, :], rs[0:C, :], AF.Sqrt)
            kb.recip(rs[0:C, :], rs[0:C, :])
            y3 = ybuf[0:C, :].rearrange("p (h v) -> p h v", h=4)
            kb.tt('dve', y3, y3, rs[0:C, :].unsqueeze(2).to_broadcast([C, 4, 1024]), ALU.mult)
            kb.tt('dve', y3, y3, nw[0:C, :].unsqueeze(1).to_broadcast([C, 4, 1024]), ALU.mult)
            kb.tt('dve', yg[0:C, :], ybuf[0:C, :], sg[0:C, :], ALU.mult)
            self.transpose_yg(yg, C)
            self.out_proj(C, dst, final)
            if gname == 'p' and ti == self.NTP - 1:
                kb.dma('sp', io['p_gla'].rearrange("h (kc p) v -> p h kc v", p=128), H[:])


_CACHE = {}

DBG = 99
def get_prog(TP, NS, layers=(0, 1, 2, 3), final=True):
    key = (TP, NS, tuple(layers), final, DBG)
    if key not in _CACHE:
        p = Prog(TP, NS, layers, final, DBG)
        p.nc_built = p.build()
        _CACHE[key] = p
    return _CACHE[key]


WEIGHT_NAMES = ['norm_gains', 'final_norm', 'rwkv_w_in', 'rwkv_mu', 'rwkv_w0', 'rwkv_w_up', 'rwkv_a0', 'rwkv_a_up',
                'rwkv_k_k', 'rwkv_k_a', 'rwkv_r_k', 'rwkv_ln_w', 'rwkv_ln_b', 'rwkv_w_out', 'ret_w_in', 'ret_w_out',
                'ssd_w_in', 'ssd_conv_w', 'ssd_conv_b', 'ssd_dt_bias', 'ssd_A_log', 'ssd_D', 'ssd_norm_w', 'ssd_w_out',
                'gla_w_in', 'gla_a_up', 'gla_a_bias', 'gla_norm_w', 'gla_w_out']
STATE_NAMES = ['state_rwkv_shift', 'state_rwkv_wkv', 'state_ret', 'state_ssd_conv', 'state_ssd', 'state_gla']


def run(inputs, n_cores=8, layers=(0, 1, 2, 3), final=True):
    xp = np.asarray(inputs['x_prompt'], np.float32); xs = np.asarray(inputs['x_sample'], np.float32)
    B, TP = xp.shape[0], xp.shape[1]
    DB = xs.shape[0]; NS = DB // n_cores
    prog = get_prog(TP, NS, layers, final)
    in_maps = []
    for c in range(n_cores):
        m = {}
        m['x_prompt'] = np.ascontiguousarray(xp[c % B])
        m['x_sample'] = np.ascontiguousarray(xs[c * NS:(c + 1) * NS].reshape(NS * 4, D))
        for nme in STATE_NAMES:
            if prog.need(nme): m[nme] = np.ascontiguousarray(np.asarray(inputs[nme], np.float32)[c * NS:(c + 1) * NS])
        for nme in WEIGHT_NAMES:
            if prog.need(nme): m[nme] = np.ascontiguousarray(np.asarray(inputs[nme], np.float32))
        for k, v in prog.consts.items(): m[k] = v
        in_maps.append(m)
    import os as _os
    if _os.environ.get('KTRACE'):
        res = run_bass_kernel_spmd(prog.nc_built, in_maps, core_ids=list(range(n_cores)), trace=True)
        print('KTRACE exec_time_ns', res.exec_time_ns)
    else:
        res = run_bass_kernel_spmd(prog.nc_built, in_maps, core_ids=list(range(n_cores)))
    R = res.results
    oshape = {'p_shift': (1, D), 'p_wkv': (64, 64, 64), 'p_ret': (8, 256, 512), 'p_conv': (3, 6144), 'p_ssd': (64, 128, 64),
              'p_gla': (4, 256, 1024), 's_shift': (NS, D), 's_wkv': (NS, 64, 64, 64), 's_ret': (NS, 8, 256, 512),
              's_conv': (NS, 3, 6144), 's_ssd': (NS, 64, 128, 64), 's_gla': (NS, 4, 256, 1024)}
    for r in R:
        for k, shp in oshape.items():
            if k not in r: r[k] = np.zeros(shp, np.float32)
    nb = min(B, n_cores)
    def cat(name, shape_tail=None):
        return np.concatenate([R[c][name] for c in range(n_cores)], axis=0)
    y_prompt = np.stack([R[c]['y_prompt'] for c in range(nb)], 0)
    y_sample = cat('y_sample').reshape(DB, 4, D)
    outs = [y_prompt, y_sample]
    outs.append(np.concatenate([R[c]['p_shift'] for c in range(nb)], 0))
    for nme in ('p_wkv', 'p_ret', 'p_conv', 'p_ssd', 'p_gla'):
        outs.append(np.stack([R[c][nme] for c in range(nb)], 0))
    for nme in ('s_shift', 's_wkv', 's_ret', 's_conv', 's_ssd', 's_gla'):
        outs.append(cat(nme))
    return tuple(outs)


def kernel(**inputs):
    return run(inputs, n_cores=8)
```
